# Optimizing a Trainium2 kernel written in Bass

```python
import jax, jax.numpy as jnp
from jax import lax
import numpy as np

D_MODEL = 1024
BATCH = 8
SEQ = 8192
DEPTH = 4

HEAD_DIM = 64
EPS = 1e-6
A_Q_HEADS = 12
A_KV_HEADS = 2
A_GROUP = A_Q_HEADS // A_KV_HEADS
WINDOW = 128
A_Q_W = A_Q_HEADS * HEAD_DIM
A_KV_W = A_KV_HEADS * HEAD_DIM
B_GROUPS = 6
B_GROUP_DIM = 128
B_WIDTH = B_GROUPS * B_GROUP_DIM
CHUNK = 128
N_MEM = 256
MEM_HEADS = 4
MEM_WIDTH = MEM_HEADS * HEAD_DIM
MIX_WIDTH = A_Q_W + MEM_WIDTH
A_IN = A_Q_W + 2 * A_KV_W + MEM_WIDTH
B_IN = 2 * B_WIDTH + MEM_WIDTH
D_FF = -(-8 * D_MODEL // (3 * 256)) * 256
N_A = (DEPTH + 1) // 2
N_B = DEPTH // 2

kernel_name = "hybrid_swa_sink_gmlp_memxattn_trunk"


def rms_norm(x, g):
    xf = x.astype(jnp.float32)
    y = xf * lax.rsqrt(jnp.mean(xf * xf, axis=-1, keepdims=True) + EPS)
    return (y * g.astype(jnp.float32)).astype(x.dtype)


def sliding_window_attention(q, k, v, sinks):
    b, s, _, hd = q.shape
    nb = s // WINDOW
    qb = q.reshape(b, nb, WINDOW, A_KV_HEADS, A_GROUP, hd)

    def with_prev(t):
        tb = t.reshape(b, nb, WINDOW, A_KV_HEADS, hd)
        prev = jnp.pad(tb[:, :-1], ((0, 0), (1, 0), (0, 0), (0, 0), (0, 0)))
        return jnp.concatenate([prev, tb], axis=2)

    kb, vb = with_prev(k), with_prev(v)
    scores = jnp.einsum('bnqhgd,bnkhd->bnhgqk', qb, kb).astype(jnp.float32) * (hd ** -0.5)
    qi = jnp.arange(WINDOW)[:, None]
    kj = jnp.arange(2 * WINDOW)[None, :]
    rel = qi + WINDOW - kj
    band = (rel >= 0) & (rel < WINDOW)
    not_pad = (jnp.arange(nb)[:, None, None] > 0) | (kj[None] >= WINDOW)
    mask = band[None] & not_pad
    scores = jnp.where(mask[None, :, None, None], scores, -jnp.inf)
    sink = sinks.astype(jnp.float32).reshape(A_KV_HEADS, A_GROUP)[None, None, :, :, None, None]
    m = jnp.maximum(jnp.max(scores, axis=-1, keepdims=True), sink)
    e = jnp.exp(scores - m)
    p = e / (jnp.sum(e, axis=-1, keepdims=True) + jnp.exp(sink - m))
    o = jnp.einsum('bnhgqk,bnkhd->bnqhgd', p.astype(v.dtype), vb)
    return o.reshape(b, s, A_Q_HEADS * hd)


def chunked_spatial_gating(z, w_s, b_s, ln_g, ln_b):
    b, s, _ = z.shape
    u, v = jnp.split(z, 2, axis=-1)
    v = v.reshape(b, s // CHUNK, CHUNK, B_GROUPS, B_GROUP_DIM)
    vf = v.astype(jnp.float32)
    mu = jnp.mean(vf, axis=-1, keepdims=True)
    var = jnp.mean(jnp.square(vf - mu), axis=-1, keepdims=True)
    vn = ((vf - mu) * lax.rsqrt(var + EPS) * ln_g.astype(jnp.float32) + ln_b.astype(jnp.float32)).astype(z.dtype)
    causal = jnp.tril(jnp.ones((CHUNK, CHUNK), dtype=bool))
    w = jnp.where(causal[None], w_s, jnp.zeros_like(w_s)).astype(vn.dtype)
    sv = jnp.einsum('gts,bnsgd->bntgd', w, vn) + b_s.T.astype(vn.dtype)[None, None, :, :, None]
    return u * sv.reshape(b, s, B_WIDTH).astype(z.dtype)


def memory_attention(q_mem, mem_n, w_kv):
    b, nm, _ = mem_n.shape
    kv = mem_n @ w_kv
    k, v = jnp.split(kv, 2, axis=-1)
    k = k.reshape(b, nm, MEM_HEADS, HEAD_DIM)
    v = v.reshape(b, nm, MEM_HEADS, HEAD_DIM)
    s = jnp.einsum('bshd,bmhd->bhsm', q_mem, k).astype(jnp.float32) * (HEAD_DIM ** -0.5)
    p = jax.nn.softmax(s, axis=-1).astype(v.dtype)
    o = jnp.einsum('bhsm,bmhd->bshd', p, v)
    return o.reshape(q_mem.shape[0], q_mem.shape[1], MEM_WIDTH)


def setup_inputs(seed: int = 0) -> dict:
    key = jax.random.key(seed)
    ks = jax.random.split(key, 20)
    f32 = jnp.float32

    def nrm(k, shape, fan_in):
        return jax.random.normal(k, shape, f32) * (fan_in ** -0.5)

    def gain(k, shape):
        return 1.0 + 0.05 * jax.random.normal(k, shape, f32)

    return {
        "x": jax.random.normal(ks[0], (BATCH, SEQ, D_MODEL), f32),
        "mem": jax.random.normal(ks[1], (BATCH, N_MEM, D_MODEL), f32),
        "mem_norm_g": gain(ks[2], (D_MODEL,)),
        "mix_norm_g": gain(ks[3], (DEPTH, D_MODEL)),
        "ffn_norm_g": gain(ks[4], (DEPTH, D_MODEL)),
        "final_norm_g": gain(ks[5], (D_MODEL,)),
        "a_w_in": nrm(ks[6], (N_A, D_MODEL, A_IN), D_MODEL),
        "a_sinks": 0.5 * jax.random.normal(ks[7], (N_A, A_Q_HEADS), f32),
        "a_w_out": nrm(ks[8], (N_A, MIX_WIDTH, D_MODEL), MIX_WIDTH),
        "b_w_in": nrm(ks[9], (N_B, D_MODEL, B_IN), D_MODEL),
        "b_w_s": nrm(ks[10], (N_B, B_GROUPS, CHUNK, CHUNK), CHUNK),
        "b_bias_s": 1.0 + 0.05 * jax.random.normal(ks[11], (N_B, B_GROUPS, CHUNK), f32),
        "b_ln_g": gain(ks[12], (N_B, B_GROUPS, B_GROUP_DIM)),
        "b_ln_b": 0.02 * jax.random.normal(ks[13], (N_B, B_GROUPS, B_GROUP_DIM), f32),
        "b_w_out": nrm(ks[14], (N_B, MIX_WIDTH, D_MODEL), MIX_WIDTH),
        "w_mem_kv": nrm(ks[15], (DEPTH, D_MODEL, 2 * MEM_WIDTH), D_MODEL),
        "w_gate_up": nrm(ks[16], (DEPTH, D_MODEL, 2 * D_FF), D_MODEL),
        "w_down": nrm(ks[17], (DEPTH, D_FF, D_MODEL), D_FF),
    }


def reference(x, mem, mem_norm_g, mix_norm_g, ffn_norm_g, final_norm_g,
              a_w_in, a_sinks, a_w_out,
              b_w_in, b_w_s, b_bias_s, b_ln_g, b_ln_b, b_w_out,
              w_mem_kv, w_gate_up, w_down):
    b, s, _ = x.shape
    mem_n = rms_norm(mem, mem_norm_g)
    h = x
    for i in range(DEPTH):
        j = i // 2
        xn = rms_norm(h, mix_norm_g[i])
        if i % 2 == 0:
            proj = xn @ a_w_in[j]
            q, k, v, q_mem = jnp.split(proj, [A_Q_W, A_Q_W + A_KV_W, A_Q_W + 2 * A_KV_W], axis=-1)
            mix = sliding_window_attention(
                q.reshape(b, s, A_Q_HEADS, HEAD_DIM),
                k.reshape(b, s, A_KV_HEADS, HEAD_DIM),
                v.reshape(b, s, A_KV_HEADS, HEAD_DIM),
                a_sinks[j])
            w_out = a_w_out[j]
        else:
            proj = xn @ b_w_in[j]
            z, q_mem = jnp.split(proj, [2 * B_WIDTH], axis=-1)
            mix = chunked_spatial_gating(jax.nn.gelu(z), b_w_s[j], b_bias_s[j], b_ln_g[j], b_ln_b[j])
            w_out = b_w_out[j]
        mem_out = memory_attention(q_mem.reshape(b, s, MEM_HEADS, HEAD_DIM), mem_n, w_mem_kv[i])
        h = h + jnp.concatenate([mix, mem_out.astype(mix.dtype)], axis=-1) @ w_out
        hn = rms_norm(h, ffn_norm_g[i])
        gate, up = jnp.split(hn @ w_gate_up[i], 2, axis=-1)
        h = h + (jax.nn.silu(gate) * up) @ w_down[i]
    return rms_norm(h, final_norm_g)
```

```python
import numpy as np
from contextlib import ExitStack
import concourse.bass as bass
import concourse.mybir as mybir
from concourse.bass_utils import run_bass_kernel_spmd

F32 = mybir.dt.float32
BF16 = mybir.dt.bfloat16
AF = mybir.ActivationFunctionType
ALU = mybir.AluOpType
AX = mybir.AxisListType

D = 1024
KC = 8
T = 512
NBLK = 4
SEQ = 8192
NMEM = 256
DFF = 2816
NJ = 22
EPS = 1e-6
NEG = -30000.0
RS = 5
NPAR = 3
SLOT = 4096


def _piece(w_cols):
    K, Fc = w_cols.shape
    kc = K // 128
    return np.ascontiguousarray(w_cols.reshape(kc, 128, Fc).transpose(1, 0, 2)).reshape(128, kc * Fc)


def _layer_pieces(l, inp):
    j = l // 2
    out = []
    if l % 2 == 0:
        w = inp["a_w_in"][j]
        q = w[:, 0:768]
        k = w[:, 768:896]
        v = w[:, 896:1024]
        qm = w[:, 1024:1280]
        kpad = np.concatenate([k, k[:, 64:128], k[:, 0:64]], axis=1)
        out.append(("in", _piece(q[:, 0:512])))
        out.append(("in", _piece(np.concatenate([q[:, 512:768], qm], axis=1))))
        out.append(("in", _piece(kpad)))
        out.append(("in", _piece(v)))
        wo = inp["a_w_out"][j]
    else:
        w = inp["b_w_in"][j]
        u = w[:, 0:768]
        v = w[:, 768:1536]
        qm = w[:, 1536:1792]
        out.append(("in", _piece(u[:, 0:512])))
        out.append(("in", _piece(np.concatenate([u[:, 512:768], qm], axis=1))))
        out.append(("in", _piece(v[:, 0:512])))
        out.append(("in", _piece(v[:, 512:768])))
        wo = inp["b_w_out"][j]
    out.append(("out", _piece(wo[:, 0:512])))
    out.append(("out", _piece(wo[:, 512:1024])))
    wgu = inp["w_gate_up"][l]
    for p in range(NJ // 2):
        j0, j1 = 2 * p, 2 * p + 1
        cols = np.concatenate([wgu[:, j0 * 128:(j0 + 1) * 128], wgu[:, DFF + j0 * 128:DFF + (j0 + 1) * 128],
                               wgu[:, j1 * 128:(j1 + 1) * 128], wgu[:, DFF + j1 * 128:DFF + (j1 + 1) * 128]], axis=1)
        out.append(("gu", _piece(cols)))
    wd = inp["w_down"][l]
    for m in range(8):
        out.append(("dn", _piece(wd[:, m * 128:(m + 1) * 128])))
    return out


def _pack_weights(inp):
    arrs = []
    table = {}
    groups = {}
    off = 0

    def put(key, grp, a):
        nonlocal off
        table[key] = (off, a.shape[1], grp)
        groups[grp] = groups.get(grp, 0) + 1
        arrs.append(a)
        off += a.shape[1]

    for l in range(4):
        put(("mem", l), ("mem",), _piece(inp["w_mem_kv"][l]))
    for l in range(4):
        for i, (fam, a) in enumerate(_layer_pieces(l, inp)):
            put((l, i), (l, fam), a)
    return np.concatenate(arrs, axis=1), table, groups, off


def _weight_table():
    table = {}
    groups = {}
    off = 0

    def put(key, grp, n):
        nonlocal off
        table[key] = (off, n, grp)
        groups[grp] = groups.get(grp, 0) + 1
        off += n

    for l in range(4):
        put(("mem", l), ("mem",), 4096)
    for l in range(4):
        sizes = [("in", 4096), ("in", 4096), ("in", 2048 if l % 2 == 0 else 4096), ("in", 1024 if l % 2 == 0 else 2048),
                 ("out", 4096), ("out", 4096)] + [("gu", 4096)] * 11 + [("dn", 2816)] * 8
        for i, (fam, n) in enumerate(sizes):
            put((l, i), (l, fam), n)
    return table, groups, off


def _pack_params(inp):
    gs = [inp["mem_norm_g"]] + [inp["mix_norm_g"][i] for i in range(4)] + \
         [inp["ffn_norm_g"][i] for i in range(4)] + [inp["final_norm_g"]]
    G = np.stack([g.reshape(KC, 128).T for g in gs], axis=1).reshape(128, 80)
    sk = np.full((2, 16), NEG, np.float32)
    sk[:, 0:12] = inp["a_sinks"]
    sk = np.broadcast_to(sk.reshape(1, 32), (128, 32))
    lg = inp["b_ln_g"].reshape(12, 128).T
    return np.ascontiguousarray(np.concatenate([G, sk, lg], axis=1).astype(np.float32))


def _pack_bt(inp):
    out = np.zeros((12, 128, 384), np.float32)
    for l in range(2):
        for g in range(6):
            i = l * 6 + g
            out[i, :, 0:128] = inp["b_w_s"][l, g].T
            out[i, :, 128:256] = np.broadcast_to(inp["b_ln_b"][l, g][None, :], (128, 128))
            out[i, :, 256:384] = np.broadcast_to(inp["b_bias_s"][l, g][None, :], (128, 128))
    return out


def _consts():
    c = np.zeros((128, 896), np.float32)
    c[:, 0:128] = np.eye(128, dtype=np.float32)
    qi = np.arange(128)[:, None]
    kj = np.arange(128)[None, :]
    prev = np.where(kj > qi, 0.0, NEG)
    cur = np.where(kj <= qi, 0.0, NEG)
    c[:, 128:256] = prev
    c[:, 256:384] = cur
    c[:, 384:512] = NEG
    c[:, 512:640] = cur
    c[:, 640:768] = (qi <= kj).astype(np.float32)
    c[:, 768:896] = 1.0
    return c


class _Op:
    __slots__ = ("eng", "fn", "deps", "signal", "done_sem", "done_val", "dma_key", "idx")


class Sched:
    ENGS = ("pe", "act", "dve", "pool", "sp")

    def __init__(self):
        self.ops = {e: [] for e in self.ENGS}
        self.n = 0
        self.lastw = {}
        self.rd = {}
        self.dma_cnt = {}
        self.dma_keys = []

    def add(self, eng, fn, reads=(), writes=(), dma_key=None, done_val=None):
        op = _Op()
        op.eng = eng
        op.fn = fn
        op.signal = False
        op.dma_key = dma_key
        op.idx = self.n
        op.done_sem = None
        op.done_val = None
        self.n += 1
        cand = []
        for k in reads:
            w = self.lastw.get(k)
            if w is not None:
                cand.append(w)
        for k in writes:
            w = self.lastw.get(k)
            if w is not None:
                cand.append(w)
            r = self.rd.get(k)
            if r:
                for v in r.values():
                    if isinstance(v, list):
                        cand.extend(v)
                    else:
                        cand.append(v)
        best = {}
        dmas = {}
        for d in cand:
            if d is op:
                continue
            if d.dma_key is not None:
                o = dmas.get(d.dma_key)
                if o is None or d.done_val > o.done_val:
                    dmas[d.dma_key] = d
            else:
                if d.eng == "pe" and eng == "pe" and dma_key is None:
                    continue
                o = best.get(d.eng)
                if o is None or d.idx > o.idx:
                    best[d.eng] = d
        op.deps = list(best.values()) + list(dmas.values())
        for d in best.values():
            d.signal = True
        for k in reads:
            r = self.rd.setdefault(k, {})
            if dma_key is not None:
                r.setdefault("dma", []).append(op)
            else:
                r[eng] = op
        for k in writes:
            self.lastw[k] = op
            self.rd[k] = {}
        if dma_key is not None:
            if dma_key not in self.dma_cnt:
                self.dma_cnt[dma_key] = 0
                self.dma_keys.append(dma_key)
            self.dma_cnt[dma_key] += 16
            op.done_sem = ("dma", dma_key)
            op.done_val = done_val if done_val is not None else self.dma_cnt[dma_key]
        self.ops[eng].append(op)
        return op

    def emit(self, nc, stack):
        engs = {"pe": nc.tensor, "act": nc.scalar, "dve": nc.vector, "pool": nc.gpsimd, "sp": nc.sync}
        sems = {}
        for e in self.ENGS:
            sems[e] = stack.enter_context(nc.semaphore("prog_" + e))
        for i, k in enumerate(self.dma_keys):
            sems[("dma", k)] = stack.enter_context(nc.semaphore("dma_%d" % i))
        for e in self.ENGS:
            c = 0
            for op in self.ops[e]:
                if op.dma_key is None and op.signal:
                    c += 1
                    op.done_sem = e
                    op.done_val = c
        with nc.Block() as blk:
            @blk.sync
            def _(sync):
                for s in sems.values():
                    sync.sem_clear(s)
        final = {("dma", k): v for k, v in self.dma_cnt.items() if k[0] == "store"}

        def body(ename):
            def run(e):
                waited = {}
                for op in self.ops[ename]:
                    need = {}
                    for d in op.deps:
                        v = need.get(d.done_sem, 0)
                        if d.done_val > v:
                            need[d.done_sem] = d.done_val
                    for s, v in need.items():
                        if waited.get(s, 0) < v:
                            e.wait_ge(sems[s], v)
                            waited[s] = v
                    ins = op.fn(e)
                    if op.dma_key is not None:
                        ins.then_inc(sems[op.done_sem], 16)
                    elif op.signal:
                        ins.then_inc(sems[ename], 1)
                if ename == "act":
                    for s, v in final.items():
                        e.wait_ge(sems[s], v)
            return run

        with nc.Block() as blk:
            blk.sync(body("sp"))
            blk.scalar(body("act"))
            blk.vector(body("dve"))
            blk.gpsimd(body("pool"))
            blk.tensor(body("pe"))


def build_program(NT=16, layers=(0, 1, 2, 3), mixers=True):
    nc = bass.Bass("TRN2", target_bir_lowering=False)
    wtab, wgroups, WTOT = _weight_table()
    ntok = NT * T
    xT = nc.dram_tensor("xT", [D, ntok], F32, kind="ExternalInput").ap()
    memT = nc.dram_tensor("memT", [D, NMEM], F32, kind="ExternalInput").ap()
    wf = nc.dram_tensor("wf", [128, WTOT], F32, kind="ExternalInput").ap()
    prm = nc.dram_tensor("prm", [128, 124], F32, kind="ExternalInput").ap()
    cst = nc.dram_tensor("cst", [128, 896], F32, kind="ExternalInput").ap()
    btin = nc.dram_tensor("btin", [12, 128, 384], F32, kind="ExternalInput").ap()
    y = nc.dram_tensor("y", [D, ntok], F32, kind="ExternalOutput").ap()
    wb = nc.dram_tensor("wb", [128, WTOT], BF16, kind="Internal").ap()

    S = Sched()
    st = ExitStack()
    with st:
        def sb(name, shape, dt):
            return st.enter_context(nc.sbuf_tensor(name, shape, dt))

        hbuf = [sb("h0", [128, KC, T], F32), sb("h1", [128, KC, T], F32)]
        sqb = sb("sqb", [128, 2, T], BF16)
        rs_s = sb("rs_s", [128, T], F32)
        rstd = sb("rstd", [128, T], F32)
        xn = sb("xn", [128, KC, T], BF16)
        qT = sb("qT", [128, 6, T], BF16)
        uT = qT
        qmT = sb("qmT", [128, 2, T], BF16)
        kpad = [sb("kpad%d" % i, [128, 4, 640], BF16) for i in range(2)]
        vpad = [sb("vpad%d" % i, [128, 5, 4, 128], BF16) for i in range(2)]
        kmp = sb("kmp", [128, 4, 4, 256], BF16)
        vmp = sb("vmp", [128, 4, 2, 4, 128], BF16)
        Pb = [sb("Pb%d" % i, [128, 4, 256], BF16) for i in range(NPAR)]
        PTs = [sb("PTs%d" % i, [128, 4, 2, 128], BF16) for i in range(NPAR)]
        dgb = [sb("dg%d" % i, [128, 4, 128], BF16) for i in range(NPAR)]
        stt_ = [sb("stt%d" % i, [128, 8, 4], F32) for i in range(NPAR)]
        actb = sb("actb", [128, NJ, T], BF16)
        sgb = sb("sgb", [128, 2, T], F32)
        vgb = sb("vgb", [128, 3, 768], F32)
        nbuf = sb("nbuf", [128, NBLK, 768], BF16)
        bnst = sb("bnst", [128, 3, 6, 6], F32)
        mvb = sb("mvb", [128, 3, 6, 2], F32)
        lnr = sb("lnr", [128, 3, 2, 6], F32)
        tmpf = sb("tmpf", [128, 2, T], F32)
        ob_ap = [sgb[:, 0, :], sgb[:, 1, :], tmpf[:, 0, :], tmpf[:, 1, :]]
        ob_key = [("sg", 0), ("sg", 1), ("tmp", 0), ("tmp", 1)]
        wring = sb("wring", [128, RS, SLOT], BF16)
        cstf = sb("cstf", [128, 896], F32)
        identb = sb("identb", [128, 128], BF16)
        onesb = sb("onesb", [128, 128], BF16)
        maskb = sb("maskb", [128, 2, 256], BF16)
        WsTb = sb("WsTb", [128, 12, 128], BF16)
        BT = sb("BT", [128, 12, 128], F32)
        prmb = sb("prmb", [128, 124], F32)
        nsink = sb("nsink", [128, 32], F32)
        ps = st.enter_context(nc.psum_tensor("ps", [128, 8, 512], F32))

        Gc = lambda col: prmb[:, col:col + 1]
        sink_ap = lambda c0: prmb[:, 80 + c0:80 + c0 + 4]
        nsink_ap = lambda c0: nsink[:, c0:c0 + 4]
        lg_ap = lambda i: prmb[:, 112 + i:112 + i + 1]

        bank_ptr = [0]

        def nb():
            b = bank_ptr[0]
            bank_ptr[0] = (b + 1) % 8
            return b

        def nb2():
            if bank_ptr[0] % 2:
                bank_ptr[0] = (bank_ptr[0] + 1) % 8
            b = bank_ptr[0]
            bank_ptr[0] = (b + 2) % 8
            return b

        def mm(out, lhsT, rhs, start, stop, reads, writes):
            S.add("pe", lambda e, o=out, l=lhsT, r=rhs, a=start, b=stop: e.matmul(o, l, r, start=a, stop=b),
                  reads=reads, writes=writes)

        def act(out, in_, func, reads, writes, scale=1.0, bias=None, accum_out=None):
            def f(e, o=out, i=in_, fn=func, sc=scale, bi=bias, ac=accum_out):
                kw = {}
                if bi is not None:
                    kw["bias"] = bi
                if ac is not None:
                    kw["accum_out"] = ac
                return e.activation(out=o, in_=i, func=fn, scale=sc, **kw)
            S.add("act", f, reads=reads, writes=writes)

        def tt(eng, out, in0, in1, op, reads, writes):
            S.add(eng, lambda e, o=out, a=in0, b=in1, p=op: e.tensor_tensor(out=o, in0=a, in1=b, op=p),
                  reads=reads, writes=writes)

        def ts(eng, out, in0, s1, s2, op0, op1, reads, writes):
            def f(e, o=out, a=in0, x=s1, y_=s2, p0=op0, p1=op1):
                if p1 is None:
                    return e.tensor_scalar(out=o, in0=a, scalar1=x, scalar2=None, op0=p0)
                return e.tensor_scalar(out=o, in0=a, scalar1=x, scalar2=y_, op0=p0, op1=p1)
            S.add(eng, f, reads=reads, writes=writes)

        def stt(out, in0, scalar, in1, op0, op1, reads, writes):
            S.add("dve", lambda e, o=out, a=in0, s=scalar, b=in1, p0=op0, p1=op1:
                  e.scalar_tensor_tensor(out=o, in0=a, scalar=s, in1=b, op0=p0, op1=p1),
                  reads=reads, writes=writes)

        def cp(eng, out, in_, reads, writes):
            if eng == "act":
                S.add("act", lambda e, o=out, i=in_: e.copy(out=o, in_=i), reads=reads, writes=writes)
            else:
                S.add(eng, lambda e, o=out, i=in_: e.tensor_copy(out=o, in_=i), reads=reads, writes=writes)

        def recip(out, in_, reads, writes):
            S.add("dve", lambda e, o=out, i=in_: e.reciprocal(out=o, in_=i), reads=reads, writes=writes)

        def recip_fast(out, in_, reads, writes, Tn=T):
            S.add("dve", lambda e, o=out, i=in_, sc=tmpf[:, 0, 0:Tn]: e.reciprocal_approx_accurate(out=o, in_=i, scratch=sc),
                  reads=reads + [("tmp", 0)], writes=writes + [("tmp", 0)])

        def memset(eng, ap, val, writes):
            S.add(eng, lambda e, a=ap, v=val: e.memset(a, v), writes=writes)

        def dma(eng, out, in_, key, reads, writes, done_val=None, **kw):
            S.add(eng, lambda e, o=out, i=in_, k=kw: e.dma_start(out=o, in_=i, **k),
                  reads=reads, writes=writes, dma_key=key, done_val=done_val)

        piece_ctr = [0]

        def load_piece(key):
            off, n, grp = wtab[key]
            slot = piece_ctr[0] % RS
            piece_ctr[0] += 1
            dma("sp", wring[:, slot, 0:n], wb[:, off:off + n], ("w", slot),
                reads=[("wb", key)], writes=[("w", slot)])
            return slot, n

        def wview(slot, n, kc):
            return wring[:, slot, 0:n].rearrange("p (k f) -> p k f", k=kc)

        dma("sp", cstf[:, :], cst[:, :], ("c", 0), reads=[], writes=[("cstf",)])
        dma("sp", prmb[:, :], prm[:, :], ("c", 1), reads=[], writes=[("prm",)])
        cp("dve", identb[:, :], cstf[:, 0:128], [("cstf",)], [("identb",)])
        cp("dve", maskb[:, :, :], cstf[:, 128:640].rearrange("p (a b) -> p a b", a=2), [("cstf",)], [("maskb",)])
        cp("dve", onesb[:, :], cstf[:, 768:896], [("cstf",)], [("onesb",)])
        ts("dve", nsink[:, :], prmb[:, 80:112], -1.0, None, ALU.mult, None, [("prm",)], [("nsink",)])
        for i in range(2):
            memset("pool", kpad[i][:, :, :], 0.0, [("kp", i, v) for v in range(4)])
            memset("pool", vpad[i][:, :, :, :], 0.0, [("vp", i, b) for b in range(5)])
        memset("pool", kmp[:, :, :, :], 0.0, [("kmp", l) for l in range(4)])
        memset("pool", vmp[:, :, :, :, :], 0.0, [("vmp", l) for l in range(4)])

        order = [("mem", l) for l in range(4)]
        for l in range(4):
            order += [(l, i) for i in range(25)]
        for key in order:
            off, n, grp = wtab[key]
            dma("pool", wb[:, off:off + n], wf[:, off:off + n], ("cast",) + grp, reads=[],
                writes=[("wb", key)], done_val=16 * wgroups[grp], max_dma_last_dim=8192)

        sq_ctr = [0]

        def rmsnorm(hb, gcol0, Tn, dst_fn, dst_keys_fn, final=False):
            bank = nb()
            for kc in range(KC):
                i = sq_ctr[0] % 2
                sq_ctr[0] += 1
                act(sqb[:, i, 0:Tn], hbuf[hb][:, kc, 0:Tn], AF.Square, [("h", hb, kc)], [("sq", i)])
                mm(ps[:, bank, 0:Tn], onesb[:, :], sqb[:, i, 0:Tn], kc == 0, kc == KC - 1,
                   [("sq", i), ("onesb",)], [("ps", bank)])
            act(rs_s[:, 0:Tn], ps[:, bank, 0:Tn], AF.Ln, [("ps", bank), ("epsb",)], [("rs",)],
                scale=1.0 / D, bias=epsb[:, 0:1])
            act(rstd[:, 0:Tn], rs_s[:, 0:Tn], AF.Exp, [("rs",)], [("rstd",)], scale=-0.5)
            for kc in range(KC):
                stt(dst_fn(kc), hbuf[hb][:, kc, 0:Tn], Gc(gcol0 + kc), rstd[:, 0:Tn], ALU.mult, ALU.mult,
                    [("h", hb, kc), ("prm",), ("rstd",)], dst_keys_fn(kc))

        epsb = sb("epsb", [128, 1], F32)
        memset("dve", epsb[:, :], EPS, [("epsb",)])

        def proj_chunk(w3, col0, nk, rhs_fn, rhs_keys_fn, wkey, Tn=T):
            bank = nb()
            for k in range(nk):
                mm(ps[:, bank, 0:Tn], w3[:, k, col0:col0 + 128], rhs_fn(k), k == 0, k == nk - 1,
                   [wkey] + rhs_keys_fn(k), [("ps", bank)])
            return bank

        def proj_chunks_kouter(w3, col0s, nk, rhs_fn, rhs_keys_fn, wkey, Tn=T):
            banks = [nb() for _ in col0s]
            for k in range(nk):
                for b, c0 in zip(banks, col0s):
                    mm(ps[:, b, 0:Tn], w3[:, k, c0:c0 + 128], rhs_fn(k), k == 0, k == nk - 1,
                       [wkey] + rhs_keys_fn(k), [("ps", b)])
            return banks

        ev_ctr = [0]

        def evac_scaled(out, bank, scale, writes, Tn=T):
            ev_ctr[0] += 1
            if ev_ctr[0] % 2:
                act(out, ps[:, bank, 0:Tn], AF.Copy, [("ps", bank)], writes, scale=scale)
            else:
                ts("dve", out, ps[:, bank, 0:Tn], scale, None, ALU.mult, None, [("ps", bank)], writes)

        xnk = lambda k: [("xn", k)]

        dma("sp", hbuf[1][:, :, 0:NMEM], memT.rearrange("(k p) t -> p k t", p=128), ("x", 1),
            reads=[], writes=[("h", 1, k) for k in range(KC)])
        rmsnorm(1, 0, NMEM, lambda kc: xn[:, kc, 0:NMEM], lambda kc: [("xn", kc)])
        for l in range(4):
            slot, n = load_piece(("mem", l))
            w3 = wview(slot, n, KC)
            for cm in range(2):
                bank = proj_chunk(w3, cm * 128, KC, lambda k: xn[:, k, 0:NMEM], xnk, ("w", slot), Tn=NMEM)
                cp("act", kmp[0:64, l, 2 * cm, :], ps[0:64, bank, 0:NMEM], [("ps", bank)], [("kmp", l)])
                cp("dve", kmp[64:128, l, 2 * cm + 1, :], ps[64:128, bank, 0:NMEM], [("ps", bank)], [("kmp", l)])
            for blk in range(2):
                bank = nb()
                for k in range(KC):
                    mm(ps[:, bank, 0:256], xn[:, k, blk * 128:(blk + 1) * 128], w3[:, k, 256:512], k == 0, k == KC - 1,
                       [("w", slot), ("xn", k)], [("ps", bank)])
                src = ps[:, bank, 0:256].rearrange("p (k o d) -> p k o d", k=2, o=2)
                dst = vmp[:, l, blk, :, :].rearrange("p (k o) d -> p k o d", o=2)
                cp("act", dst[:, :, 0, 0:64], src[:, :, 0, :], [("ps", bank)], [("vmp", l)])
                cp("dve", dst[:, :, 1, 64:128], src[:, :, 1, :], [("ps", bank)], [("vmp", l)])

        if mixers:
            for i in range(12):
                sgi = i % 2
                stg = sgb[:, sgi, 0:384]
                wm = tmpf[:, sgi, 0:128]
                dma("sp", stg, btin[i, :, :], ("bt", sgi), reads=[], writes=[("sg", sgi)])
                tt("dve", wm, stg[:, 0:128], cstf[:, 640:768], ALU.mult,
                   [("sg", sgi), ("cstf",)], [("tmp", sgi)])
                cp("dve", WsTb[:, i, :], wm, [("tmp", sgi)], [("WsTb", i)])
                bank = nb()
                mm(ps[:, bank, 0:128], stg[:, 128:256], wm, True, False,
                   [("sg", sgi), ("tmp", sgi)], [("ps", bank)])
                mm(ps[:, bank, 0:128], cstf[0:1, 768:896], stg[0:1, 256:384], False, True,
                   [("sg", sgi), ("cstf",)], [("ps", bank)])
                cp("act", BT[:, i, :], ps[:, bank, 0:128], [("ps", bank)], [("BT", i)])

        par = [0]

        def make_group(heads, sink_c0, out_ap, out_keys, same_kv=False):
            gidx = par[0]
            i = par[0] % NPAR
            par[0] += 1
            stv = stt_[i]
            nm, negm, rsum, stx, den, rr = (stv[:, j, :] for j in range(6))
            state = {}

            def stA():
                b0 = 2 * (gidx % 2)
                state["sc"] = b0
                for pr in range(2):
                    hd = heads[2 * pr]
                    bank = b0 + pr
                    o2 = ps[:, bank, :].rearrange("p (a k) -> p a k", a=2)
                    mm(o2, hd["q"], hd["k2"], True, hd["mask"] is None, hd["keys_qk"], [("ps", bank)])
                    if hd["mask"] is not None:
                        mm(o2, identb[:, :], hd["mask"].unsqueeze(1).broadcast_to([128, 2, 256]), False, True,
                           [("identb",), ("maskb",)], [("ps", bank)])

            def stB1():
                b0 = state["sc"]
                scv = ps[:, b0:b0 + 2, :].rearrange("p b (s k) -> p (b s) k", s=2)
                S.add("dve", lambda e: e.tensor_reduce(out=nm, in_=scv, axis=AX.X, op=ALU.max, negate=True),
                      reads=[("ps", b0), ("ps", b0 + 1)], writes=[("st", i, "nm")])
                tt("dve", negm, nm, nsink_ap(sink_c0), ALU.min, [("st", i, "nm"), ("nsink",)], [("st", i, "negm")])
                tt("dve", stx, negm, sink_ap(sink_c0), ALU.add, [("st", i, "negm"), ("prm",)], [("st", i, "stx")])

            def stB2():
                b0 = state["sc"]
                for s in range(4):
                    bank = b0 + s // 2
                    col = (s % 2) * 256
                    act(Pb[i][:, s, :], ps[:, bank, col:col + 256], AF.Exp, [("ps", bank), ("st", i, "negm")],
                        [("P", i), ("st", i, "rsum")], bias=negm[:, s:s + 1], accum_out=rsum[:, s:s + 1])
                act(stx, stx, AF.Exp, [("st", i, "stx")], [("st", i, "stx")])

            def stB3():
                tt("dve", den, rsum, stx, ALU.add, [("st", i, "rsum"), ("st", i, "stx")], [("st", i, "den")])
                recip(rr, den, [("st", i, "den")], [("st", i, "rr")])
                for s in range(4):
                    ts("pool", dgb[i][:, s, :], identb[:, :], rr[:, s:s + 1], 1.0, ALU.mult, ALU.mult,
                       [("identb",), ("st", i, "rr")], [("dg", i)])

            def stC():
                p0 = 4
                for s in range(4):
                    bank = p0 + s // 2
                    for kb in range(2):
                        col = (s % 2) * 256 + kb * 128
                        mm(ps[:, bank, col:col + 128], Pb[i][:, s, kb * 128:(kb + 1) * 128], dgb[i][:, s, :], True, True,
                           [("P", i), ("dg", i)], [("ps", bank)])
                cp("act",
                   PTs[i][:, :, :, :].rearrange("p (b s) k q -> p b (s k q)", b=2), ps[:, p0:p0 + 2, :],
                   [("ps", p0), ("ps", p0 + 1)], [("PT", i)])

            def stD():
                ob = 6 + gidx % 2
                if same_kv:
                    o2 = ps[:, ob, 0:256].rearrange("p (a q) -> p a q", a=2)
                    ptv = PTs[i][:, :, :, :].rearrange("p (a o) k q -> p o k a q", o=2)
                    cnt = 0
                    for o_ in range(2):
                        hd = heads[o_]
                        for kb in range(2):
                            mm(o2, hd["v%d" % kb], ptv[:, o_, kb, :, :], cnt == 0, cnt == 3,
                               [("PT", i)] + hd["keys_v"], [("ps", ob)])
                            cnt += 1
                    cp("dve", out_ap, o2, [("ps", ob)], out_keys)
                    return
                for pr in range(2):
                    cnt = 0
                    for s in (2 * pr, 2 * pr + 1):
                        hd = heads[s]
                        for kb in range(2):
                            mm(ps[:, ob, pr * 128:(pr + 1) * 128], hd["v%d" % kb], PTs[i][:, s, kb, :], cnt == 0, cnt == 3,
                               [("PT", i)] + hd["keys_v"], [("ps", ob)])
                            cnt += 1
                cp("dve", out_ap, ps[:, ob, 0:256].rearrange("p (a q) -> p a q", a=2), [("ps", ob)], out_keys)

            return [stA, stB1, stB2, stB3, stC, stD]

        SKEW = (0, 1, 1, 2, 3, 4)

        deferred = []

        def run_deferred():
            while deferred:
                deferred.pop(0)()

        def run_groups(groups, extra=()):
            n = len(groups)
            for step in range(max(n + SKEW[-1], len(extra))):
                for sidx in range(6):
                    g = step - SKEW[sidx]
                    if 0 <= g < n:
                        groups[g][sidx]()
                if step < len(extra):
                    extra[step]()

        def mem_heads(l, blk):
            hs = []
            for hm in range(4):
                hs.append(dict(q=qmT[:, hm // 2, blk * 128:(blk + 1) * 128], k2=kmp[:, l, 2 * (hm // 2):2 * (hm // 2) + 2, :], mask=None,
                               v0=vmp[:, l, 0, hm, :], v1=vmp[:, l, 1, hm, :],
                               keys_qk=[("qm", hm // 2), ("kmp", l)], keys_v=[("vmp", l)]))
            return hs

        def out_proj_and_ffn(l, hb):
            for pi in range(2):
                slot, n = load_piece((l, 4 + pi))
                w3 = wview(slot, n, KC)
                for m_ in range(4):
                    m = pi * 4 + m_
                    bank = proj_chunk(w3, m_ * 128, KC, lambda k: xn[:, k, :], xnk, ("w", slot))
                    tt("dve", hbuf[hb][:, m, :], ps[:, bank, :], hbuf[hb][:, m, :], ALU.add,
                       [("ps", bank), ("h", hb, m)], [("h", hb, m)])
            ffn(l, hb)

        def ffn(l, hb):
            rmsnorm(hb, 8 * (5 + l), T, lambda kc: xn[:, kc, :], lambda kc: [("xn", kc)])
            for pi in range(NJ // 2):
                slot, n = load_piece((l, 6 + pi))
                w3 = wview(slot, n, KC)
                if pi == 0:
                    b4 = proj_chunks_kouter(w3, [0, 128, 256, 384], KC, lambda k: xn[:, k, :], xnk, ("w", slot))
                for jj in range(2):
                    j = 2 * pi + jj
                    if pi == 0:
                        bg, bu = b4[2 * jj], b4[2 * jj + 1]
                    else:
                        bg = proj_chunk(w3, jj * 256, KC, lambda k: xn[:, k, :], xnk, ("w", slot))
                        bu = proj_chunk(w3, jj * 256 + 128, KC, lambda k: xn[:, k, :], xnk, ("w", slot))
                    si = j % 2
                    act(sgb[:, si, :], ps[:, bg, :], AF.Silu, [("ps", bg)], [("sg", si)])
                    tt("dve", actb[:, j, :], ps[:, bu, :], sgb[:, si, :], ALU.mult,
                       [("ps", bu), ("sg", si)], [("act", j)])
            for m in range(8):
                slot, n = load_piece((l, 17 + m))
                w3 = wview(slot, n, NJ)
                bank = proj_chunk(w3, 0, NJ, lambda k: actb[:, k, :], lambda k: [("act", k)], ("w", slot))
                tt("dve", hbuf[hb][:, m, :], ps[:, bank, :], hbuf[hb][:, m, :], ALU.add,
                   [("ps", bank), ("h", hb, m)], [("h", hb, m)])

        def layer_A(l, hb, t):
            la = l // 2
            rmsnorm(hb, 8 * (1 + l), T, lambda kc: xn[:, kc, :], lambda kc: [("xn", kc)])
            rhs = lambda k: xn[:, k, :]
            s0, n0 = load_piece((l, 0))
            w3 = wview(s0, n0, KC)
            banks = proj_chunks_kouter(w3, [0, 128, 256, 384], KC, rhs, xnk, ("w", s0))
            for c in range(4):
                evac_scaled(qT[:, c, :], banks[c], 0.125, [("q", c)])
            s1, n1 = load_piece((l, 1))
            w3 = wview(s1, n1, KC)
            for c in range(2):
                bank = proj_chunk(w3, c * 128, KC, rhs, xnk, ("w", s1))
                evac_scaled(qT[:, 4 + c, :], bank, 0.125, [("q", 4 + c)])
            for c in range(2):
                bank = proj_chunk(w3, 256 + c * 128, KC, rhs, xnk, ("w", s1))
                evac_scaled(qmT[:, c, :], bank, 0.125, [("qm", c)])
            s2, n2 = load_piece((l, 2))
            w3 = wview(s2, n2, KC)
            bka = proj_chunk(w3, 0, KC, rhs, xnk, ("w", s2))
            bkb = proj_chunk(w3, 128, KC, rhs, xnk, ("w", s2))
            cp("act", kpad[la][0:64, 0, 128:640], ps[0:64, bka, :], [("ps", bka)], [("kp", la, 0)])
            cp("dve", kpad[la][64:128, 3, 128:640], ps[64:128, bka, :], [("ps", bka)], [("kp", la, 3)])
            cp("act", kpad[la][0:64, 2, 128:640], ps[0:64, bkb, :], [("ps", bkb)], [("kp", la, 2)])
            cp("dve", kpad[la][64:128, 1, 128:640], ps[64:128, bkb, :], [("ps", bkb)], [("kp", la, 1)])
            s3, n3 = load_piece((l, 3))
            w3 = wview(s3, n3, KC)
            for blk in range(NBLK):
                bank = nb()
                for k in range(KC):
                    mm(ps[:, bank, 0:128], xn[:, k, blk * 128:(blk + 1) * 128], w3[:, k, 0:128], k == 0, k == KC - 1,
                       [("w", s3), ("xn", k)], [("ps", bank)])
                src = ps[:, bank, 0:128].rearrange("p (k d) -> p k d", k=2)
                dst = vpad[la][:, 1 + blk, :, :].rearrange("p (k o) d -> p k o d", o=2)
                cp("act", dst[:, :, 0, 0:64], src, [("ps", bank)], [("vp", la, 1 + blk)])
                cp("dve", dst[:, :, 1, 64:128], src, [("ps", bank)], [("vp", la, 1 + blk)])
            groups = []
            for blk in range(NBLK):
                first = (t == 0 and blk == 0)
                for gi in range(3):
                    hs = []
                    for s in range(4):
                        h = 4 * gi + s
                        var = (h // 6) * 2 + (h % 2)
                        hs.append(dict(q=qT[:, h // 2, blk * 128:(blk + 1) * 128],
                                       k2=kpad[la][:, 2 * (h // 6):2 * (h // 6) + 2, blk * 128:blk * 128 + 256],
                                       mask=maskb[:, 1 if first else 0, :],
                                       v0=vpad[la][:, blk, var, :], v1=vpad[la][:, blk + 1, var, :],
                                       keys_qk=[("q", h // 2), ("kp", la, 2 * (h // 6)), ("kp", la, 2 * (h // 6) + 1)],
                                       keys_v=[("vp", la, blk), ("vp", la, blk + 1)]))
                    groups.append(make_group(hs, la * 16 + gi * 4, xn[:, 2 * gi:2 * gi + 2, blk * 128:(blk + 1) * 128],
                                             [("xn", 2 * gi), ("xn", 2 * gi + 1)], same_kv=(gi != 1)))
                groups.append(make_group(mem_heads(l, blk), 12, xn[:, 6:8, blk * 128:(blk + 1) * 128],
                                         [("xn", 6), ("xn", 7)]))
            run_deferred()
            run_groups(groups)
            cp("pool", kpad[la][:, :, 0:128], kpad[la][:, :, 512:640], [("kp", la, v) for v in range(4)],
               [("kp", la, v) for v in range(4)])
            cp("pool", vpad[la][:, 0, :, :], vpad[la][:, 4, :, :], [("vp", la, 4)], [("vp", la, 0)])
            out_proj_and_ffn(l, hb)

        def layer_B(l, hb, t):
            lb = l // 2
            rmsnorm(hb, 8 * (1 + l), T, lambda kc: xn[:, kc, :], lambda kc: [("xn", kc)])
            rhs = lambda k: xn[:, k, :]
            s2, n2 = load_piece((l, 2))
            s3, n3 = load_piece((l, 3))
            s0, n0 = load_piece((l, 0))
            s1, n1 = load_piece((l, 1))
            w3a = wview(s2, n2, KC)
            w3b = wview(s3, n3, KC)
            w30 = wview(s0, n0, KC)
            w31 = wview(s1, n1, KC)
            chunks = [(w30, s0, c * 128, "u", c) for c in range(4)] + \
                     [(w31, s1, c * 128, "u", 4 + c) for c in range(2)] + \
                     [(w31, s1, 256 + c * 128, "qm", c) for c in range(2)]
            def ln_chain(blk, vi):
                for g in range(6):
                    S.add("dve", lambda e, o=bnst[:, vi, g, :], a=vgb[:, vi, g * 128:(g + 1) * 128]: e.bn_stats(out=o, in_=a),
                          reads=[("vg", vi, 0), ("vg", vi, 1)], writes=[("bn", vi, g)])
                    S.add("dve", lambda e, o=mvb[:, vi, g, :], a=bnst[:, vi, g, :]: e.bn_aggr(out=o, in_=a),
                          reads=[("bn", vi, g)], writes=[("mv", vi)])
                act(lnr[:, vi, 0, :], mvb[:, vi, :, 1], AF.Ln, [("mv", vi), ("epsb",)], [("lnr", vi, 0)],
                    bias=epsb[:, 0:1])
                act(lnr[:, vi, 1, :], lnr[:, vi, 0, :], AF.Exp, [("lnr", vi, 0)], [("lnr", vi, 1)], scale=-0.5)
                stt(lnr[:, vi, 0, :], mvb[:, vi, :, 0], -1.0, lnr[:, vi, 1, :], ALU.mult, ALU.mult,
                    [("mv", vi), ("lnr", vi, 1)], [("lnr", vi, 0)])
                for g in range(6):
                    ts("pool", nbuf[:, blk, g * 128:(g + 1) * 128], vgb[:, vi, g * 128:(g + 1) * 128],
                       lnr[:, vi, 1, g:g + 1], lnr[:, vi, 0, g:g + 1], ALU.mult, ALU.add,
                       [("vg", vi, 0), ("vg", vi, 1), ("lnr", vi, 0), ("lnr", vi, 1)], [("n", blk)])

            for blk in range(NBLK):
                ba = nb()
                bb = nb()
                if blk == 0:
                    ba1 = nb()
                    bb1 = nb()
                    for k in range(KC):
                        for (bq, wq, nn, bl, sk) in ((ba, w3a, 512, 0, s2), (bb, w3b, 256, 0, s3), (ba1, w3a, 512, 1, s2), (bb1, w3b, 256, 1, s3)):
                            mm(ps[:, bq, 0:nn], xn[:, k, bl * 128:(bl + 1) * 128], wq[:, k, :], k == 0, k == KC - 1,
                               [("w", sk), ("xn", k)], [("ps", bq)])
                elif blk == 1:
                    ba, bb = ba1, bb1
                else:
                    for k in range(KC):
                        mm(ps[:, ba, :], xn[:, k, blk * 128:(blk + 1) * 128], w3a[:, k, :], k == 0, k == KC - 1,
                           [("w", s2), ("xn", k)], [("ps", ba)])
                    for k in range(KC):
                        mm(ps[:, bb, 0:256], xn[:, k, blk * 128:(blk + 1) * 128], w3b[:, k, :], k == 0, k == KC - 1,
                           [("w", s3), ("xn", k)], [("ps", bb)])
                vi = blk % 3
                if blk == 3:
                    ln_chain(0, 0)
                act(vgb[:, vi, 0:512], ps[:, ba, :], AF.Gelu_apprx_tanh, [("ps", ba)], [("vg", vi, 0)])
                act(vgb[:, vi, 512:768], ps[:, bb, 0:256], AF.Gelu_apprx_tanh, [("ps", bb)], [("vg", vi, 1)])
                for (w3c, sc, col0, kind, ci) in chunks[2 * blk:2 * blk + 2]:
                    bank = proj_chunk(w3c, col0, KC, rhs, xnk, ("w", sc))
                    if kind == "u":
                        act(uT[:, ci, :], ps[:, bank, :], AF.Gelu_apprx_tanh, [("ps", bank)], [("q", ci)])
                    else:
                        evac_scaled(qmT[:, ci, :], bank, 0.125, [("qm", ci)])
            ln_chains = [lambda: ln_chain(1, 1), lambda: ln_chain(2, 2), lambda: ln_chain(3, 0)]
            groups = []
            for blk in range(NBLK):
                groups.append(make_group(mem_heads(l, blk), 12, xn[:, 6:8, blk * 128:(blk + 1) * 128],
                                         [("xn", 6), ("xn", 7)]))
            run_deferred()
            run_groups(groups, extra=ln_chains)
            for g in range(6):
                bank = nb()
                idx = lb * 6 + g
                for blk in range(NBLK):
                    mm(ps[:, bank, blk * 128:(blk + 1) * 128], nbuf[:, blk, g * 128:(g + 1) * 128], WsTb[:, idx, :], True, True,
                       [("n", blk), ("WsTb", idx)], [("ps", bank)])
                ti = g % 2
                stt(tmpf[:, ti, :].rearrange("p (b t) -> p b t", b=NBLK),
                    ps[:, bank, :].rearrange("p (b t) -> p b t", b=NBLK), lg_ap(idx),
                    BT[:, idx, :].unsqueeze(1).broadcast_to([128, NBLK, 128]), ALU.mult, ALU.add,
                    [("ps", bank), ("prm",), ("BT", idx)], [("tmp", ti)])
                tt("pool", xn[:, g, :], tmpf[:, ti, :], uT[:, g, :], ALU.mult, [("tmp", ti), ("q", g)], [("xn", g)])
            out_proj_and_ffn(l, hb)

        xTv = xT.rearrange("(k p) t -> p k t", p=128)
        dma("sp", hbuf[0][:, :, :], xTv[:, :, 0:T], ("x", 0), reads=[], writes=[("h", 0, k) for k in range(KC)])
        if NT > 1:
            dma("sp", hbuf[1][:, :, :], xTv[:, :, T:2 * T], ("x", 1), reads=[],
                writes=[("h", 1, k) for k in range(KC)])
        ob_ctr = [0]

        def final_norm(t):
            hb = t % 2
            bank = nb()
            for kc in range(KC):
                i = sq_ctr[0] % 2
                sq_ctr[0] += 1
                act(sqb[:, i, :], hbuf[hb][:, kc, :], AF.Square, [("h", hb, kc)], [("sq", i)])
                mm(ps[:, bank, :], onesb[:, :], sqb[:, i, :], kc == 0, kc == KC - 1, [("sq", i), ("onesb",)], [("ps", bank)])
            act(rs_s[:, :], ps[:, bank, :], AF.Ln, [("ps", bank), ("epsb",)], [("rs",)], scale=1.0 / D, bias=epsb[:, 0:1])
            act(rstd[:, :], rs_s[:, :], AF.Exp, [("rs",)], [("rstd",)], scale=-0.5)
            for kc in range(KC):
                oi = ob_ctr[0] % 4
                ob_ctr[0] += 1
                stt(ob_ap[oi], hbuf[hb][:, kc, :], Gc(72 + kc), rstd[:, :], ALU.mult, ALU.mult,
                    [("h", hb, kc), ("prm",), ("rstd",)], [ob_key[oi]])
                dma("sp", y[kc * 128:(kc + 1) * 128, t * T:(t + 1) * T], ob_ap[oi], ("store", oi),
                    reads=[ob_key[oi]], writes=[])
            if t + 2 < NT:
                dma("sp", hbuf[hb][:, :, :], xTv[:, :, (t + 2) * T:(t + 3) * T], ("x", hb), reads=[],
                    writes=[("h", hb, k) for k in range(KC)])

        for t in range(NT):
            hb = t % 2
            for l in layers:
                if not mixers:
                    ffn(l, hb)
                elif l % 2 == 0:
                    layer_A(l, hb, t)
                else:
                    layer_B(l, hb, t)
            if t + 1 < NT and mixers:
                deferred.append(lambda t=t: final_norm(t))
            else:
                final_norm(t)

        S.emit(nc, st)
    return nc


_CACHE = {}


def kernel(**inputs):
    inp = {k: np.asarray(v) for k, v in inputs.items()}
    x = inp["x"]
    mem = inp["mem"]
    B = x.shape[0]
    wfull, _, _, _ = _pack_weights(inp)
    prm = _pack_params(inp)
    bt = _pack_bt(inp)
    cst = _consts()
    if "nc" not in _CACHE:
        _CACHE["nc"] = build_program()
    nc = _CACHE["nc"]
    in_maps = []
    for b in range(B):
        in_maps.append({
            "xT": np.ascontiguousarray(x[b].T),
            "memT": np.ascontiguousarray(mem[b].T),
            "wf": wfull, "prm": prm, "cst": cst, "btin": bt,
        })
    res = run_bass_kernel_spmd(nc, in_maps, core_ids=list(range(B)))
    out = np.stack([np.ascontiguousarray(res.results[b]["y"].T) for b in range(B)], axis=0)
    return out.astype(np.float32)
```

```python
import numpy as np
from contextlib import ExitStack
import concourse.bass as bass
import concourse.mybir as mybir
from concourse.bass_utils import run_bass_kernel_spmd

F32 = mybir.dt.float32
BF16 = mybir.dt.bfloat16
AF = mybir.ActivationFunctionType
ALU = mybir.AluOpType
AX = mybir.AxisListType

D = 1024
KC = 8
T = 512
NBLK = 4
SEQ = 8192
NMEM = 256
DFF = 2816
NJ = 22
EPS = 1e-6
NEG = -30000.0
RS = 5
NPAR = 3
SLOT = 4096


def _piece(w_cols):
    K, Fc = w_cols.shape
    kc = K // 128
    return np.ascontiguousarray(w_cols.reshape(kc, 128, Fc).transpose(1, 0, 2)).reshape(128, kc * Fc)


def _layer_pieces(l, inp):
    j = l // 2
    out = []
    if l % 2 == 0:
        w = inp["a_w_in"][j]
        q = w[:, 0:768]
        k = w[:, 768:896]
        v = w[:, 896:1024]
        qm = w[:, 1024:1280]
        kpad = np.concatenate([k, k[:, 64:128], k[:, 0:64]], axis=1)
        out.append(("in", _piece(q[:, 0:512])))
        out.append(("in", _piece(np.concatenate([q[:, 512:768], qm], axis=1))))
        out.append(("in", _piece(kpad)))
        out.append(("in", _piece(v)))
        wo = inp["a_w_out"][j]
    else:
        w = inp["b_w_in"][j]
        u = w[:, 0:768]
        v = w[:, 768:1536]
        qm = w[:, 1536:1792]
        out.append(("in", _piece(u[:, 0:512])))
        out.append(("in", _piece(np.concatenate([u[:, 512:768], qm], axis=1))))
        out.append(("in", _piece(v[:, 0:512])))
        out.append(("in", _piece(v[:, 512:768])))
        wo = inp["b_w_out"][j]
    out.append(("out", _piece(wo[:, 0:512])))
    out.append(("out", _piece(wo[:, 512:1024])))
    wgu = inp["w_gate_up"][l]
    for p in range(NJ // 2):
        j0, j1 = 2 * p, 2 * p + 1
        cols = np.concatenate([wgu[:, j0 * 128:(j0 + 1) * 128], wgu[:, DFF + j0 * 128:DFF + (j0 + 1) * 128],
                               wgu[:, j1 * 128:(j1 + 1) * 128], wgu[:, DFF + j1 * 128:DFF + (j1 + 1) * 128]], axis=1)
        out.append(("gu", _piece(cols)))
    wd = inp["w_down"][l]
    for m in range(8):
        out.append(("dn", _piece(wd[:, m * 128:(m + 1) * 128])))
    return out


def _pack_weights(inp):
    arrs = []
    table = {}
    groups = {}
    off = 0

    def put(key, grp, a):
        nonlocal off
        table[key] = (off, a.shape[1], grp)
        groups[grp] = groups.get(grp, 0) + 1
        arrs.append(a)
        off += a.shape[1]

    for l in range(4):
        put(("mem", l), ("mem",), _piece(inp["w_mem_kv"][l]))
    for l in range(4):
        for i, (fam, a) in enumerate(_layer_pieces(l, inp)):
            put((l, i), (l, fam), a)
    return np.concatenate(arrs, axis=1), table, groups, off


def _weight_table():
    table = {}
    groups = {}
    off = 0

    def put(key, grp, n):
        nonlocal off
        table[key] = (off, n, grp)
        groups[grp] = groups.get(grp, 0) + 1
        off += n

    for l in range(4):
        put(("mem", l), ("mem",), 4096)
    for l in range(4):
        sizes = [("in", 4096), ("in", 4096), ("in", 2048 if l % 2 == 0 else 4096), ("in", 1024 if l % 2 == 0 else 2048),
                 ("out", 4096), ("out", 4096)] + [("gu", 4096)] * 11 + [("dn", 2816)] * 8
        for i, (fam, n) in enumerate(sizes):
            put((l, i), (l, fam), n)
    return table, groups, off


def _pack_params(inp):
    gs = [inp["mem_norm_g"]] + [inp["mix_norm_g"][i] for i in range(4)] + \
         [inp["ffn_norm_g"][i] for i in range(4)] + [inp["final_norm_g"]]
    G = np.stack([g.reshape(KC, 128).T for g in gs], axis=1).reshape(128, 80)
    sk = np.full((2, 16), NEG, np.float32)
    sk[:, 0:12] = inp["a_sinks"]
    sk = np.broadcast_to(sk.reshape(1, 32), (128, 32))
    lg = inp["b_ln_g"].reshape(12, 128).T
    return np.ascontiguousarray(np.concatenate([G, sk, lg], axis=1).astype(np.float32))


def _pack_bt(inp):
    out = np.zeros((12, 128, 384), np.float32)
    for l in range(2):
        for g in range(6):
            i = l * 6 + g
            out[i, :, 0:128] = inp["b_w_s"][l, g].T
            out[i, :, 128:256] = np.broadcast_to(inp["b_ln_b"][l, g][None, :], (128, 128))
            out[i, :, 256:384] = np.broadcast_to(inp["b_bias_s"][l, g][None, :], (128, 128))
    return out


def _consts():
    c = np.zeros((128, 896), np.float32)
    c[:, 0:128] = np.eye(128, dtype=np.float32)
    qi = np.arange(128)[:, None]
    kj = np.arange(128)[None, :]
    prev = np.where(kj > qi, 0.0, NEG)
    cur = np.where(kj <= qi, 0.0, NEG)
    c[:, 128:256] = prev
    c[:, 256:384] = cur
    c[:, 384:512] = NEG
    c[:, 512:640] = cur
    c[:, 640:768] = (qi <= kj).astype(np.float32)
    c[:, 768:896] = 1.0
    return c


class _Op:
    __slots__ = ("eng", "fn", "deps", "signal", "done_sem", "done_val", "dma_key", "idx")


class Sched:
    ENGS = ("pe", "act", "dve", "pool", "sp")

    def __init__(self):
        self.ops = {e: [] for e in self.ENGS}
        self.n = 0
        self.lastw = {}
        self.rd = {}
        self.dma_cnt = {}
        self.dma_keys = []

    def add(self, eng, fn, reads=(), writes=(), dma_key=None, done_val=None):
        op = _Op()
        op.eng = eng
        op.fn = fn
        op.signal = False
        op.dma_key = dma_key
        op.idx = self.n
        op.done_sem = None
        op.done_val = None
        self.n += 1
        cand = []
        for k in reads:
            w = self.lastw.get(k)
            if w is not None:
                cand.append(w)
        for k in writes:
            w = self.lastw.get(k)
            if w is not None:
                cand.append(w)
            r = self.rd.get(k)
            if r:
                for v in r.values():
                    if isinstance(v, list):
                        cand.extend(v)
                    else:
                        cand.append(v)
        best = {}
        dmas = {}
        for d in cand:
            if d is op:
                continue
            if d.dma_key is not None:
                o = dmas.get(d.dma_key)
                if o is None or d.done_val > o.done_val:
                    dmas[d.dma_key] = d
            else:
                if d.eng == "pe" and eng == "pe" and dma_key is None:
                    continue
                o = best.get(d.eng)
                if o is None or d.idx > o.idx:
                    best[d.eng] = d
        op.deps = list(best.values()) + list(dmas.values())
        for d in best.values():
            d.signal = True
        for k in reads:
            r = self.rd.setdefault(k, {})
            if dma_key is not None:
                r.setdefault("dma", []).append(op)
            else:
                r[eng] = op
        for k in writes:
            self.lastw[k] = op
            self.rd[k] = {}
        if dma_key is not None:
            if dma_key not in self.dma_cnt:
                self.dma_cnt[dma_key] = 0
                self.dma_keys.append(dma_key)
            self.dma_cnt[dma_key] += 16
            op.done_sem = ("dma", dma_key)
            op.done_val = done_val if done_val is not None else self.dma_cnt[dma_key]
        self.ops[eng].append(op)
        return op

    def emit(self, nc, stack):
        engs = {"pe": nc.tensor, "act": nc.scalar, "dve": nc.vector, "pool": nc.gpsimd, "sp": nc.sync}
        sems = {}
        for e in self.ENGS:
            sems[e] = stack.enter_context(nc.semaphore("prog_" + e))
        for i, k in enumerate(self.dma_keys):
            sems[("dma", k)] = stack.enter_context(nc.semaphore("dma_%d" % i))
        for e in self.ENGS:
            c = 0
            for op in self.ops[e]:
                if op.dma_key is None and op.signal:
                    c += 1
                    op.done_sem = e
                    op.done_val = c
        with nc.Block() as blk:
            @blk.sync
            def _(sync):
                for s in sems.values():
                    sync.sem_clear(s)
        final = {("dma", k): v for k, v in self.dma_cnt.items() if k[0] == "store"}

        def body(ename):
            def run(e):
                waited = {}
                for op in self.ops[ename]:
                    need = {}
                    for d in op.deps:
                        v = need.get(d.done_sem, 0)
                        if d.done_val > v:
                            need[d.done_sem] = d.done_val
                    for s, v in need.items():
                        if waited.get(s, 0) < v:
                            e.wait_ge(sems[s], v)
                            waited[s] = v
                    ins = op.fn(e)
                    if op.dma_key is not None:
                        ins.then_inc(sems[op.done_sem], 16)
                    elif op.signal:
                        ins.then_inc(sems[ename], 1)
                if ename == "act":
                    for s, v in final.items():
                        e.wait_ge(sems[s], v)
            return run

        with nc.Block() as blk:
            blk.sync(body("sp"))
            blk.scalar(body("act"))
            blk.vector(body("dve"))
            blk.gpsimd(body("pool"))
            blk.tensor(body("pe"))


def build_program(NT=16, layers=(0, 1, 2, 3), mixers=True):
    nc = bass.Bass("TRN2", target_bir_lowering=False)
    wtab, wgroups, WTOT = _weight_table()
    ntok = NT * T
    xT = nc.dram_tensor("xT", [D, ntok], F32, kind="ExternalInput").ap()
    memT = nc.dram_tensor("memT", [D, NMEM], F32, kind="ExternalInput").ap()
    wf = nc.dram_tensor("wf", [128, WTOT], F32, kind="ExternalInput").ap()
    prm = nc.dram_tensor("prm", [128, 124], F32, kind="ExternalInput").ap()
    cst = nc.dram_tensor("cst", [128, 896], F32, kind="ExternalInput").ap()
    btin = nc.dram_tensor("btin", [12, 128, 384], F32, kind="ExternalInput").ap()
    y = nc.dram_tensor("y", [D, ntok], F32, kind="ExternalOutput").ap()
    wb = nc.dram_tensor("wb", [128, WTOT], BF16, kind="Internal").ap()

    S = Sched()
    st = ExitStack()
    with st:
        def sb(name, shape, dt):
            return st.enter_context(nc.sbuf_tensor(name, shape, dt))

        hbuf = [sb("h0", [128, KC, T], F32), sb("h1", [128, KC, T], F32)]
        sqb = sb("sqb", [128, 2, T], BF16)
        rs_s = sb("rs_s", [128, T], F32)
        rstd = sb("rstd", [128, T], F32)
        rstd2 = sb("rstd2", [128, T], F32)
        xn = sb("xn", [128, KC, T], BF16)
        qT = sb("qT", [128, 6, T], BF16)
        uT = qT
        qmT = sb("qmT", [128, 2, T], BF16)
        kpad = [sb("kpad%d" % i, [128, 4, 640], BF16) for i in range(2)]
        vpad = [sb("vpad%d" % i, [128, 5, 4, 128], BF16) for i in range(2)]
        kmp = sb("kmp", [128, 4, 4, 256], BF16)
        vmp = sb("vmp", [128, 4, 2, 4, 128], BF16)
        Pb = [sb("Pb%d" % i, [128, 4, 256], BF16) for i in range(NPAR)]
        PTs = [sb("PTs%d" % i, [128, 4, 2, 128], BF16) for i in range(NPAR)]
        dgb = [sb("dg%d" % i, [128, 4, 128], BF16) for i in range(NPAR)]
        stt_ = [sb("stt%d" % i, [128, 8, 4], F32) for i in range(NPAR)]
        actb = sb("actb", [128, NJ, T], BF16)
        sgb = sb("sgb", [128, 2, T], F32)
        vgb = sb("vgb", [128, 3, 768], F32)
        nbuf = sb("nbuf", [128, NBLK, 768], BF16)
        bnst = sb("bnst", [128, 3, 6, 6], F32)
        mvb = sb("mvb", [128, 3, 6, 2], F32)
        lnr = sb("lnr", [128, 3, 2, 6], F32)
        tmpf = sb("tmpf", [128, 2, T], F32)
        ob_ap = [tmpf[:, 0, :], tmpf[:, 1, :]]
        ob_key = [("tmp", 0), ("tmp", 1)]
        wring = sb("wring", [128, RS, SLOT], BF16)
        cstf = sb("cstf", [128, 896], F32)
        identb = sb("identb", [128, 128], BF16)
        onesb = sb("onesb", [128, 128], BF16)
        maskb = sb("maskb", [128, 2, 256], BF16)
        WsTb = sb("WsTb", [128, 12, 128], BF16)
        BT = sb("BT", [128, 12, 128], F32)
        prmb = sb("prmb", [128, 124], F32)
        nsink = sb("nsink", [128, 32], F32)
        ps = st.enter_context(nc.psum_tensor("ps", [128, 8, 512], F32))

        Gc = lambda col: prmb[:, col:col + 1]
        sink_ap = lambda c0: prmb[:, 80 + c0:80 + c0 + 4]
        nsink_ap = lambda c0: nsink[:, c0:c0 + 4]
        lg_ap = lambda i: prmb[:, 112 + i:112 + i + 1]

        bank_ptr = [0]

        def nb():
            b = bank_ptr[0]
            bank_ptr[0] = (b + 1) % 8
            return b

        def nb2():
            if bank_ptr[0] % 2:
                bank_ptr[0] = (bank_ptr[0] + 1) % 8
            b = bank_ptr[0]
            bank_ptr[0] = (b + 2) % 8
            return b

        def mm(out, lhsT, rhs, start, stop, reads, writes):
            S.add("pe", lambda e, o=out, l=lhsT, r=rhs, a=start, b=stop: e.matmul(o, l, r, start=a, stop=b),
                  reads=reads, writes=writes)

        def act(out, in_, func, reads, writes, scale=1.0, bias=None, accum_out=None):
            def f(e, o=out, i=in_, fn=func, sc=scale, bi=bias, ac=accum_out):
                kw = {}
                if bi is not None:
                    kw["bias"] = bi
                if ac is not None:
                    kw["accum_out"] = ac
                return e.activation(out=o, in_=i, func=fn, scale=sc, **kw)
            S.add("act", f, reads=reads, writes=writes)

        def tt(eng, out, in0, in1, op, reads, writes):
            S.add(eng, lambda e, o=out, a=in0, b=in1, p=op: e.tensor_tensor(out=o, in0=a, in1=b, op=p),
                  reads=reads, writes=writes)

        def ts(eng, out, in0, s1, s2, op0, op1, reads, writes):
            def f(e, o=out, a=in0, x=s1, y_=s2, p0=op0, p1=op1):
                if p1 is None:
                    return e.tensor_scalar(out=o, in0=a, scalar1=x, scalar2=None, op0=p0)
                return e.tensor_scalar(out=o, in0=a, scalar1=x, scalar2=y_, op0=p0, op1=p1)
            S.add(eng, f, reads=reads, writes=writes)

        def stt(out, in0, scalar, in1, op0, op1, reads, writes):
            S.add("dve", lambda e, o=out, a=in0, s=scalar, b=in1, p0=op0, p1=op1:
                  e.scalar_tensor_tensor(out=o, in0=a, scalar=s, in1=b, op0=p0, op1=p1),
                  reads=reads, writes=writes)

        def cp(eng, out, in_, reads, writes):
            if eng == "act":
                S.add("act", lambda e, o=out, i=in_: e.copy(out=o, in_=i), reads=reads, writes=writes)
            else:
                S.add(eng, lambda e, o=out, i=in_: e.tensor_copy(out=o, in_=i), reads=reads, writes=writes)

        def recip(out, in_, reads, writes):
            S.add("dve", lambda e, o=out, i=in_: e.reciprocal(out=o, in_=i), reads=reads, writes=writes)

        def recip_fast(out, in_, reads, writes, Tn=T):
            S.add("dve", lambda e, o=out, i=in_, sc=tmpf[:, 0, 0:Tn]: e.reciprocal_approx_accurate(out=o, in_=i, scratch=sc),
                  reads=reads + [("tmp", 0)], writes=writes + [("tmp", 0)])

        def memset(eng, ap, val, writes):
            S.add(eng, lambda e, a=ap, v=val: e.memset(a, v), writes=writes)

        def dma(eng, out, in_, key, reads, writes, done_val=None, **kw):
            S.add(eng, lambda e, o=out, i=in_, k=kw: e.dma_start(out=o, in_=i, **k),
                  reads=reads, writes=writes, dma_key=key, done_val=done_val)

        piece_ctr = [0]

        def load_piece(key):
            off, n, grp = wtab[key]
            slot = piece_ctr[0] % RS
            piece_ctr[0] += 1
            dma("sp", wring[:, slot, 0:n], wb[:, off:off + n], ("w", slot),
                reads=[("wb", key)], writes=[("w", slot)])
            return slot, n

        def wview(slot, n, kc):
            return wring[:, slot, 0:n].rearrange("p (k f) -> p k f", k=kc)

        dma("sp", cstf[:, :], cst[:, :], ("c", 0), reads=[], writes=[("cstf",)])
        dma("sp", prmb[:, :], prm[:, :], ("c", 1), reads=[], writes=[("prm",)])
        cp("dve", identb[:, :], cstf[:, 0:128], [("cstf",)], [("identb",)])
        cp("dve", maskb[:, :, :], cstf[:, 128:640].rearrange("p (a b) -> p a b", a=2), [("cstf",)], [("maskb",)])
        cp("dve", onesb[:, :], cstf[:, 768:896], [("cstf",)], [("onesb",)])
        ts("dve", nsink[:, :], prmb[:, 80:112], -1.0, None, ALU.mult, None, [("prm",)], [("nsink",)])
        for i in range(2):
            memset("pool", kpad[i][:, :, :], 0.0, [("kp", i, v) for v in range(4)])
            memset("pool", vpad[i][:, :, :, :], 0.0, [("vp", i, b) for b in range(5)])
        memset("pool", kmp[:, :, :, :], 0.0, [("kmp", l) for l in range(4)])
        memset("pool", vmp[:, :, :, :, :], 0.0, [("vmp", l) for l in range(4)])

        order = [("mem", l) for l in range(4)]
        for l in range(4):
            order += [(l, i) for i in range(25)]
        for key in order:
            off, n, grp = wtab[key]
            dma("pool", wb[:, off:off + n], wf[:, off:off + n], ("cast",) + grp, reads=[],
                writes=[("wb", key)], done_val=16 * wgroups[grp], max_dma_last_dim=8192)

        sq_ctr = [0]

        def rmsnorm(hb, gcol0, Tn, dst_fn, dst_keys_fn, final=False):
            bank = nb()
            for kc in range(KC):
                i = sq_ctr[0] % 2
                sq_ctr[0] += 1
                act(sqb[:, i, 0:Tn], hbuf[hb][:, kc, 0:Tn], AF.Square, [("h", hb, kc)], [("sq", i)])
                mm(ps[:, bank, 0:Tn], onesb[:, :], sqb[:, i, 0:Tn], kc == 0, kc == KC - 1,
                   [("sq", i), ("onesb",)], [("ps", bank)])
            act(rs_s[:, 0:Tn], ps[:, bank, 0:Tn], AF.Ln, [("ps", bank), ("epsb",)], [("rs",)],
                scale=1.0 / D, bias=epsb[:, 0:1])
            act(rstd[:, 0:Tn], rs_s[:, 0:Tn], AF.Exp, [("rs",)], [("rstd",)], scale=-0.5)
            for kc in range(KC):
                stt(dst_fn(kc), hbuf[hb][:, kc, 0:Tn], Gc(gcol0 + kc), rstd[:, 0:Tn], ALU.mult, ALU.mult,
                    [("h", hb, kc), ("prm",), ("rstd",)], dst_keys_fn(kc))

        epsb = sb("epsb", [128, 1], F32)
        memset("dve", epsb[:, :], EPS, [("epsb",)])

        def proj_chunk(w3, col0, nk, rhs_fn, rhs_keys_fn, wkey, Tn=T):
            bank = nb()
            for k in range(nk):
                mm(ps[:, bank, 0:Tn], w3[:, k, col0:col0 + 128], rhs_fn(k), k == 0, k == nk - 1,
                   [wkey] + rhs_keys_fn(k), [("ps", bank)])
            return bank

        def proj_chunks_kouter(w3, col0s, nk, rhs_fn, rhs_keys_fn, wkey, Tn=T):
            banks = [nb() for _ in col0s]
            for k in range(nk):
                for b, c0 in zip(banks, col0s):
                    mm(ps[:, b, 0:Tn], w3[:, k, c0:c0 + 128], rhs_fn(k), k == 0, k == nk - 1,
                       [wkey] + rhs_keys_fn(k), [("ps", b)])
            return banks

        ev_ctr = [0]

        def evac_scaled(out, bank, scale, writes, Tn=T):
            ev_ctr[0] += 1
            if ev_ctr[0] % 2:
                act(out, ps[:, bank, 0:Tn], AF.Copy, [("ps", bank)], writes, scale=scale)
            else:
                ts("dve", out, ps[:, bank, 0:Tn], scale, None, ALU.mult, None, [("ps", bank)], writes)

        xnk = lambda k: [("xn", k)]

        dma("sp", hbuf[1][:, :, 0:NMEM], memT.rearrange("(k p) t -> p k t", p=128), ("x", 1),
            reads=[], writes=[("h", 1, k) for k in range(KC)])
        rmsnorm(1, 0, NMEM, lambda kc: xn[:, kc, 0:NMEM], lambda kc: [("xn", kc)])
        for l in range(4):
            slot, n = load_piece(("mem", l))
            w3 = wview(slot, n, KC)
            for cm in range(2):
                bank = proj_chunk(w3, cm * 128, KC, lambda k: xn[:, k, 0:NMEM], xnk, ("w", slot), Tn=NMEM)
                cp("act", kmp[0:64, l, 2 * cm, :], ps[0:64, bank, 0:NMEM], [("ps", bank)], [("kmp", l)])
                cp("dve", kmp[64:128, l, 2 * cm + 1, :], ps[64:128, bank, 0:NMEM], [("ps", bank)], [("kmp", l)])
            for blk in range(2):
                bank = nb()
                for k in range(KC):
                    mm(ps[:, bank, 0:256], xn[:, k, blk * 128:(blk + 1) * 128], w3[:, k, 256:512], k == 0, k == KC - 1,
                       [("w", slot), ("xn", k)], [("ps", bank)])
                src = ps[:, bank, 0:256].rearrange("p (k o d) -> p k o d", k=2, o=2)
                dst = vmp[:, l, blk, :, :].rearrange("p (k o) d -> p k o d", o=2)
                cp("act", dst[:, :, 0, 0:64], src[:, :, 0, :], [("ps", bank)], [("vmp", l)])
                cp("dve", dst[:, :, 1, 64:128], src[:, :, 1, :], [("ps", bank)], [("vmp", l)])

        if mixers:
            for i in range(12):
                sgi = i % 2
                stg = sgb[:, sgi, 0:384]
                wm = tmpf[:, sgi, 0:128]
                dma("sp", stg, btin[i, :, :], ("bt", sgi), reads=[], writes=[("sg", sgi)])
                tt("dve", wm, stg[:, 0:128], cstf[:, 640:768], ALU.mult,
                   [("sg", sgi), ("cstf",)], [("tmp", sgi)])
                cp("dve", WsTb[:, i, :], wm, [("tmp", sgi)], [("WsTb", i)])
                bank = nb()
                mm(ps[:, bank, 0:128], stg[:, 128:256], wm, True, False,
                   [("sg", sgi), ("tmp", sgi)], [("ps", bank)])
                mm(ps[:, bank, 0:128], cstf[0:1, 768:896], stg[0:1, 256:384], False, True,
                   [("sg", sgi), ("cstf",)], [("ps", bank)])
                cp("act", BT[:, i, :], ps[:, bank, 0:128], [("ps", bank)], [("BT", i)])

        par = [0]

        def make_group(heads, sink_c0, out_ap, out_keys, same_kv=False):
            gidx = par[0]
            i = par[0] % NPAR
            par[0] += 1
            stv = stt_[i]
            nm, negm, rsum, stx, den, rr = (stv[:, j, :] for j in range(6))
            state = {}

            def stA():
                b0 = 2 * (gidx % 2)
                state["sc"] = b0
                for pr in range(2):
                    hd = heads[2 * pr]
                    bank = b0 + pr
                    o2 = ps[:, bank, :].rearrange("p (a k) -> p a k", a=2)
                    mm(o2, hd["q"], hd["k2"], True, hd["mask"] is None, hd["keys_qk"], [("ps", bank)])
                    if hd["mask"] is not None:
                        mm(o2, identb[:, :], hd["mask"].unsqueeze(1).broadcast_to([128, 2, 256]), False, True,
                           [("identb",), ("maskb",)], [("ps", bank)])

            def stB1():
                b0 = state["sc"]
                scv = ps[:, b0:b0 + 2, :].rearrange("p b (s k) -> p (b s) k", s=2)
                S.add("dve", lambda e: e.tensor_reduce(out=nm, in_=scv, axis=AX.X, op=ALU.max, negate=True),
                      reads=[("ps", b0), ("ps", b0 + 1)], writes=[("st", i, "nm")])
                tt("dve", negm, nm, nsink_ap(sink_c0), ALU.min, [("st", i, "nm"), ("nsink",)], [("st", i, "negm")])
                tt("dve", stx, negm, sink_ap(sink_c0), ALU.add, [("st", i, "negm"), ("prm",)], [("st", i, "stx")])

            def stB2():
                b0 = state["sc"]
                for s in range(4):
                    bank = b0 + s // 2
                    col = (s % 2) * 256
                    act(Pb[i][:, s, :], ps[:, bank, col:col + 256], AF.Exp, [("ps", bank), ("st", i, "negm")],
                        [("P", i), ("st", i, "rsum")], bias=negm[:, s:s + 1], accum_out=rsum[:, s:s + 1])
                act(stx, stx, AF.Exp, [("st", i, "stx")], [("st", i, "stx")])

            def stB3():
                tt("dve", den, rsum, stx, ALU.add, [("st", i, "rsum"), ("st", i, "stx")], [("st", i, "den")])
                recip(rr, den, [("st", i, "den")], [("st", i, "rr")])
                for s in range(4):
                    ts("pool", dgb[i][:, s, :], identb[:, :], rr[:, s:s + 1], 1.0, ALU.mult, ALU.mult,
                       [("identb",), ("st", i, "rr")], [("dg", i)])

            def stC():
                p0 = 4
                for s in range(4):
                    bank = p0 + s // 2
                    for kb in range(2):
                        col = (s % 2) * 256 + kb * 128
                        mm(ps[:, bank, col:col + 128], Pb[i][:, s, kb * 128:(kb + 1) * 128], dgb[i][:, s, :], True, True,
                           [("P", i), ("dg", i)], [("ps", bank)])
                cp("act",
                   PTs[i][:, :, :, :].rearrange("p (b s) k q -> p b (s k q)", b=2), ps[:, p0:p0 + 2, :],
                   [("ps", p0), ("ps", p0 + 1)], [("PT", i)])

            def stD():
                ob = 6 + gidx % 2
                if same_kv:
                    o2 = ps[:, ob, 0:256].rearrange("p (a q) -> p a q", a=2)
                    ptv = PTs[i][:, :, :, :].rearrange("p (a o) k q -> p o k a q", o=2)
                    cnt = 0
                    for o_ in range(2):
                        hd = heads[o_]
                        for kb in range(2):
                            mm(o2, hd["v%d" % kb], ptv[:, o_, kb, :, :], cnt == 0, cnt == 3,
                               [("PT", i)] + hd["keys_v"], [("ps", ob)])
                            cnt += 1
                    cp("dve", out_ap, o2, [("ps", ob)], out_keys)
                    return
                for pr in range(2):
                    cnt = 0
                    for s in (2 * pr, 2 * pr + 1):
                        hd = heads[s]
                        for kb in range(2):
                            mm(ps[:, ob, pr * 128:(pr + 1) * 128], hd["v%d" % kb], PTs[i][:, s, kb, :], cnt == 0, cnt == 3,
                               [("PT", i)] + hd["keys_v"], [("ps", ob)])
                            cnt += 1
                cp("dve", out_ap, ps[:, ob, 0:256].rearrange("p (a q) -> p a q", a=2), [("ps", ob)], out_keys)

            return [stA, stB1, stB2, stB3, stC, stD]

        SKEW = (0, 1, 1, 2, 3, 4)

        deferred = []

        def run_deferred():
            while deferred:
                deferred.pop(0)()

        def run_groups(groups, extra=()):
            n = len(groups)
            for step in range(max(n + SKEW[-1], len(extra))):
                for sidx in range(6):
                    g = step - SKEW[sidx]
                    if 0 <= g < n:
                        groups[g][sidx]()
                if step < len(extra):
                    extra[step]()

        def mem_heads(l, blk):
            hs = []
            for hm in range(4):
                hs.append(dict(q=qmT[:, hm // 2, blk * 128:(blk + 1) * 128], k2=kmp[:, l, 2 * (hm // 2):2 * (hm // 2) + 2, :], mask=None,
                               v0=vmp[:, l, 0, hm, :], v1=vmp[:, l, 1, hm, :],
                               keys_qk=[("qm", hm // 2), ("kmp", l)], keys_v=[("vmp", l)]))
            return hs

        def out_proj_and_ffn(l, hb):
            for pi in range(2):
                slot, n = load_piece((l, 4 + pi))
                w3 = wview(slot, n, KC)
                for m_ in range(4):
                    m = pi * 4 + m_
                    bank = proj_chunk(w3, m_ * 128, KC, lambda k: xn[:, k, :], xnk, ("w", slot))
                    tt("dve", hbuf[hb][:, m, :], ps[:, bank, :], hbuf[hb][:, m, :], ALU.add,
                       [("ps", bank), ("h", hb, m)], [("h", hb, m)])
            ffn(l, hb)

        def ffn(l, hb):
            rmsnorm(hb, 8 * (5 + l), T, lambda kc: xn[:, kc, :], lambda kc: [("xn", kc)])
            for pi in range(NJ // 2):
                slot, n = load_piece((l, 6 + pi))
                w3 = wview(slot, n, KC)
                if pi == 0:
                    b4 = proj_chunks_kouter(w3, [0, 128, 256, 384], KC, lambda k: xn[:, k, :], xnk, ("w", slot))
                for jj in range(2):
                    j = 2 * pi + jj
                    if j >= 2 and deferred:
                        deferred.pop(0)()
                    if pi == 0:
                        bg, bu = b4[2 * jj], b4[2 * jj + 1]
                    else:
                        bg = proj_chunk(w3, jj * 256, KC, lambda k: xn[:, k, :], xnk, ("w", slot))
                        bu = proj_chunk(w3, jj * 256 + 128, KC, lambda k: xn[:, k, :], xnk, ("w", slot))
                    si = j % 2
                    act(sgb[:, si, :], ps[:, bg, :], AF.Silu, [("ps", bg)], [("sg", si)])
                    tt("dve", actb[:, j, :], ps[:, bu, :], sgb[:, si, :], ALU.mult,
                       [("ps", bu), ("sg", si)], [("act", j)])
            for m in range(8):
                slot, n = load_piece((l, 17 + m))
                w3 = wview(slot, n, NJ)
                bank = proj_chunk(w3, 0, NJ, lambda k: actb[:, k, :], lambda k: [("act", k)], ("w", slot))
                tt("dve", hbuf[hb][:, m, :], ps[:, bank, :], hbuf[hb][:, m, :], ALU.add,
                   [("ps", bank), ("h", hb, m)], [("h", hb, m)])

        def layer_A(l, hb, t):
            la = l // 2
            rmsnorm(hb, 8 * (1 + l), T, lambda kc: xn[:, kc, :], lambda kc: [("xn", kc)])
            rhs = lambda k: xn[:, k, :]
            s0, n0 = load_piece((l, 0))
            w3 = wview(s0, n0, KC)
            banks = proj_chunks_kouter(w3, [0, 128, 256, 384], KC, rhs, xnk, ("w", s0))
            for c in range(4):
                evac_scaled(qT[:, c, :], banks[c], 0.125, [("q", c)])
            s1, n1 = load_piece((l, 1))
            w3 = wview(s1, n1, KC)
            for c in range(2):
                bank = proj_chunk(w3, c * 128, KC, rhs, xnk, ("w", s1))
                evac_scaled(qT[:, 4 + c, :], bank, 0.125, [("q", 4 + c)])
            for c in range(2):
                bank = proj_chunk(w3, 256 + c * 128, KC, rhs, xnk, ("w", s1))
                evac_scaled(qmT[:, c, :], bank, 0.125, [("qm", c)])
            s2, n2 = load_piece((l, 2))
            w3 = wview(s2, n2, KC)
            bka = proj_chunk(w3, 0, KC, rhs, xnk, ("w", s2))
            bkb = proj_chunk(w3, 128, KC, rhs, xnk, ("w", s2))
            cp("act", kpad[la][0:64, 0, 128:640], ps[0:64, bka, :], [("ps", bka)], [("kp", la, 0)])
            cp("dve", kpad[la][64:128, 3, 128:640], ps[64:128, bka, :], [("ps", bka)], [("kp", la, 3)])
            cp("act", kpad[la][0:64, 2, 128:640], ps[0:64, bkb, :], [("ps", bkb)], [("kp", la, 2)])
            cp("dve", kpad[la][64:128, 1, 128:640], ps[64:128, bkb, :], [("ps", bkb)], [("kp", la, 1)])
            s3, n3 = load_piece((l, 3))
            w3 = wview(s3, n3, KC)
            for blk in range(NBLK):
                bank = nb()
                for k in range(KC):
                    mm(ps[:, bank, 0:128], xn[:, k, blk * 128:(blk + 1) * 128], w3[:, k, 0:128], k == 0, k == KC - 1,
                       [("w", s3), ("xn", k)], [("ps", bank)])
                src = ps[:, bank, 0:128].rearrange("p (k d) -> p k d", k=2)
                dst = vpad[la][:, 1 + blk, :, :].rearrange("p (k o) d -> p k o d", o=2)
                cp("act", dst[:, :, 0, 0:64], src, [("ps", bank)], [("vp", la, 1 + blk)])
                cp("dve", dst[:, :, 1, 64:128], src, [("ps", bank)], [("vp", la, 1 + blk)])
            groups = []
            for blk in range(NBLK):
                first = (t == 0 and blk == 0)
                for gi in range(3):
                    hs = []
                    for s in range(4):
                        h = 4 * gi + s
                        var = (h // 6) * 2 + (h % 2)
                        hs.append(dict(q=qT[:, h // 2, blk * 128:(blk + 1) * 128],
                                       k2=kpad[la][:, 2 * (h // 6):2 * (h // 6) + 2, blk * 128:blk * 128 + 256],
                                       mask=maskb[:, 1 if first else 0, :],
                                       v0=vpad[la][:, blk, var, :], v1=vpad[la][:, blk + 1, var, :],
                                       keys_qk=[("q", h // 2), ("kp", la, 2 * (h // 6)), ("kp", la, 2 * (h // 6) + 1)],
                                       keys_v=[("vp", la, blk), ("vp", la, blk + 1)]))
                    groups.append(make_group(hs, la * 16 + gi * 4, xn[:, 2 * gi:2 * gi + 2, blk * 128:(blk + 1) * 128],
                                             [("xn", 2 * gi), ("xn", 2 * gi + 1)], same_kv=(gi != 1)))
                groups.append(make_group(mem_heads(l, blk), 12, xn[:, 6:8, blk * 128:(blk + 1) * 128],
                                         [("xn", 6), ("xn", 7)]))
            run_groups(groups)
            cp("pool", kpad[la][:, :, 0:128], kpad[la][:, :, 512:640], [("kp", la, v) for v in range(4)],
               [("kp", la, v) for v in range(4)])
            cp("pool", vpad[la][:, 0, :, :], vpad[la][:, 4, :, :], [("vp", la, 4)], [("vp", la, 0)])
            out_proj_and_ffn(l, hb)

        def layer_B(l, hb, t):
            lb = l // 2
            rmsnorm(hb, 8 * (1 + l), T, lambda kc: xn[:, kc, :], lambda kc: [("xn", kc)])
            rhs = lambda k: xn[:, k, :]
            s2, n2 = load_piece((l, 2))
            s3, n3 = load_piece((l, 3))
            s0, n0 = load_piece((l, 0))
            s1, n1 = load_piece((l, 1))
            w3a = wview(s2, n2, KC)
            w3b = wview(s3, n3, KC)
            w30 = wview(s0, n0, KC)
            w31 = wview(s1, n1, KC)
            chunks = [(w30, s0, c * 128, "u", c) for c in range(4)] + \
                     [(w31, s1, c * 128, "u", 4 + c) for c in range(2)] + \
                     [(w31, s1, 256 + c * 128, "qm", c) for c in range(2)]
            def ln_chain(blk, vi):
                for g in range(6):
                    S.add("dve", lambda e, o=bnst[:, vi, g, :], a=vgb[:, vi, g * 128:(g + 1) * 128]: e.bn_stats(out=o, in_=a),
                          reads=[("vg", vi, 0), ("vg", vi, 1)], writes=[("bn", vi, g)])
                    S.add("dve", lambda e, o=mvb[:, vi, g, :], a=bnst[:, vi, g, :]: e.bn_aggr(out=o, in_=a),
                          reads=[("bn", vi, g)], writes=[("mv", vi)])
                act(lnr[:, vi, 0, :], mvb[:, vi, :, 1], AF.Ln, [("mv", vi), ("epsb",)], [("lnr", vi, 0)],
                    bias=epsb[:, 0:1])
                act(lnr[:, vi, 1, :], lnr[:, vi, 0, :], AF.Exp, [("lnr", vi, 0)], [("lnr", vi, 1)], scale=-0.5)
                stt(lnr[:, vi, 0, :], mvb[:, vi, :, 0], -1.0, lnr[:, vi, 1, :], ALU.mult, ALU.mult,
                    [("mv", vi), ("lnr", vi, 1)], [("lnr", vi, 0)])
                for g in range(6):
                    ts("pool", nbuf[:, blk, g * 128:(g + 1) * 128], vgb[:, vi, g * 128:(g + 1) * 128],
                       lnr[:, vi, 1, g:g + 1], lnr[:, vi, 0, g:g + 1], ALU.mult, ALU.add,
                       [("vg", vi, 0), ("vg", vi, 1), ("lnr", vi, 0), ("lnr", vi, 1)], [("n", blk)])

            for blk in range(NBLK):
                ba = nb()
                bb = nb()
                if blk == 0:
                    ba1 = nb()
                    bb1 = nb()
                    for k in range(KC):
                        for (bq, wq, nn, bl, sk) in ((ba, w3a, 512, 0, s2), (bb, w3b, 256, 0, s3), (ba1, w3a, 512, 1, s2), (bb1, w3b, 256, 1, s3)):
                            mm(ps[:, bq, 0:nn], xn[:, k, bl * 128:(bl + 1) * 128], wq[:, k, :], k == 0, k == KC - 1,
                               [("w", sk), ("xn", k)], [("ps", bq)])
                elif blk == 1:
                    ba, bb = ba1, bb1
                else:
                    for k in range(KC):
                        mm(ps[:, ba, :], xn[:, k, blk * 128:(blk + 1) * 128], w3a[:, k, :], k == 0, k == KC - 1,
                           [("w", s2), ("xn", k)], [("ps", ba)])
                    for k in range(KC):
                        mm(ps[:, bb, 0:256], xn[:, k, blk * 128:(blk + 1) * 128], w3b[:, k, :], k == 0, k == KC - 1,
                           [("w", s3), ("xn", k)], [("ps", bb)])
                vi = blk % 3
                if blk == 3:
                    ln_chain(0, 0)
                act(vgb[:, vi, 0:512], ps[:, ba, :], AF.Gelu_apprx_tanh, [("ps", ba)], [("vg", vi, 0)])
                act(vgb[:, vi, 512:768], ps[:, bb, 0:256], AF.Gelu_apprx_tanh, [("ps", bb)], [("vg", vi, 1)])
                for (w3c, sc, col0, kind, ci) in chunks[2 * blk:2 * blk + 2]:
                    bank = proj_chunk(w3c, col0, KC, rhs, xnk, ("w", sc))
                    if kind == "u":
                        act(uT[:, ci, :], ps[:, bank, :], AF.Gelu_apprx_tanh, [("ps", bank)], [("q", ci)])
                    else:
                        evac_scaled(qmT[:, ci, :], bank, 0.125, [("qm", ci)])
            ln_chains = [lambda: ln_chain(1, 1), lambda: ln_chain(2, 2), lambda: ln_chain(3, 0)]
            groups = []
            for blk in range(NBLK):
                groups.append(make_group(mem_heads(l, blk), 12, xn[:, 6:8, blk * 128:(blk + 1) * 128],
                                         [("xn", 6), ("xn", 7)]))
            run_groups(groups, extra=ln_chains)
            for g in range(6):
                bank = nb()
                idx = lb * 6 + g
                for blk in range(NBLK):
                    mm(ps[:, bank, blk * 128:(blk + 1) * 128], nbuf[:, blk, g * 128:(g + 1) * 128], WsTb[:, idx, :], True, True,
                       [("n", blk), ("WsTb", idx)], [("ps", bank)])
                ti = g % 2
                stt(tmpf[:, ti, :].rearrange("p (b t) -> p b t", b=NBLK),
                    ps[:, bank, :].rearrange("p (b t) -> p b t", b=NBLK), lg_ap(idx),
                    BT[:, idx, :].unsqueeze(1).broadcast_to([128, NBLK, 128]), ALU.mult, ALU.add,
                    [("ps", bank), ("prm",), ("BT", idx)], [("tmp", ti)])
                tt("pool", xn[:, g, :], tmpf[:, ti, :], uT[:, g, :], ALU.mult, [("tmp", ti), ("q", g)], [("xn", g)])
            out_proj_and_ffn(l, hb)

        xTv = xT.rearrange("(k p) t -> p k t", p=128)
        dma("sp", hbuf[0][:, :, :], xTv[:, :, 0:T], ("x", 0), reads=[], writes=[("h", 0, k) for k in range(KC)])
        if NT > 1:
            dma("sp", hbuf[1][:, :, :], xTv[:, :, T:2 * T], ("x", 1), reads=[],
                writes=[("h", 1, k) for k in range(KC)])
        ob_ctr = [0]

        def final_norm_parts(t):
            hb = t % 2
            parts = []

            def stats():
                bank = nb()
                for kc in range(KC):
                    i = sq_ctr[0] % 2
                    sq_ctr[0] += 1
                    act(sqb[:, i, :], hbuf[hb][:, kc, :], AF.Square, [("h", hb, kc)], [("sq", i)])
                    mm(ps[:, bank, :], onesb[:, :], sqb[:, i, :], kc == 0, kc == KC - 1, [("sq", i), ("onesb",)], [("ps", bank)])
                act(rs_s[:, :], ps[:, bank, :], AF.Ln, [("ps", bank), ("epsb",)], [("rs",)], scale=1.0 / D, bias=epsb[:, 0:1])
                act(rstd2[:, :], rs_s[:, :], AF.Exp, [("rs",)], [("rstd2",)], scale=-0.5)

            parts.append(stats)

            def chunk(kc):
                oi = ob_ctr[0] % 2
                ob_ctr[0] += 1
                stt(ob_ap[oi], hbuf[hb][:, kc, :], Gc(72 + kc), rstd2[:, :], ALU.mult, ALU.mult,
                    [("h", hb, kc), ("prm",), ("rstd2",)], [ob_key[oi]])
                dma("act", y[kc * 128:(kc + 1) * 128, t * T:(t + 1) * T], ob_ap[oi], ("store", oi),
                    reads=[ob_key[oi]], writes=[])
                if kc == KC - 1 and t + 2 < NT:
                    dma("sp", hbuf[hb][:, :, :], xTv[:, :, (t + 2) * T:(t + 3) * T], ("x", hb), reads=[],
                        writes=[("h", hb, k) for k in range(KC)])

            for kc in range(KC):
                parts.append(lambda kc=kc: chunk(kc))
            return parts

        for t in range(NT):
            hb = t % 2
            for l in layers:
                if not mixers:
                    ffn(l, hb)
                elif l % 2 == 0:
                    layer_A(l, hb, t)
                else:
                    layer_B(l, hb, t)
            if t + 1 < NT and mixers:
                deferred.extend(final_norm_parts(t))
            else:
                for p in final_norm_parts(t):
                    p()

        S.emit(nc, st)
    return nc


_CACHE = {}


def kernel(**inputs):
    inp = {k: np.asarray(v) for k, v in inputs.items()}
    x = inp["x"]
    mem = inp["mem"]
    B = x.shape[0]
    wfull, _, _, _ = _pack_weights(inp)
    prm = _pack_params(inp)
    bt = _pack_bt(inp)
    cst = _consts()
    if "nc" not in _CACHE:
        _CACHE["nc"] = build_program()
    nc = _CACHE["nc"]
    in_maps = []
    for b in range(B):
        in_maps.append({
            "xT": np.ascontiguousarray(x[b].T),
            "memT": np.ascontiguousarray(mem[b].T),
            "wf": wfull, "prm": prm, "cst": cst, "btin": bt,
        })
    res = run_bass_kernel_spmd(nc, in_maps, core_ids=list(range(B)))
    out = np.stack([np.ascontiguousarray(res.results[b]["y"].T) for b in range(B)], axis=0)
    return out.astype(np.float32)
```

```python
import numpy as np
from contextlib import ExitStack
import concourse.bass as bass
import concourse.mybir as mybir
from concourse.bass_utils import run_bass_kernel_spmd

F32 = mybir.dt.float32
BF16 = mybir.dt.bfloat16
AF = mybir.ActivationFunctionType
ALU = mybir.AluOpType
AX = mybir.AxisListType

D = 1024
KC = 8
T = 512
NBLK = 4
SEQ = 8192
NMEM = 256
DFF = 2816
NJ = 22
EPS = 1e-6
NEG = -30000.0
RS = 5
NPAR = 3
SLOT = 4096


def _piece(w_cols):
    K, Fc = w_cols.shape
    kc = K // 128
    return np.ascontiguousarray(w_cols.reshape(kc, 128, Fc).transpose(1, 0, 2)).reshape(128, kc * Fc)


def _layer_pieces(l, inp):
    j = l // 2
    out = []
    if l % 2 == 0:
        w = inp["a_w_in"][j]
        q = w[:, 0:768]
        k = w[:, 768:896]
        v = w[:, 896:1024]
        qm = w[:, 1024:1280]
        kpad = np.concatenate([k, k[:, 64:128], k[:, 0:64]], axis=1)
        out.append(("in", _piece(q[:, 0:512])))
        out.append(("in", _piece(np.concatenate([q[:, 512:768], qm], axis=1))))
        out.append(("in", _piece(kpad)))
        out.append(("in", _piece(v)))
        wo = inp["a_w_out"][j]
    else:
        w = inp["b_w_in"][j]
        u = w[:, 0:768]
        v = w[:, 768:1536]
        qm = w[:, 1536:1792]
        out.append(("in", _piece(u[:, 0:512])))
        out.append(("in", _piece(np.concatenate([u[:, 512:768], qm], axis=1))))
        out.append(("in", _piece(v[:, 0:512])))
        out.append(("in", _piece(v[:, 512:768])))
        wo = inp["b_w_out"][j]
    out.append(("out", _piece(wo[:, 0:512])))
    out.append(("out", _piece(wo[:, 512:1024])))
    wgu = inp["w_gate_up"][l]
    for p in range(NJ // 2):
        j0, j1 = 2 * p, 2 * p + 1
        cols = np.concatenate([wgu[:, j0 * 128:(j0 + 1) * 128], wgu[:, DFF + j0 * 128:DFF + (j0 + 1) * 128],
                               wgu[:, j1 * 128:(j1 + 1) * 128], wgu[:, DFF + j1 * 128:DFF + (j1 + 1) * 128]], axis=1)
        out.append(("gu", _piece(cols)))
    wd = inp["w_down"][l]
    for m in range(8):
        out.append(("dn", _piece(wd[:, m * 128:(m + 1) * 128])))
    return out


def _pack_weights(inp):
    arrs = []
    table = {}
    groups = {}
    off = 0

    def put(key, grp, a):
        nonlocal off
        table[key] = (off, a.shape[1], grp)
        groups[grp] = groups.get(grp, 0) + 1
        arrs.append(a)
        off += a.shape[1]

    for l in range(4):
        put(("mem", l), ("mem",), _piece(inp["w_mem_kv"][l]))
    for l in range(4):
        for i, (fam, a) in enumerate(_layer_pieces(l, inp)):
            put((l, i), (l, fam), a)
    return np.concatenate(arrs, axis=1), table, groups, off


def _weight_table():
    table = {}
    groups = {}
    off = 0

    def put(key, grp, n):
        nonlocal off
        table[key] = (off, n, grp)
        groups[grp] = groups.get(grp, 0) + 1
        off += n

    for l in range(4):
        put(("mem", l), ("mem",), 4096)
    for l in range(4):
        sizes = [("in", 4096), ("in", 4096), ("in", 2048 if l % 2 == 0 else 4096), ("in", 1024 if l % 2 == 0 else 2048),
                 ("out", 4096), ("out", 4096)] + [("gu", 4096)] * 11 + [("dn", 2816)] * 8
        for i, (fam, n) in enumerate(sizes):
            put((l, i), (l, fam), n)
    return table, groups, off


def _pack_params(inp):
    gs = [inp["mem_norm_g"]] + [inp["mix_norm_g"][i] for i in range(4)] + \
         [inp["ffn_norm_g"][i] for i in range(4)] + [inp["final_norm_g"]]
    G = np.stack([g.reshape(KC, 128).T for g in gs], axis=1).reshape(128, 80)
    sk = np.full((2, 16), NEG, np.float32)
    sk[:, 0:12] = inp["a_sinks"]
    sk = np.broadcast_to(sk.reshape(1, 32), (128, 32))
    lg = inp["b_ln_g"].reshape(12, 128).T
    return np.ascontiguousarray(np.concatenate([G, sk, lg], axis=1).astype(np.float32))


def _pack_bt(inp):
    out = np.zeros((12, 128, 384), np.float32)
    for l in range(2):
        for g in range(6):
            i = l * 6 + g
            out[i, :, 0:128] = inp["b_w_s"][l, g].T
            out[i, :, 128:256] = np.broadcast_to(inp["b_ln_b"][l, g][None, :], (128, 128))
            out[i, :, 256:384] = np.broadcast_to(inp["b_bias_s"][l, g][None, :], (128, 128))
    return out


def _consts():
    c = np.zeros((128, 896), np.float32)
    c[:, 0:128] = np.eye(128, dtype=np.float32)
    qi = np.arange(128)[:, None]
    kj = np.arange(128)[None, :]
    prev = np.where(kj > qi, 0.0, NEG)
    cur = np.where(kj <= qi, 0.0, NEG)
    c[:, 128:256] = prev
    c[:, 256:384] = cur
    c[:, 384:512] = NEG
    c[:, 512:640] = cur
    c[:, 640:768] = (qi <= kj).astype(np.float32)
    c[:, 768:896] = 1.0
    return c


class _Op:
    __slots__ = ("eng", "fn", "deps", "signal", "done_sem", "done_val", "dma_key", "idx")


class Sched:
    ENGS = ("pe", "act", "dve", "pool", "sp")

    def __init__(self):
        self.ops = {e: [] for e in self.ENGS}
        self.n = 0
        self.lastw = {}
        self.rd = {}
        self.dma_cnt = {}
        self.dma_keys = []

    def add(self, eng, fn, reads=(), writes=(), dma_key=None, done_val=None):
        op = _Op()
        op.eng = eng
        op.fn = fn
        op.signal = False
        op.dma_key = dma_key
        op.idx = self.n
        op.done_sem = None
        op.done_val = None
        self.n += 1
        cand = []
        for k in reads:
            w = self.lastw.get(k)
            if w is not None:
                cand.append(w)
        for k in writes:
            w = self.lastw.get(k)
            if w is not None:
                cand.append(w)
            r = self.rd.get(k)
            if r:
                for v in r.values():
                    if isinstance(v, list):
                        cand.extend(v)
                    else:
                        cand.append(v)
        best = {}
        dmas = {}
        for d in cand:
            if d is op:
                continue
            if d.dma_key is not None:
                o = dmas.get(d.dma_key)
                if o is None or d.done_val > o.done_val:
                    dmas[d.dma_key] = d
            else:
                if d.eng == "pe" and eng == "pe" and dma_key is None:
                    continue
                o = best.get(d.eng)
                if o is None or d.idx > o.idx:
                    best[d.eng] = d
        op.deps = list(best.values()) + list(dmas.values())
        for d in best.values():
            d.signal = True
        for k in reads:
            r = self.rd.setdefault(k, {})
            if dma_key is not None:
                r.setdefault("dma", []).append(op)
            else:
                r[eng] = op
        for k in writes:
            self.lastw[k] = op
            self.rd[k] = {}
        if dma_key is not None:
            if dma_key not in self.dma_cnt:
                self.dma_cnt[dma_key] = 0
                self.dma_keys.append(dma_key)
            self.dma_cnt[dma_key] += 16
            op.done_sem = ("dma", dma_key)
            op.done_val = done_val if done_val is not None else self.dma_cnt[dma_key]
        self.ops[eng].append(op)
        return op

    def emit(self, nc, stack):
        engs = {"pe": nc.tensor, "act": nc.scalar, "dve": nc.vector, "pool": nc.gpsimd, "sp": nc.sync}
        sems = {}
        for e in self.ENGS:
            sems[e] = stack.enter_context(nc.semaphore("prog_" + e))
        for i, k in enumerate(self.dma_keys):
            sems[("dma", k)] = stack.enter_context(nc.semaphore("dma_%d" % i))
        for e in self.ENGS:
            c = 0
            for op in self.ops[e]:
                if op.dma_key is None and op.signal:
                    c += 1
                    op.done_sem = e
                    op.done_val = c
        with nc.Block() as blk:
            @blk.sync
            def _(sync):
                for s in sems.values():
                    sync.sem_clear(s)
        final = {("dma", k): v for k, v in self.dma_cnt.items() if k[0] == "store"}

        def body(ename):
            def run(e):
                waited = {}
                for op in self.ops[ename]:
                    need = {}
                    for d in op.deps:
                        v = need.get(d.done_sem, 0)
                        if d.done_val > v:
                            need[d.done_sem] = d.done_val
                    for s, v in need.items():
                        if waited.get(s, 0) < v:
                            e.wait_ge(sems[s], v)
                            waited[s] = v
                    ins = op.fn(e)
                    if op.dma_key is not None:
                        ins.then_inc(sems[op.done_sem], 16)
                    elif op.signal:
                        ins.then_inc(sems[ename], 1)
                if ename == "act":
                    for s, v in final.items():
                        e.wait_ge(sems[s], v)
            return run

        with nc.Block() as blk:
            blk.sync(body("sp"))
            blk.scalar(body("act"))
            blk.vector(body("dve"))
            blk.gpsimd(body("pool"))
            blk.tensor(body("pe"))


def build_program(NT=16, layers=(0, 1, 2, 3), mixers=True):
    nc = bass.Bass("TRN2", target_bir_lowering=False)
    wtab, wgroups, WTOT = _weight_table()
    ntok = NT * T
    xT = nc.dram_tensor("xT", [D, ntok], F32, kind="ExternalInput").ap()
    memT = nc.dram_tensor("memT", [D, NMEM], F32, kind="ExternalInput").ap()
    wf = nc.dram_tensor("wf", [128, WTOT], F32, kind="ExternalInput").ap()
    prm = nc.dram_tensor("prm", [128, 124], F32, kind="ExternalInput").ap()
    cst = nc.dram_tensor("cst", [128, 896], F32, kind="ExternalInput").ap()
    btin = nc.dram_tensor("btin", [12, 128, 384], F32, kind="ExternalInput").ap()
    y = nc.dram_tensor("y", [D, ntok], F32, kind="ExternalOutput").ap()
    wb = nc.dram_tensor("wb", [128, WTOT], BF16, kind="Internal").ap()

    S = Sched()
    st = ExitStack()
    with st:
        def sb(name, shape, dt):
            return st.enter_context(nc.sbuf_tensor(name, shape, dt))

        hbuf = [sb("h0", [128, KC, T], F32), sb("h1", [128, KC, T], F32)]
        sqb = sb("sqb", [128, 2, T], BF16)
        rs_s = sb("rs_s", [128, T], F32)
        rstd = sb("rstd", [128, T], F32)
        xn = sb("xn", [128, KC, T], BF16)
        qT = sb("qT", [128, 6, T], BF16)
        uT = qT
        qmT = sb("qmT", [128, 2, T], BF16)
        kpad = [sb("kpad%d" % i, [128, 4, 640], BF16) for i in range(2)]
        vpad = [sb("vpad%d" % i, [128, 5, 4, 128], BF16) for i in range(2)]
        kmp = sb("kmp", [128, 4, 4, 256], BF16)
        vmp = sb("vmp", [128, 4, 2, 4, 128], BF16)
        Pb = [sb("Pb%d" % i, [128, 4, 256], BF16) for i in range(NPAR)]
        PTs = [sb("PTs%d" % i, [128, 4, 2, 128], BF16) for i in range(NPAR)]
        dgb = [sb("dg%d" % i, [128, 4, 128], BF16) for i in range(NPAR)]
        stt_ = [sb("stt%d" % i, [128, 8, 4], F32) for i in range(NPAR)]
        actb = sb("actb", [128, NJ, T], BF16)
        sgb = sb("sgb", [128, 2, T], F32)
        vgb = sb("vgb", [128, 3, 768], F32)
        nbuf = sb("nbuf", [128, NBLK, 768], BF16)
        bnst = sb("bnst", [128, 3, 6, 6], F32)
        mvb = sb("mvb", [128, 3, 6, 2], F32)
        lnr = sb("lnr", [128, 3, 2, 6], F32)
        tmpf = sb("tmpf", [128, 2, T], F32)
        ob_ap = [sgb[:, 0, :], sgb[:, 1, :], tmpf[:, 0, :], tmpf[:, 1, :]]
        ob_key = [("sg", 0), ("sg", 1), ("tmp", 0), ("tmp", 1)]
        wring = sb("wring", [128, RS, SLOT], BF16)
        cstf = sb("cstf", [128, 896], F32)
        identb = sb("identb", [128, 128], BF16)
        onesb = sb("onesb", [128, 128], BF16)
        maskb = sb("maskb", [128, 2, 256], BF16)
        WsTb = sb("WsTb", [128, 12, 128], BF16)
        BT = sb("BT", [128, 12, 128], F32)
        prmb = sb("prmb", [128, 124], F32)
        nsink = sb("nsink", [128, 32], F32)
        ps = st.enter_context(nc.psum_tensor("ps", [128, 8, 512], F32))

        Gc = lambda col: prmb[:, col:col + 1]
        sink_ap = lambda c0: prmb[:, 80 + c0:80 + c0 + 4]
        nsink_ap = lambda c0: nsink[:, c0:c0 + 4]
        lg_ap = lambda i: prmb[:, 112 + i:112 + i + 1]

        bank_ptr = [0]

        nb_allowed = [None]

        def nb():
            while True:
                b = bank_ptr[0]
                bank_ptr[0] = (b + 1) % 8
                if nb_allowed[0] is None or b in nb_allowed[0]:
                    return b

        def nb2():
            if bank_ptr[0] % 2:
                bank_ptr[0] = (bank_ptr[0] + 1) % 8
            b = bank_ptr[0]
            bank_ptr[0] = (b + 2) % 8
            return b

        def mm(out, lhsT, rhs, start, stop, reads, writes):
            S.add("pe", lambda e, o=out, l=lhsT, r=rhs, a=start, b=stop: e.matmul(o, l, r, start=a, stop=b),
                  reads=reads, writes=writes)

        def act(out, in_, func, reads, writes, scale=1.0, bias=None, accum_out=None):
            def f(e, o=out, i=in_, fn=func, sc=scale, bi=bias, ac=accum_out):
                kw = {}
                if bi is not None:
                    kw["bias"] = bi
                if ac is not None:
                    kw["accum_out"] = ac
                return e.activation(out=o, in_=i, func=fn, scale=sc, **kw)
            S.add("act", f, reads=reads, writes=writes)

        def tt(eng, out, in0, in1, op, reads, writes):
            S.add(eng, lambda e, o=out, a=in0, b=in1, p=op: e.tensor_tensor(out=o, in0=a, in1=b, op=p),
                  reads=reads, writes=writes)

        def ts(eng, out, in0, s1, s2, op0, op1, reads, writes):
            def f(e, o=out, a=in0, x=s1, y_=s2, p0=op0, p1=op1):
                if p1 is None:
                    return e.tensor_scalar(out=o, in0=a, scalar1=x, scalar2=None, op0=p0)
                return e.tensor_scalar(out=o, in0=a, scalar1=x, scalar2=y_, op0=p0, op1=p1)
            S.add(eng, f, reads=reads, writes=writes)

        def stt(out, in0, scalar, in1, op0, op1, reads, writes):
            S.add("dve", lambda e, o=out, a=in0, s=scalar, b=in1, p0=op0, p1=op1:
                  e.scalar_tensor_tensor(out=o, in0=a, scalar=s, in1=b, op0=p0, op1=p1),
                  reads=reads, writes=writes)

        def cp(eng, out, in_, reads, writes):
            if eng == "act":
                S.add("act", lambda e, o=out, i=in_: e.copy(out=o, in_=i), reads=reads, writes=writes)
            else:
                S.add(eng, lambda e, o=out, i=in_: e.tensor_copy(out=o, in_=i), reads=reads, writes=writes)

        def recip(out, in_, reads, writes):
            S.add("dve", lambda e, o=out, i=in_: e.reciprocal(out=o, in_=i), reads=reads, writes=writes)

        def recip_fast(out, in_, reads, writes, Tn=T):
            S.add("dve", lambda e, o=out, i=in_, sc=tmpf[:, 0, 0:Tn]: e.reciprocal_approx_accurate(out=o, in_=i, scratch=sc),
                  reads=reads + [("tmp", 0)], writes=writes + [("tmp", 0)])

        def memset(eng, ap, val, writes):
            S.add(eng, lambda e, a=ap, v=val: e.memset(a, v), writes=writes)

        def dma(eng, out, in_, key, reads, writes, done_val=None, **kw):
            S.add(eng, lambda e, o=out, i=in_, k=kw: e.dma_start(out=o, in_=i, **k),
                  reads=reads, writes=writes, dma_key=key, done_val=done_val)

        piece_ctr = [0]

        def load_piece(key):
            off, n, grp = wtab[key]
            slot = piece_ctr[0] % RS
            piece_ctr[0] += 1
            dma("sp", wring[:, slot, 0:n], wb[:, off:off + n], ("w", slot),
                reads=[("wb", key)], writes=[("w", slot)])
            return slot, n

        def wview(slot, n, kc):
            return wring[:, slot, 0:n].rearrange("p (k f) -> p k f", k=kc)

        dma("sp", cstf[:, :], cst[:, :], ("c", 0), reads=[], writes=[("cstf",)])
        dma("sp", prmb[:, :], prm[:, :], ("c", 1), reads=[], writes=[("prm",)])
        cp("dve", identb[:, :], cstf[:, 0:128], [("cstf",)], [("identb",)])
        cp("dve", maskb[:, :, :], cstf[:, 128:640].rearrange("p (a b) -> p a b", a=2), [("cstf",)], [("maskb",)])
        cp("dve", onesb[:, :], cstf[:, 768:896], [("cstf",)], [("onesb",)])
        ts("dve", nsink[:, :], prmb[:, 80:112], -1.0, None, ALU.mult, None, [("prm",)], [("nsink",)])
        for i in range(2):
            memset("pool", kpad[i][:, :, :], 0.0, [("kp", i, v) for v in range(4)])
            memset("pool", vpad[i][:, :, :, :], 0.0, [("vp", i, b) for b in range(5)])
        memset("pool", kmp[:, :, :, :], 0.0, [("kmp", l) for l in range(4)])
        memset("pool", vmp[:, :, :, :, :], 0.0, [("vmp", l) for l in range(4)])

        order = [("mem", l) for l in range(4)]
        for l in range(4):
            order += [(l, i) for i in range(25)]
        for key in order:
            off, n, grp = wtab[key]
            dma("pool", wb[:, off:off + n], wf[:, off:off + n], ("cast",) + grp, reads=[],
                writes=[("wb", key)], done_val=16 * wgroups[grp], max_dma_last_dim=8192)

        sq_ctr = [0]

        def rmsnorm(hb, gcol0, Tn, dst_fn, dst_keys_fn, final=False):
            bank = nb()
            for kc in range(KC):
                i = sq_ctr[0] % 2
                sq_ctr[0] += 1
                act(sqb[:, i, 0:Tn], hbuf[hb][:, kc, 0:Tn], AF.Square, [("h", hb, kc)], [("sq", i)])
                mm(ps[:, bank, 0:Tn], onesb[:, :], sqb[:, i, 0:Tn], kc == 0, kc == KC - 1,
                   [("sq", i), ("onesb",)], [("ps", bank)])
            act(rs_s[:, 0:Tn], ps[:, bank, 0:Tn], AF.Ln, [("ps", bank), ("epsb",)], [("rs",)],
                scale=1.0 / D, bias=epsb[:, 0:1])
            act(rstd[:, 0:Tn], rs_s[:, 0:Tn], AF.Exp, [("rs",)], [("rstd",)], scale=-0.5)
            for kc in range(KC):
                stt(dst_fn(kc), hbuf[hb][:, kc, 0:Tn], Gc(gcol0 + kc), rstd[:, 0:Tn], ALU.mult, ALU.mult,
                    [("h", hb, kc), ("prm",), ("rstd",)], dst_keys_fn(kc))

        epsb = sb("epsb", [128, 1], F32)
        dmy = sb("dmy", [128, 1], F32)
        memset("dve", epsb[:, :], EPS, [("epsb",)])

        def proj_chunk(w3, col0, nk, rhs_fn, rhs_keys_fn, wkey, Tn=T):
            bank = nb()
            for k in range(nk):
                mm(ps[:, bank, 0:Tn], w3[:, k, col0:col0 + 128], rhs_fn(k), k == 0, k == nk - 1,
                   [wkey] + rhs_keys_fn(k), [("ps", bank)])
            return bank

        def proj_chunks_kouter(w3, col0s, nk, rhs_fn, rhs_keys_fn, wkey, Tn=T):
            banks = [nb() for _ in col0s]
            for k in range(nk):
                for b, c0 in zip(banks, col0s):
                    mm(ps[:, b, 0:Tn], w3[:, k, c0:c0 + 128], rhs_fn(k), k == 0, k == nk - 1,
                       [wkey] + rhs_keys_fn(k), [("ps", b)])
            return banks

        ev_ctr = [0]

        def evac_scaled(out, bank, scale, writes, Tn=T):
            ev_ctr[0] += 1
            if ev_ctr[0] % 2:
                act(out, ps[:, bank, 0:Tn], AF.Copy, [("ps", bank)], writes, scale=scale)
            else:
                ts("dve", out, ps[:, bank, 0:Tn], scale, None, ALU.mult, None, [("ps", bank)], writes)

        xnk = lambda k: [("xn", k)]

        dma("sp", hbuf[1][:, :, 0:NMEM], memT.rearrange("(k p) t -> p k t", p=128), ("x", 1),
            reads=[], writes=[("h", 1, k) for k in range(KC)])
        rmsnorm(1, 0, NMEM, lambda kc: xn[:, kc, 0:NMEM], lambda kc: [("xn", kc)])
        for l in range(4):
            slot, n = load_piece(("mem", l))
            w3 = wview(slot, n, KC)
            for cm in range(2):
                bank = proj_chunk(w3, cm * 128, KC, lambda k: xn[:, k, 0:NMEM], xnk, ("w", slot), Tn=NMEM)
                cp("act", kmp[0:64, l, 2 * cm, :], ps[0:64, bank, 0:NMEM], [("ps", bank)], [("kmp", l)])
                cp("dve", kmp[64:128, l, 2 * cm + 1, :], ps[64:128, bank, 0:NMEM], [("ps", bank)], [("kmp", l)])
            for blk in range(2):
                bank = nb()
                for k in range(KC):
                    mm(ps[:, bank, 0:256], xn[:, k, blk * 128:(blk + 1) * 128], w3[:, k, 256:512], k == 0, k == KC - 1,
                       [("w", slot), ("xn", k)], [("ps", bank)])
                src = ps[:, bank, 0:256].rearrange("p (k o d) -> p k o d", k=2, o=2)
                dst = vmp[:, l, blk, :, :].rearrange("p (k o) d -> p k o d", o=2)
                cp("act", dst[:, :, 0, 0:64], src[:, :, 0, :], [("ps", bank)], [("vmp", l)])
                cp("dve", dst[:, :, 1, 64:128], src[:, :, 1, :], [("ps", bank)], [("vmp", l)])

        if mixers:
            for i in range(12):
                sgi = i % 2
                stg = sgb[:, sgi, 0:384]
                wm = tmpf[:, sgi, 0:128]
                dma("sp", stg, btin[i, :, :], ("bt", sgi), reads=[], writes=[("sg", sgi)])
                tt("dve", wm, stg[:, 0:128], cstf[:, 640:768], ALU.mult,
                   [("sg", sgi), ("cstf",)], [("tmp", sgi)])
                cp("dve", WsTb[:, i, :], wm, [("tmp", sgi)], [("WsTb", i)])
                bank = nb()
                mm(ps[:, bank, 0:128], stg[:, 128:256], wm, True, False,
                   [("sg", sgi), ("tmp", sgi)], [("ps", bank)])
                mm(ps[:, bank, 0:128], cstf[0:1, 768:896], stg[0:1, 256:384], False, True,
                   [("sg", sgi), ("cstf",)], [("ps", bank)])
                cp("act", BT[:, i, :], ps[:, bank, 0:128], [("ps", bank)], [("BT", i)])

        par = [0]

        def make_group(heads, sink_c0, out_ap, out_keys, same_kv=False):
            gidx = par[0]
            i = par[0] % NPAR
            par[0] += 1
            stv = stt_[i]
            nm, negm, rsum, stx, den, rr = (stv[:, j, :] for j in range(6))
            state = {}

            def stA():
                b0 = 2 * (gidx % 2)
                state["sc"] = b0
                for pr in range(2):
                    hd = heads[2 * pr]
                    bank = b0 + pr
                    o2 = ps[:, bank, :].rearrange("p (a k) -> p a k", a=2)
                    mm(o2, hd["q"], hd["k2"], True, hd["mask"] is None, hd["keys_qk"], [("ps", bank)])
                    if hd["mask"] is not None:
                        mm(o2, identb[:, :], hd["mask"].unsqueeze(1).broadcast_to([128, 2, 256]), False, True,
                           [("identb",), ("maskb",)], [("ps", bank)])

            def stB1():
                b0 = state["sc"]
                scv = ps[:, b0:b0 + 2, :].rearrange("p b (s k) -> p (b s) k", s=2)
                S.add("dve", lambda e: e.tensor_reduce(out=nm, in_=scv, axis=AX.X, op=ALU.max, negate=True),
                      reads=[("ps", b0), ("ps", b0 + 1)], writes=[("st", i, "nm")])
                tt("dve", negm, nm, nsink_ap(sink_c0), ALU.min, [("st", i, "nm"), ("nsink",)], [("st", i, "negm")])
                tt("dve", stx, negm, sink_ap(sink_c0), ALU.add, [("st", i, "negm"), ("prm",)], [("st", i, "stx")])

            def stB2():
                b0 = state["sc"]
                for s in range(4):
                    bank = b0 + s // 2
                    col = (s % 2) * 256
                    act(Pb[i][:, s, :], ps[:, bank, col:col + 256], AF.Exp, [("ps", bank), ("st", i, "negm")],
                        [("P", i), ("st", i, "rsum")], bias=negm[:, s:s + 1], accum_out=rsum[:, s:s + 1])
                act(stx, stx, AF.Exp, [("st", i, "stx")], [("st", i, "stx")])

            def stB3():
                tt("dve", den, rsum, stx, ALU.add, [("st", i, "rsum"), ("st", i, "stx")], [("st", i, "den")])
                recip(rr, den, [("st", i, "den")], [("st", i, "rr")])
                for s in range(4):
                    ts("pool", dgb[i][:, s, :], identb[:, :], rr[:, s:s + 1], 1.0, ALU.mult, ALU.mult,
                       [("identb",), ("st", i, "rr")], [("dg", i)])

            def stC():
                p0 = 4
                for s in range(4):
                    bank = p0 + s // 2
                    for kb in range(2):
                        col = (s % 2) * 256 + kb * 128
                        mm(ps[:, bank, col:col + 128], Pb[i][:, s, kb * 128:(kb + 1) * 128], dgb[i][:, s, :], True, True,
                           [("P", i), ("dg", i)], [("ps", bank)])
                cp("act",
                   PTs[i][:, :, :, :].rearrange("p (b s) k q -> p b (s k q)", b=2), ps[:, p0:p0 + 2, :],
                   [("ps", p0), ("ps", p0 + 1)], [("PT", i)])

            def stD():
                ob = 6 + gidx % 2
                if same_kv:
                    o2 = ps[:, ob, 0:256].rearrange("p (a q) -> p a q", a=2)
                    ptv = PTs[i][:, :, :, :].rearrange("p (a o) k q -> p o k a q", o=2)
                    cnt = 0
                    for o_ in range(2):
                        hd = heads[o_]
                        for kb in range(2):
                            mm(o2, hd["v%d" % kb], ptv[:, o_, kb, :, :], cnt == 0, cnt == 3,
                               [("PT", i)] + hd["keys_v"], [("ps", ob)])
                            cnt += 1
                    cp("dve", out_ap, o2, [("ps", ob)], out_keys)
                    return
                for pr in range(2):
                    cnt = 0
                    for s in (2 * pr, 2 * pr + 1):
                        hd = heads[s]
                        for kb in range(2):
                            mm(ps[:, ob, pr * 128:(pr + 1) * 128], hd["v%d" % kb], PTs[i][:, s, kb, :], cnt == 0, cnt == 3,
                               [("PT", i)] + hd["keys_v"], [("ps", ob)])
                            cnt += 1
                cp("dve", out_ap, ps[:, ob, 0:256].rearrange("p (a q) -> p a q", a=2), [("ps", ob)], out_keys)

            return [stA, stB1, stB2, stB3, stC, stD]

        SKEW = (0, 1, 1, 2, 3, 4)

        deferred = []

        def run_deferred():
            while deferred:
                deferred.pop(0)()

        def run_groups(groups, extra=(), done=()):
            n = len(groups)
            for step in range(max(n + SKEW[-1], len(extra))):
                for sidx in range(6):
                    g = step - SKEW[sidx]
                    if 0 <= g < n and (g, sidx) not in done:
                        groups[g][sidx]()
                if step < len(extra):
                    extra[step]()

        def mem_heads(l, blk):
            hs = []
            for hm in range(4):
                hs.append(dict(q=qmT[:, hm // 2, blk * 128:(blk + 1) * 128], k2=kmp[:, l, 2 * (hm // 2):2 * (hm // 2) + 2, :], mask=None,
                               v0=vmp[:, l, 0, hm, :], v1=vmp[:, l, 1, hm, :],
                               keys_qk=[("qm", hm // 2), ("kmp", l)], keys_v=[("vmp", l)]))
            return hs

        def out_proj_and_ffn(l, hb):
            for pi in range(2):
                slot, n = load_piece((l, 4 + pi))
                w3 = wview(slot, n, KC)
                for m_ in range(4):
                    m = pi * 4 + m_
                    bank = proj_chunk(w3, m_ * 128, KC, lambda k: xn[:, k, :], xnk, ("w", slot))
                    tt("dve", hbuf[hb][:, m, :], ps[:, bank, :], hbuf[hb][:, m, :], ALU.add,
                       [("ps", bank), ("h", hb, m)], [("h", hb, m)])
            ffn(l, hb)

        def ffn(l, hb):
            rmsnorm(hb, 8 * (5 + l), T, lambda kc: xn[:, kc, :], lambda kc: [("xn", kc)])
            for pi in range(NJ // 2):
                slot, n = load_piece((l, 6 + pi))
                w3 = wview(slot, n, KC)
                if pi == 0:
                    b4 = proj_chunks_kouter(w3, [0, 128, 256, 384], KC, lambda k: xn[:, k, :], xnk, ("w", slot))
                for jj in range(2):
                    j = 2 * pi + jj
                    if pi == 0:
                        bg, bu = b4[2 * jj], b4[2 * jj + 1]
                    else:
                        bg = proj_chunk(w3, jj * 256, KC, lambda k: xn[:, k, :], xnk, ("w", slot))
                        bu = proj_chunk(w3, jj * 256 + 128, KC, lambda k: xn[:, k, :], xnk, ("w", slot))
                    si = j % 2
                    act(sgb[:, si, :], ps[:, bg, :], AF.Silu, [("ps", bg)], [("sg", si)])
                    tt("dve", actb[:, j, :], ps[:, bu, :], sgb[:, si, :], ALU.mult,
                       [("ps", bu), ("sg", si)], [("act", j)])
            act(dmy[:, 0:1], epsb[:, 0:1], AF.Exp, [("epsb",)], [("dmy",)])
            for m in range(8):
                slot, n = load_piece((l, 17 + m))
                w3 = wview(slot, n, NJ)
                bank = proj_chunk(w3, 0, NJ, lambda k: actb[:, k, :], lambda k: [("act", k)], ("w", slot))
                tt("dve", hbuf[hb][:, m, :], ps[:, bank, :], hbuf[hb][:, m, :], ALU.add,
                   [("ps", bank), ("h", hb, m)], [("h", hb, m)])

        def layer_A(l, hb, t):
            la = l // 2
            rmsnorm(hb, 8 * (1 + l), T, lambda kc: xn[:, kc, :], lambda kc: [("xn", kc)])
            rhs = lambda k: xn[:, k, :]
            s0, n0 = load_piece((l, 0))
            w3 = wview(s0, n0, KC)
            banks = proj_chunks_kouter(w3, [0, 128, 256, 384], KC, rhs, xnk, ("w", s0))
            for c in range(4):
                evac_scaled(qT[:, c, :], banks[c], 0.125, [("q", c)])
            s1, n1 = load_piece((l, 1))
            w3 = wview(s1, n1, KC)
            for c in range(2):
                bank = proj_chunk(w3, c * 128, KC, rhs, xnk, ("w", s1))
                evac_scaled(qT[:, 4 + c, :], bank, 0.125, [("q", 4 + c)])
            for c in range(2):
                bank = proj_chunk(w3, 256 + c * 128, KC, rhs, xnk, ("w", s1))
                evac_scaled(qmT[:, c, :], bank, 0.125, [("qm", c)])
            s2, n2 = load_piece((l, 2))
            w3 = wview(s2, n2, KC)
            bka = proj_chunk(w3, 0, KC, rhs, xnk, ("w", s2))
            bkb = proj_chunk(w3, 128, KC, rhs, xnk, ("w", s2))
            cp("act", kpad[la][0:64, 0, 128:640], ps[0:64, bka, :], [("ps", bka)], [("kp", la, 0)])
            cp("dve", kpad[la][64:128, 3, 128:640], ps[64:128, bka, :], [("ps", bka)], [("kp", la, 3)])
            cp("act", kpad[la][0:64, 2, 128:640], ps[0:64, bkb, :], [("ps", bkb)], [("kp", la, 2)])
            cp("dve", kpad[la][64:128, 1, 128:640], ps[64:128, bkb, :], [("ps", bkb)], [("kp", la, 1)])
            s3, n3 = load_piece((l, 3))
            w3 = wview(s3, n3, KC)
            for blk in range(NBLK):
                bank = nb()
                for k in range(KC):
                    mm(ps[:, bank, 0:128], xn[:, k, blk * 128:(blk + 1) * 128], w3[:, k, 0:128], k == 0, k == KC - 1,
                       [("w", s3), ("xn", k)], [("ps", bank)])
                src = ps[:, bank, 0:128].rearrange("p (k d) -> p k d", k=2)
                dst = vpad[la][:, 1 + blk, :, :].rearrange("p (k o) d -> p k o d", o=2)
                cp("act", dst[:, :, 0, 0:64], src, [("ps", bank)], [("vp", la, 1 + blk)])
                cp("dve", dst[:, :, 1, 64:128], src, [("ps", bank)], [("vp", la, 1 + blk)])
            groups = []
            for blk in range(NBLK):
                first = (t == 0 and blk == 0)
                for gi in range(3):
                    hs = []
                    for s in range(4):
                        h = 4 * gi + s
                        var = (h // 6) * 2 + (h % 2)
                        hs.append(dict(q=qT[:, h // 2, blk * 128:(blk + 1) * 128],
                                       k2=kpad[la][:, 2 * (h // 6):2 * (h // 6) + 2, blk * 128:blk * 128 + 256],
                                       mask=maskb[:, 1 if first else 0, :],
                                       v0=vpad[la][:, blk, var, :], v1=vpad[la][:, blk + 1, var, :],
                                       keys_qk=[("q", h // 2), ("kp", la, 2 * (h // 6)), ("kp", la, 2 * (h // 6) + 1)],
                                       keys_v=[("vp", la, blk), ("vp", la, blk + 1)]))
                    groups.append(make_group(hs, la * 16 + gi * 4, xn[:, 2 * gi:2 * gi + 2, blk * 128:(blk + 1) * 128],
                                             [("xn", 2 * gi), ("xn", 2 * gi + 1)], same_kv=(gi != 1)))
                groups.append(make_group(mem_heads(l, blk), 12, xn[:, 6:8, blk * 128:(blk + 1) * 128],
                                         [("xn", 6), ("xn", 7)]))
            run_deferred()
            run_groups(groups)
            cp("pool", kpad[la][:, :, 0:128], kpad[la][:, :, 512:640], [("kp", la, v) for v in range(4)],
               [("kp", la, v) for v in range(4)])
            cp("pool", vpad[la][:, 0, :, :], vpad[la][:, 4, :, :], [("vp", la, 4)], [("vp", la, 0)])
            out_proj_and_ffn(l, hb)

        def layer_B(l, hb, t):
            lb = l // 2
            rmsnorm(hb, 8 * (1 + l), T, lambda kc: xn[:, kc, :], lambda kc: [("xn", kc)])
            rhs = lambda k: xn[:, k, :]
            s2, n2 = load_piece((l, 2))
            s3, n3 = load_piece((l, 3))
            s0, n0 = load_piece((l, 0))
            s1, n1 = load_piece((l, 1))
            w3a = wview(s2, n2, KC)
            w3b = wview(s3, n3, KC)
            w30 = wview(s0, n0, KC)
            w31 = wview(s1, n1, KC)
            chunks = [(w30, s0, c * 128, "u", c) for c in range(4)] + \
                     [(w31, s1, c * 128, "u", 4 + c) for c in range(2)] + \
                     [(w31, s1, 256 + c * 128, "qm", c) for c in range(2)]
            chunks = chunks[6:8] + chunks[0:6]
            groups = []
            for blk_ in range(NBLK):
                groups.append(make_group(mem_heads(l, blk_), 12, xn[:, 6:8, blk_ * 128:(blk_ + 1) * 128],
                                         [("xn", 6), ("xn", 7)]))
            early = set()

            def ln_chain(blk, vi):
                for g in range(6):
                    S.add("dve", lambda e, o=bnst[:, vi, g, :], a=vgb[:, vi, g * 128:(g + 1) * 128]: e.bn_stats(out=o, in_=a),
                          reads=[("vg", vi, 0), ("vg", vi, 1)], writes=[("bn", vi, g)])
                    S.add("dve", lambda e, o=mvb[:, vi, g, :], a=bnst[:, vi, g, :]: e.bn_aggr(out=o, in_=a),
                          reads=[("bn", vi, g)], writes=[("mv", vi)])
                act(lnr[:, vi, 0, :], mvb[:, vi, :, 1], AF.Ln, [("mv", vi), ("epsb",)], [("lnr", vi, 0)],
                    bias=epsb[:, 0:1])
                act(lnr[:, vi, 1, :], lnr[:, vi, 0, :], AF.Exp, [("lnr", vi, 0)], [("lnr", vi, 1)], scale=-0.5)
                stt(lnr[:, vi, 0, :], mvb[:, vi, :, 0], -1.0, lnr[:, vi, 1, :], ALU.mult, ALU.mult,
                    [("mv", vi), ("lnr", vi, 1)], [("lnr", vi, 0)])
                for g in range(6):
                    ts("pool", nbuf[:, blk, g * 128:(g + 1) * 128], vgb[:, vi, g * 128:(g + 1) * 128],
                       lnr[:, vi, 1, g:g + 1], lnr[:, vi, 0, g:g + 1], ALU.mult, ALU.add,
                       [("vg", vi, 0), ("vg", vi, 1), ("lnr", vi, 0), ("lnr", vi, 1)], [("n", blk)])

            for blk in range(NBLK):
                ba = nb()
                bb = nb()
                if blk == 0:
                    ba1 = nb()
                    bb1 = nb()
                    for k in range(KC):
                        for (bq, wq, nn, bl, sk) in ((ba, w3a, 512, 0, s2), (bb, w3b, 256, 0, s3), (ba1, w3a, 512, 1, s2), (bb1, w3b, 256, 1, s3)):
                            mm(ps[:, bq, 0:nn], xn[:, k, bl * 128:(bl + 1) * 128], wq[:, k, :], k == 0, k == KC - 1,
                               [("w", sk), ("xn", k)], [("ps", bq)])
                elif blk == 1:
                    ba, bb = ba1, bb1
                else:
                    for k in range(KC):
                        mm(ps[:, ba, :], xn[:, k, blk * 128:(blk + 1) * 128], w3a[:, k, :], k == 0, k == KC - 1,
                           [("w", s2), ("xn", k)], [("ps", ba)])
                    for k in range(KC):
                        mm(ps[:, bb, 0:256], xn[:, k, blk * 128:(blk + 1) * 128], w3b[:, k, :], k == 0, k == KC - 1,
                           [("w", s3), ("xn", k)], [("ps", bb)])
                vi = blk % 3
                if blk == 3:
                    ln_chain(0, 0)
                act(vgb[:, vi, 0:512], ps[:, ba, :], AF.Gelu_apprx_tanh, [("ps", ba)], [("vg", vi, 0)])
                act(vgb[:, vi, 512:768], ps[:, bb, 0:256], AF.Gelu_apprx_tanh, [("ps", bb)], [("vg", vi, 1)])
                for (w3c, sc, col0, kind, ci) in chunks[2 * blk:2 * blk + 2]:
                    bank = proj_chunk(w3c, col0, KC, rhs, xnk, ("w", sc))
                    if kind == "u":
                        act(uT[:, ci, :], ps[:, bank, :], AF.Gelu_apprx_tanh, [("ps", bank)], [("q", ci)])
                    else:
                        evac_scaled(qmT[:, ci, :], bank, 0.125, [("qm", ci)])
                if blk == 1:
                    for sidx in (0, 1, 2, 3):
                        for g in (0, 1):
                            groups[g][sidx]()
                            early.add((g, sidx))
                    nb_allowed[0] = (4, 5, 6, 7)
            nb_allowed[0] = None
            ln_chains = [lambda: ln_chain(1, 1), lambda: ln_chain(2, 2), lambda: ln_chain(3, 0)]
            run_deferred()
            run_groups(groups, extra=ln_chains, done=early)
            for g in range(6):
                bank = nb()
                idx = lb * 6 + g
                for blk in range(NBLK):
                    mm(ps[:, bank, blk * 128:(blk + 1) * 128], nbuf[:, blk, g * 128:(g + 1) * 128], WsTb[:, idx, :], True, True,
                       [("n", blk), ("WsTb", idx)], [("ps", bank)])
                ti = g % 2
                stt(tmpf[:, ti, :].rearrange("p (b t) -> p b t", b=NBLK),
                    ps[:, bank, :].rearrange("p (b t) -> p b t", b=NBLK), lg_ap(idx),
                    BT[:, idx, :].unsqueeze(1).broadcast_to([128, NBLK, 128]), ALU.mult, ALU.add,
                    [("ps", bank), ("prm",), ("BT", idx)], [("tmp", ti)])
                tt("pool", xn[:, g, :], tmpf[:, ti, :], uT[:, g, :], ALU.mult, [("tmp", ti), ("q", g)], [("xn", g)])
            out_proj_and_ffn(l, hb)

        xTv = xT.rearrange("(k p) t -> p k t", p=128)
        dma("sp", hbuf[0][:, :, :], xTv[:, :, 0:T], ("x", 0), reads=[], writes=[("h", 0, k) for k in range(KC)])
        if NT > 1:
            dma("sp", hbuf[1][:, :, :], xTv[:, :, T:2 * T], ("x", 1), reads=[],
                writes=[("h", 1, k) for k in range(KC)])
        ob_ctr = [0]

        def final_norm(t):
            hb = t % 2
            bank = nb()
            for kc in range(KC):
                i = sq_ctr[0] % 2
                sq_ctr[0] += 1
                act(sqb[:, i, :], hbuf[hb][:, kc, :], AF.Square, [("h", hb, kc)], [("sq", i)])
                mm(ps[:, bank, :], onesb[:, :], sqb[:, i, :], kc == 0, kc == KC - 1, [("sq", i), ("onesb",)], [("ps", bank)])
            act(rs_s[:, :], ps[:, bank, :], AF.Ln, [("ps", bank), ("epsb",)], [("rs",)], scale=1.0 / D, bias=epsb[:, 0:1])
            act(rstd[:, :], rs_s[:, :], AF.Exp, [("rs",)], [("rstd",)], scale=-0.5)
            for kc in range(KC):
                oi = ob_ctr[0] % 4
                ob_ctr[0] += 1
                stt(ob_ap[oi], hbuf[hb][:, kc, :], Gc(72 + kc), rstd[:, :], ALU.mult, ALU.mult,
                    [("h", hb, kc), ("prm",), ("rstd",)], [ob_key[oi]])
                dma("sp", y[kc * 128:(kc + 1) * 128, t * T:(t + 1) * T], ob_ap[oi], ("store", oi),
                    reads=[ob_key[oi]], writes=[])
            if t + 2 < NT:
                dma("sp", hbuf[hb][:, :, :], xTv[:, :, (t + 2) * T:(t + 3) * T], ("x", hb), reads=[],
                    writes=[("h", hb, k) for k in range(KC)])

        for t in range(NT):
            hb = t % 2
            for l in layers:
                if not mixers:
                    ffn(l, hb)
                elif l % 2 == 0:
                    layer_A(l, hb, t)
                else:
                    layer_B(l, hb, t)
            if t + 1 < NT and mixers:
                deferred.append(lambda t=t: final_norm(t))
            else:
                final_norm(t)

        S.emit(nc, st)
    return nc


_CACHE = {}


def kernel(**inputs):
    inp = {k: np.asarray(v) for k, v in inputs.items()}
    x = inp["x"]
    mem = inp["mem"]
    B = x.shape[0]
    wfull, _, _, _ = _pack_weights(inp)
    prm = _pack_params(inp)
    bt = _pack_bt(inp)
    cst = _consts()
    if "nc" not in _CACHE:
        _CACHE["nc"] = build_program()
    nc = _CACHE["nc"]
    in_maps = []
    for b in range(B):
        in_maps.append({
            "xT": np.ascontiguousarray(x[b].T),
            "memT": np.ascontiguousarray(mem[b].T),
            "wf": wfull, "prm": prm, "cst": cst, "btin": bt,
        })
    res = run_bass_kernel_spmd(nc, in_maps, core_ids=list(range(B)))
    out = np.stack([np.ascontiguousarray(res.results[b]["y"].T) for b in range(B)], axis=0)
    return out.astype(np.float32)
```

```python
import numpy as np
from contextlib import ExitStack
import concourse.bass as bass
import concourse.mybir as mybir
from concourse.bass_utils import run_bass_kernel_spmd

F32 = mybir.dt.float32
BF16 = mybir.dt.bfloat16
AF = mybir.ActivationFunctionType
ALU = mybir.AluOpType
AX = mybir.AxisListType

D = 1024
KC = 8
T = 512
NBLK = 4
SEQ = 8192
NMEM = 256
DFF = 2816
NJ = 22
EPS = 1e-6
NEG = -30000.0
RS = 5
NPAR = 3
SLOT = 4096


def _piece(w_cols):
    K, Fc = w_cols.shape
    kc = K // 128
    return np.ascontiguousarray(w_cols.reshape(kc, 128, Fc).transpose(1, 0, 2)).reshape(128, kc * Fc)


def _layer_pieces(l, inp):
    j = l // 2
    out = []
    if l % 2 == 0:
        w = inp["a_w_in"][j]
        q = w[:, 0:768]
        k = w[:, 768:896]
        v = w[:, 896:1024]
        qm = w[:, 1024:1280]
        kpad = np.concatenate([k, k[:, 64:128], k[:, 0:64]], axis=1)
        out.append(("in", _piece(q[:, 0:512])))
        out.append(("in", _piece(np.concatenate([q[:, 512:768], qm], axis=1))))
        out.append(("in", _piece(kpad)))
        out.append(("in", _piece(v)))
        wo = inp["a_w_out"][j]
    else:
        w = inp["b_w_in"][j]
        u = w[:, 0:768]
        v = w[:, 768:1536]
        qm = w[:, 1536:1792]
        out.append(("in", _piece(u[:, 0:512])))
        out.append(("in", _piece(np.concatenate([u[:, 512:768], qm], axis=1))))
        out.append(("in", _piece(v[:, 0:512])))
        out.append(("in", _piece(v[:, 512:768])))
        wo = inp["b_w_out"][j]
    out.append(("out", _piece(wo[:, 0:512])))
    out.append(("out", _piece(wo[:, 512:1024])))
    wgu = inp["w_gate_up"][l]
    for p in range(NJ // 2):
        j0, j1 = 2 * p, 2 * p + 1
        cols = np.concatenate([wgu[:, j0 * 128:(j0 + 1) * 128], wgu[:, DFF + j0 * 128:DFF + (j0 + 1) * 128],
                               wgu[:, j1 * 128:(j1 + 1) * 128], wgu[:, DFF + j1 * 128:DFF + (j1 + 1) * 128]], axis=1)
        out.append(("gu", _piece(cols)))
    wd = inp["w_down"][l]
    for m in range(8):
        out.append(("dn", _piece(wd[:, m * 128:(m + 1) * 128])))
    return out


def _pack_weights(inp):
    arrs = []
    table = {}
    groups = {}
    off = 0

    def put(key, grp, a):
        nonlocal off
        table[key] = (off, a.shape[1], grp)
        groups[grp] = groups.get(grp, 0) + 1
        arrs.append(a)
        off += a.shape[1]

    for l in range(4):
        put(("mem", l), ("mem",), _piece(inp["w_mem_kv"][l]))
    for l in range(4):
        for i, (fam, a) in enumerate(_layer_pieces(l, inp)):
            put((l, i), (l, fam), a)
    return np.concatenate(arrs, axis=1), table, groups, off


def _weight_table():
    table = {}
    groups = {}
    off = 0

    def put(key, grp, n):
        nonlocal off
        table[key] = (off, n, grp)
        groups[grp] = groups.get(grp, 0) + 1
        off += n

    for l in range(4):
        put(("mem", l), ("mem",), 4096)
    for l in range(4):
        sizes = [("in", 4096), ("in", 4096), ("in", 2048 if l % 2 == 0 else 4096), ("in", 1024 if l % 2 == 0 else 2048),
                 ("out", 4096), ("out", 4096)] + [("gu", 4096)] * 11 + [("dn", 2816)] * 8
        for i, (fam, n) in enumerate(sizes):
            put((l, i), (l, fam), n)
    return table, groups, off


def _pack_params(inp):
    gs = [inp["mem_norm_g"]] + [inp["mix_norm_g"][i] for i in range(4)] + \
         [inp["ffn_norm_g"][i] for i in range(4)] + [inp["final_norm_g"]]
    G = np.stack([g.reshape(KC, 128).T for g in gs], axis=1).reshape(128, 80)
    sk = np.full((2, 16), NEG, np.float32)
    sk[:, 0:12] = inp["a_sinks"]
    sk = np.broadcast_to(sk.reshape(1, 32), (128, 32))
    lg = inp["b_ln_g"].reshape(12, 128).T
    return np.ascontiguousarray(np.concatenate([G, sk, lg], axis=1).astype(np.float32))


def _pack_bt(inp):
    out = np.zeros((12, 128, 384), np.float32)
    for l in range(2):
        for g in range(6):
            i = l * 6 + g
            out[i, :, 0:128] = inp["b_w_s"][l, g].T
            out[i, :, 128:256] = np.broadcast_to(inp["b_ln_b"][l, g][None, :], (128, 128))
            out[i, :, 256:384] = np.broadcast_to(inp["b_bias_s"][l, g][None, :], (128, 128))
    return out


def _consts():
    c = np.zeros((128, 896), np.float32)
    c[:, 0:128] = np.eye(128, dtype=np.float32)
    qi = np.arange(128)[:, None]
    kj = np.arange(128)[None, :]
    prev = np.where(kj > qi, 0.0, NEG)
    cur = np.where(kj <= qi, 0.0, NEG)
    c[:, 128:256] = prev
    c[:, 256:384] = cur
    c[:, 384:512] = NEG
    c[:, 512:640] = cur
    c[:, 640:768] = (qi <= kj).astype(np.float32)
    c[:, 768:896] = 1.0
    return c


class _Op:
    __slots__ = ("eng", "fn", "deps", "signal", "done_sem", "done_val", "dma_key", "idx")


class Sched:
    ENGS = ("pe", "act", "dve", "pool", "sp")

    def __init__(self):
        self.ops = {e: [] for e in self.ENGS}
        self.n = 0
        self.lastw = {}
        self.rd = {}
        self.dma_cnt = {}
        self.dma_keys = []

    def add(self, eng, fn, reads=(), writes=(), dma_key=None, done_val=None):
        op = _Op()
        op.eng = eng
        op.fn = fn
        op.signal = False
        op.dma_key = dma_key
        op.idx = self.n
        op.done_sem = None
        op.done_val = None
        self.n += 1
        cand = []
        for k in reads:
            w = self.lastw.get(k)
            if w is not None:
                cand.append(w)
        for k in writes:
            w = self.lastw.get(k)
            if w is not None:
                cand.append(w)
            r = self.rd.get(k)
            if r:
                for v in r.values():
                    if isinstance(v, list):
                        cand.extend(v)
                    else:
                        cand.append(v)
        best = {}
        dmas = {}
        for d in cand:
            if d is op:
                continue
            if d.dma_key is not None:
                o = dmas.get(d.dma_key)
                if o is None or d.done_val > o.done_val:
                    dmas[d.dma_key] = d
            else:
                if d.eng == "pe" and eng == "pe" and dma_key is None:
                    continue
                o = best.get(d.eng)
                if o is None or d.idx > o.idx:
                    best[d.eng] = d
        op.deps = list(best.values()) + list(dmas.values())
        for d in best.values():
            d.signal = True
        for k in reads:
            r = self.rd.setdefault(k, {})
            if dma_key is not None:
                r.setdefault("dma", []).append(op)
            else:
                r[eng] = op
        for k in writes:
            self.lastw[k] = op
            self.rd[k] = {}
        if dma_key is not None:
            if dma_key not in self.dma_cnt:
                self.dma_cnt[dma_key] = 0
                self.dma_keys.append(dma_key)
            self.dma_cnt[dma_key] += 16
            op.done_sem = ("dma", dma_key)
            op.done_val = done_val if done_val is not None else self.dma_cnt[dma_key]
        self.ops[eng].append(op)
        return op

    def emit(self, nc, stack):
        engs = {"pe": nc.tensor, "act": nc.scalar, "dve": nc.vector, "pool": nc.gpsimd, "sp": nc.sync}
        sems = {}
        for e in self.ENGS:
            sems[e] = stack.enter_context(nc.semaphore("prog_" + e))
        for i, k in enumerate(self.dma_keys):
            sems[("dma", k)] = stack.enter_context(nc.semaphore("dma_%d" % i))
        for e in self.ENGS:
            c = 0
            for op in self.ops[e]:
                if op.dma_key is None and op.signal:
                    c += 1
                    op.done_sem = e
                    op.done_val = c
        with nc.Block() as blk:
            @blk.sync
            def _(sync):
                for s in sems.values():
                    sync.sem_clear(s)
        final = {("dma", k): v for k, v in self.dma_cnt.items() if k[0] == "store"}

        def body(ename):
            def run(e):
                waited = {}
                for op in self.ops[ename]:
                    need = {}
                    for d in op.deps:
                        v = need.get(d.done_sem, 0)
                        if d.done_val > v:
                            need[d.done_sem] = d.done_val
                    for s, v in need.items():
                        if waited.get(s, 0) < v:
                            e.wait_ge(sems[s], v)
                            waited[s] = v
                    ins = op.fn(e)
                    if op.dma_key is not None:
                        ins.then_inc(sems[op.done_sem], 16)
                    elif op.signal:
                        ins.then_inc(sems[ename], 1)
                if ename == "act":
                    for s, v in final.items():
                        e.wait_ge(sems[s], v)
            return run

        with nc.Block() as blk:
            blk.sync(body("sp"))
            blk.scalar(body("act"))
            blk.vector(body("dve"))
            blk.gpsimd(body("pool"))
            blk.tensor(body("pe"))


def build_program(NT=16, layers=(0, 1, 2, 3), mixers=True):
    nc = bass.Bass("TRN2", target_bir_lowering=False)
    wtab, wgroups, WTOT = _weight_table()
    ntok = NT * T
    xT = nc.dram_tensor("xT", [D, ntok], F32, kind="ExternalInput").ap()
    memT = nc.dram_tensor("memT", [D, NMEM], F32, kind="ExternalInput").ap()
    wf = nc.dram_tensor("wf", [128, WTOT], F32, kind="ExternalInput").ap()
    prm = nc.dram_tensor("prm", [128, 124], F32, kind="ExternalInput").ap()
    cst = nc.dram_tensor("cst", [128, 896], F32, kind="ExternalInput").ap()
    btin = nc.dram_tensor("btin", [12, 128, 384], F32, kind="ExternalInput").ap()
    y = nc.dram_tensor("y", [D, ntok], F32, kind="ExternalOutput").ap()
    wb = nc.dram_tensor("wb", [128, WTOT], BF16, kind="Internal").ap()

    S = Sched()
    st = ExitStack()
    with st:
        def sb(name, shape, dt):
            return st.enter_context(nc.sbuf_tensor(name, shape, dt))

        hbuf = [sb("h0", [128, KC, T], F32), sb("h1", [128, KC, T], F32)]
        sqb = sb("sqb", [128, 2, T], BF16)
        rs_s = sb("rs_s", [128, T], F32)
        rstd = sb("rstd", [128, T], F32)
        xn = sb("xn", [128, KC, T], BF16)
        qT = sb("qT", [128, 6, T], BF16)
        uT = qT
        qmT = sb("qmT", [128, 2, T], BF16)
        kpad = [sb("kpad%d" % i, [128, 4, 640], BF16) for i in range(2)]
        vpad = [sb("vpad%d" % i, [128, 5, 4, 128], BF16) for i in range(2)]
        kmp = sb("kmp", [128, 4, 4, 256], BF16)
        vmp = sb("vmp", [128, 4, 2, 4, 128], BF16)
        Pb = [sb("Pb%d" % i, [128, 4, 256], BF16) for i in range(NPAR)]
        PTs = [sb("PTs%d" % i, [128, 4, 2, 128], BF16) for i in range(NPAR)]
        dgb = [sb("dg%d" % i, [128, 4, 128], BF16) for i in range(NPAR)]
        stt_ = [sb("stt%d" % i, [128, 8, 4], F32) for i in range(NPAR)]
        actb = sb("actb", [128, NJ, T], BF16)
        sgb = sb("sgb", [128, 2, T], F32)
        vgb = sb("vgb", [128, 3, 768], F32)
        nbuf = sb("nbuf", [128, NBLK, 768], BF16)
        bnst = sb("bnst", [128, 3, 6, 6], F32)
        mvb = sb("mvb", [128, 3, 6, 2], F32)
        lnr = sb("lnr", [128, 3, 2, 6], F32)
        tmpf = sb("tmpf", [128, 2, T], F32)
        ob_ap = [sgb[:, 0, :], sgb[:, 1, :], tmpf[:, 0, :], tmpf[:, 1, :]]
        ob_key = [("sg", 0), ("sg", 1), ("tmp", 0), ("tmp", 1)]
        wring = sb("wring", [128, RS, SLOT], BF16)
        cstf = sb("cstf", [128, 896], F32)
        identb = sb("identb", [128, 128], BF16)
        onesb = sb("onesb", [128, 128], BF16)
        maskb = sb("maskb", [128, 2, 256], BF16)
        WsTb = sb("WsTb", [128, 12, 128], BF16)
        BT = sb("BT", [128, 12, 128], F32)
        prmb = sb("prmb", [128, 124], F32)
        nsink = sb("nsink", [128, 32], F32)
        ps = st.enter_context(nc.psum_tensor("ps", [128, 8, 512], F32))

        Gc = lambda col: prmb[:, col:col + 1]
        sink_ap = lambda c0: prmb[:, 80 + c0:80 + c0 + 4]
        nsink_ap = lambda c0: nsink[:, c0:c0 + 4]
        lg_ap = lambda i: prmb[:, 112 + i:112 + i + 1]

        bank_ptr = [0]

        def nb():
            b = bank_ptr[0]
            bank_ptr[0] = (b + 1) % 8
            return b

        def nb2():
            if bank_ptr[0] % 2:
                bank_ptr[0] = (bank_ptr[0] + 1) % 8
            b = bank_ptr[0]
            bank_ptr[0] = (b + 2) % 8
            return b

        def mm(out, lhsT, rhs, start, stop, reads, writes):
            S.add("pe", lambda e, o=out, l=lhsT, r=rhs, a=start, b=stop: e.matmul(o, l, r, start=a, stop=b),
                  reads=reads, writes=writes)

        def act(out, in_, func, reads, writes, scale=1.0, bias=None, accum_out=None):
            def f(e, o=out, i=in_, fn=func, sc=scale, bi=bias, ac=accum_out):
                kw = {}
                if bi is not None:
                    kw["bias"] = bi
                if ac is not None:
                    kw["accum_out"] = ac
                return e.activation(out=o, in_=i, func=fn, scale=sc, **kw)
            S.add("act", f, reads=reads, writes=writes)

        def tt(eng, out, in0, in1, op, reads, writes):
            S.add(eng, lambda e, o=out, a=in0, b=in1, p=op: e.tensor_tensor(out=o, in0=a, in1=b, op=p),
                  reads=reads, writes=writes)

        def ts(eng, out, in0, s1, s2, op0, op1, reads, writes):
            def f(e, o=out, a=in0, x=s1, y_=s2, p0=op0, p1=op1):
                if p1 is None:
                    return e.tensor_scalar(out=o, in0=a, scalar1=x, scalar2=None, op0=p0)
                return e.tensor_scalar(out=o, in0=a, scalar1=x, scalar2=y_, op0=p0, op1=p1)
            S.add(eng, f, reads=reads, writes=writes)

        def stt(out, in0, scalar, in1, op0, op1, reads, writes):
            S.add("dve", lambda e, o=out, a=in0, s=scalar, b=in1, p0=op0, p1=op1:
                  e.scalar_tensor_tensor(out=o, in0=a, scalar=s, in1=b, op0=p0, op1=p1),
                  reads=reads, writes=writes)

        def cp(eng, out, in_, reads, writes):
            if eng == "act":
                S.add("act", lambda e, o=out, i=in_: e.copy(out=o, in_=i), reads=reads, writes=writes)
            else:
                S.add(eng, lambda e, o=out, i=in_: e.tensor_copy(out=o, in_=i), reads=reads, writes=writes)

        def recip(out, in_, reads, writes):
            S.add("dve", lambda e, o=out, i=in_: e.reciprocal(out=o, in_=i), reads=reads, writes=writes)

        def recip_fast(out, in_, reads, writes, Tn=T):
            S.add("dve", lambda e, o=out, i=in_, sc=tmpf[:, 0, 0:Tn]: e.reciprocal_approx_accurate(out=o, in_=i, scratch=sc),
                  reads=reads + [("tmp", 0)], writes=writes + [("tmp", 0)])

        def memset(eng, ap, val, writes):
            S.add(eng, lambda e, a=ap, v=val: e.memset(a, v), writes=writes)

        def dma(eng, out, in_, key, reads, writes, done_val=None, **kw):
            S.add(eng, lambda e, o=out, i=in_, k=kw: e.dma_start(out=o, in_=i, **k),
                  reads=reads, writes=writes, dma_key=key, done_val=done_val)

        piece_ctr = [0]

        def load_piece(key):
            off, n, grp = wtab[key]
            slot = piece_ctr[0] % RS
            piece_ctr[0] += 1
            dma("sp", wring[:, slot, 0:n], wb[:, off:off + n], ("w", slot),
                reads=[("wb", key)], writes=[("w", slot)])
            return slot, n

        def wview(slot, n, kc):
            return wring[:, slot, 0:n].rearrange("p (k f) -> p k f", k=kc)

        dma("sp", cstf[:, :], cst[:, :], ("c", 0), reads=[], writes=[("cstf",)])
        dma("sp", prmb[:, :], prm[:, :], ("c", 1), reads=[], writes=[("prm",)])
        cp("dve", identb[:, :], cstf[:, 0:128], [("cstf",)], [("identb",)])
        cp("dve", maskb[:, :, :], cstf[:, 128:640].rearrange("p (a b) -> p a b", a=2), [("cstf",)], [("maskb",)])
        cp("dve", onesb[:, :], cstf[:, 768:896], [("cstf",)], [("onesb",)])
        ts("dve", nsink[:, :], prmb[:, 80:112], -1.0, None, ALU.mult, None, [("prm",)], [("nsink",)])
        for i in range(2):
            memset("pool", kpad[i][:, :, :], 0.0, [("kp", i, v) for v in range(4)])
            memset("pool", vpad[i][:, :, :, :], 0.0, [("vp", i, b) for b in range(5)])
        memset("pool", kmp[:, :, :, :], 0.0, [("kmp", l) for l in range(4)])
        memset("pool", vmp[:, :, :, :, :], 0.0, [("vmp", l) for l in range(4)])

        order = [("mem", l) for l in range(4)]
        for l in range(4):
            order += [(l, i) for i in range(25)]
        for key in order:
            off, n, grp = wtab[key]
            dma("pool", wb[:, off:off + n], wf[:, off:off + n], ("cast",) + grp, reads=[],
                writes=[("wb", key)], done_val=16 * wgroups[grp], max_dma_last_dim=8192)

        sq_ctr = [0]

        def rmsnorm(hb, gcol0, Tn, dst_fn, dst_keys_fn, final=False):
            bank = nb()
            for kc in range(KC):
                i = sq_ctr[0] % 2
                sq_ctr[0] += 1
                act(sqb[:, i, 0:Tn], hbuf[hb][:, kc, 0:Tn], AF.Square, [("h", hb, kc)], [("sq", i)])
                mm(ps[:, bank, 0:Tn], onesb[:, :], sqb[:, i, 0:Tn], kc == 0, kc == KC - 1,
                   [("sq", i), ("onesb",)], [("ps", bank)])
            act(rs_s[:, 0:Tn], ps[:, bank, 0:Tn], AF.Ln, [("ps", bank), ("epsb",)], [("rs",)],
                scale=1.0 / D, bias=epsb[:, 0:1])
            act(rstd[:, 0:Tn], rs_s[:, 0:Tn], AF.Exp, [("rs",)], [("rstd",)], scale=-0.5)
            for kc in range(KC):
                stt(dst_fn(kc), hbuf[hb][:, kc, 0:Tn], Gc(gcol0 + kc), rstd[:, 0:Tn], ALU.mult, ALU.mult,
                    [("h", hb, kc), ("prm",), ("rstd",)], dst_keys_fn(kc))

        epsb = sb("epsb", [128, 1], F32)
        dmy = sb("dmy", [128, 1], F32)
        memset("dve", epsb[:, :], EPS, [("epsb",)])

        def proj_chunk(w3, col0, nk, rhs_fn, rhs_keys_fn, wkey, Tn=T):
            bank = nb()
            for k in range(nk):
                mm(ps[:, bank, 0:Tn], w3[:, k, col0:col0 + 128], rhs_fn(k), k == 0, k == nk - 1,
                   [wkey] + rhs_keys_fn(k), [("ps", bank)])
            return bank

        def proj_chunks_kouter(w3, col0s, nk, rhs_fn, rhs_keys_fn, wkey, Tn=T):
            banks = [nb() for _ in col0s]
            for k in range(nk):
                for b, c0 in zip(banks, col0s):
                    mm(ps[:, b, 0:Tn], w3[:, k, c0:c0 + 128], rhs_fn(k), k == 0, k == nk - 1,
                       [wkey] + rhs_keys_fn(k), [("ps", b)])
            return banks

        ev_ctr = [0]

        def evac_scaled(out, bank, scale, writes, Tn=T):
            ev_ctr[0] += 1
            if ev_ctr[0] % 2:
                act(out, ps[:, bank, 0:Tn], AF.Copy, [("ps", bank)], writes, scale=scale)
            else:
                ts("dve", out, ps[:, bank, 0:Tn], scale, None, ALU.mult, None, [("ps", bank)], writes)

        xnk = lambda k: [("xn", k)]

        dma("sp", hbuf[1][:, :, 0:NMEM], memT.rearrange("(k p) t -> p k t", p=128), ("x", 1),
            reads=[], writes=[("h", 1, k) for k in range(KC)])
        rmsnorm(1, 0, NMEM, lambda kc: xn[:, kc, 0:NMEM], lambda kc: [("xn", kc)])
        for l in range(4):
            slot, n = load_piece(("mem", l))
            w3 = wview(slot, n, KC)
            for cm in range(2):
                bank = proj_chunk(w3, cm * 128, KC, lambda k: xn[:, k, 0:NMEM], xnk, ("w", slot), Tn=NMEM)
                cp("act", kmp[0:64, l, 2 * cm, :], ps[0:64, bank, 0:NMEM], [("ps", bank)], [("kmp", l)])
                cp("dve", kmp[64:128, l, 2 * cm + 1, :], ps[64:128, bank, 0:NMEM], [("ps", bank)], [("kmp", l)])
            for blk in range(2):
                bank = nb()
                for k in range(KC):
                    mm(ps[:, bank, 0:256], xn[:, k, blk * 128:(blk + 1) * 128], w3[:, k, 256:512], k == 0, k == KC - 1,
                       [("w", slot), ("xn", k)], [("ps", bank)])
                src = ps[:, bank, 0:256].rearrange("p (k o d) -> p k o d", k=2, o=2)
                dst = vmp[:, l, blk, :, :].rearrange("p (k o) d -> p k o d", o=2)
                cp("act", dst[:, :, 0, 0:64], src[:, :, 0, :], [("ps", bank)], [("vmp", l)])
                cp("dve", dst[:, :, 1, 64:128], src[:, :, 1, :], [("ps", bank)], [("vmp", l)])

        if mixers:
            for i in range(12):
                sgi = i % 2
                stg = sgb[:, sgi, 0:384]
                wm = tmpf[:, sgi, 0:128]
                dma("sp", stg, btin[i, :, :], ("bt", sgi), reads=[], writes=[("sg", sgi)])
                tt("dve", wm, stg[:, 0:128], cstf[:, 640:768], ALU.mult,
                   [("sg", sgi), ("cstf",)], [("tmp", sgi)])
                cp("dve", WsTb[:, i, :], wm, [("tmp", sgi)], [("WsTb", i)])
                bank = nb()
                mm(ps[:, bank, 0:128], stg[:, 128:256], wm, True, False,
                   [("sg", sgi), ("tmp", sgi)], [("ps", bank)])
                mm(ps[:, bank, 0:128], cstf[0:1, 768:896], stg[0:1, 256:384], False, True,
                   [("sg", sgi), ("cstf",)], [("ps", bank)])
                cp("act", BT[:, i, :], ps[:, bank, 0:128], [("ps", bank)], [("BT", i)])

        par = [0]

        def make_group(heads, sink_c0, out_ap, out_keys, same_kv=False):
            gidx = par[0]
            i = par[0] % NPAR
            par[0] += 1
            stv = stt_[i]
            nm, negm, rsum, stx, den, rr = (stv[:, j, :] for j in range(6))
            state = {}

            def stA():
                b0 = 2 * (gidx % 2)
                state["sc"] = b0
                for pr in range(2):
                    hd = heads[2 * pr]
                    bank = b0 + pr
                    o2 = ps[:, bank, :].rearrange("p (a k) -> p a k", a=2)
                    mm(o2, hd["q"], hd["k2"], True, hd["mask"] is None, hd["keys_qk"], [("ps", bank)])
                    if hd["mask"] is not None:
                        mm(o2, identb[:, :], hd["mask"].unsqueeze(1).broadcast_to([128, 2, 256]), False, True,
                           [("identb",), ("maskb",)], [("ps", bank)])

            def stB1():
                b0 = state["sc"]
                scv = ps[:, b0:b0 + 2, :].rearrange("p b (s k) -> p (b s) k", s=2)
                S.add("dve", lambda e: e.tensor_reduce(out=nm, in_=scv, axis=AX.X, op=ALU.max, negate=True),
                      reads=[("ps", b0), ("ps", b0 + 1)], writes=[("st", i, "nm")])
                tt("dve", negm, nm, nsink_ap(sink_c0), ALU.min, [("st", i, "nm"), ("nsink",)], [("st", i, "negm")])
                tt("dve", stx, negm, sink_ap(sink_c0), ALU.add, [("st", i, "negm"), ("prm",)], [("st", i, "stx")])

            def stB2():
                b0 = state["sc"]
                for s in range(4):
                    bank = b0 + s // 2
                    col = (s % 2) * 256
                    act(Pb[i][:, s, :], ps[:, bank, col:col + 256], AF.Exp, [("ps", bank), ("st", i, "negm")],
                        [("P", i), ("st", i, "rsum")], bias=negm[:, s:s + 1], accum_out=rsum[:, s:s + 1])
                act(stx, stx, AF.Exp, [("st", i, "stx")], [("st", i, "stx")])

            def stB3():
                tt("dve", den, rsum, stx, ALU.add, [("st", i, "rsum"), ("st", i, "stx")], [("st", i, "den")])
                recip(rr, den, [("st", i, "den")], [("st", i, "rr")])
                for s in range(4):
                    ts("pool", dgb[i][:, s, :], identb[:, :], rr[:, s:s + 1], 1.0, ALU.mult, ALU.mult,
                       [("identb",), ("st", i, "rr")], [("dg", i)])

            def stC():
                p0 = 4
                for s in range(4):
                    bank = p0 + s // 2
                    for kb in range(2):
                        col = (s % 2) * 256 + kb * 128
                        mm(ps[:, bank, col:col + 128], Pb[i][:, s, kb * 128:(kb + 1) * 128], dgb[i][:, s, :], True, True,
                           [("P", i), ("dg", i)], [("ps", bank)])
                cp("act",
                   PTs[i][:, :, :, :].rearrange("p (b s) k q -> p b (s k q)", b=2), ps[:, p0:p0 + 2, :],
                   [("ps", p0), ("ps", p0 + 1)], [("PT", i)])

            def stD():
                ob = 6 + gidx % 2
                if same_kv:
                    o2 = ps[:, ob, 0:256].rearrange("p (a q) -> p a q", a=2)
                    ptv = PTs[i][:, :, :, :].rearrange("p (a o) k q -> p o k a q", o=2)
                    cnt = 0
                    for o_ in range(2):
                        hd = heads[o_]
                        for kb in range(2):
                            mm(o2, hd["v%d" % kb], ptv[:, o_, kb, :, :], cnt == 0, cnt == 3,
                               [("PT", i)] + hd["keys_v"], [("ps", ob)])
                            cnt += 1
                    cp("dve", out_ap, o2, [("ps", ob)], out_keys)
                    return
                for pr in range(2):
                    cnt = 0
                    for s in (2 * pr, 2 * pr + 1):
                        hd = heads[s]
                        for kb in range(2):
                            mm(ps[:, ob, pr * 128:(pr + 1) * 128], hd["v%d" % kb], PTs[i][:, s, kb, :], cnt == 0, cnt == 3,
                               [("PT", i)] + hd["keys_v"], [("ps", ob)])
                            cnt += 1
                cp("dve", out_ap, ps[:, ob, 0:256].rearrange("p (a q) -> p a q", a=2), [("ps", ob)], out_keys)

            return [stA, stB1, stB2, stB3, stC, stD]

        SKEW = (0, 1, 1, 2, 3, 4)

        deferred = []

        def run_deferred():
            while deferred:
                deferred.pop(0)()

        def run_groups(groups, extra=()):
            n = len(groups)
            for step in range(max(n + SKEW[-1], len(extra))):
                for sidx in range(6):
                    g = step - SKEW[sidx]
                    if 0 <= g < n:
                        groups[g][sidx]()
                if step < len(extra):
                    extra[step]()

        def mem_heads(l, blk):
            hs = []
            for hm in range(4):
                hs.append(dict(q=qmT[:, hm // 2, blk * 128:(blk + 1) * 128], k2=kmp[:, l, 2 * (hm // 2):2 * (hm // 2) + 2, :], mask=None,
                               v0=vmp[:, l, 0, hm, :], v1=vmp[:, l, 1, hm, :],
                               keys_qk=[("qm", hm // 2), ("kmp", l)], keys_v=[("vmp", l)]))
            return hs

        def out_proj_and_ffn(l, hb):
            for pi in range(2):
                slot, n = load_piece((l, 4 + pi))
                w3 = wview(slot, n, KC)
                for m_ in range(4):
                    m = pi * 4 + m_
                    bank = proj_chunk(w3, m_ * 128, KC, lambda k: xn[:, k, :], xnk, ("w", slot))
                    tt("dve", hbuf[hb][:, m, :], ps[:, bank, :], hbuf[hb][:, m, :], ALU.add,
                       [("ps", bank), ("h", hb, m)], [("h", hb, m)])
            ffn(l, hb)

        def ffn(l, hb):
            rmsnorm(hb, 8 * (5 + l), T, lambda kc: xn[:, kc, :], lambda kc: [("xn", kc)])
            for pi in range(NJ // 2):
                slot, n = load_piece((l, 6 + pi))
                w3 = wview(slot, n, KC)
                if pi == 0:
                    b4 = proj_chunks_kouter(w3, [0, 128, 256, 384], KC, lambda k: xn[:, k, :], xnk, ("w", slot))
                for jj in range(2):
                    j = 2 * pi + jj
                    if pi == 0:
                        bg, bu = b4[2 * jj], b4[2 * jj + 1]
                    else:
                        bg = proj_chunk(w3, jj * 256, KC, lambda k: xn[:, k, :], xnk, ("w", slot))
                        bu = proj_chunk(w3, jj * 256 + 128, KC, lambda k: xn[:, k, :], xnk, ("w", slot))
                    si = j % 2
                    act(sgb[:, si, :], ps[:, bg, :], AF.Silu, [("ps", bg)], [("sg", si)])
                    tt("dve", actb[:, j, :], ps[:, bu, :], sgb[:, si, :], ALU.mult,
                       [("ps", bu), ("sg", si)], [("act", j)])
            act(dmy[:, 0:1], epsb[:, 0:1], AF.Exp, [("epsb",)], [("dmy",)])
            for m in range(8):
                slot, n = load_piece((l, 17 + m))
                w3 = wview(slot, n, NJ)
                bank = proj_chunk(w3, 0, NJ, lambda k: actb[:, k, :], lambda k: [("act", k)], ("w", slot))
                tt("dve", hbuf[hb][:, m, :], ps[:, bank, :], hbuf[hb][:, m, :], ALU.add,
                   [("ps", bank), ("h", hb, m)], [("h", hb, m)])

        def layer_A(l, hb, t):
            la = l // 2
            rmsnorm(hb, 8 * (1 + l), T, lambda kc: xn[:, kc, :], lambda kc: [("xn", kc)])
            rhs = lambda k: xn[:, k, :]
            s0, n0 = load_piece((l, 0))
            w3 = wview(s0, n0, KC)
            banks = proj_chunks_kouter(w3, [0, 128, 256, 384], KC, rhs, xnk, ("w", s0))
            for c in range(4):
                evac_scaled(qT[:, c, :], banks[c], 0.125, [("q", c)])
            s1, n1 = load_piece((l, 1))
            w3 = wview(s1, n1, KC)
            for c in range(2):
                bank = proj_chunk(w3, c * 128, KC, rhs, xnk, ("w", s1))
                evac_scaled(qT[:, 4 + c, :], bank, 0.125, [("q", 4 + c)])
            for c in range(2):
                bank = proj_chunk(w3, 256 + c * 128, KC, rhs, xnk, ("w", s1))
                evac_scaled(qmT[:, c, :], bank, 0.125, [("qm", c)])
            s2, n2 = load_piece((l, 2))
            w3 = wview(s2, n2, KC)
            bka = proj_chunk(w3, 0, KC, rhs, xnk, ("w", s2))
            bkb = proj_chunk(w3, 128, KC, rhs, xnk, ("w", s2))
            cp("act", kpad[la][0:64, 0, 128:640], ps[0:64, bka, :], [("ps", bka)], [("kp", la, 0)])
            cp("dve", kpad[la][64:128, 3, 128:640], ps[64:128, bka, :], [("ps", bka)], [("kp", la, 3)])
            cp("act", kpad[la][0:64, 2, 128:640], ps[0:64, bkb, :], [("ps", bkb)], [("kp", la, 2)])
            cp("dve", kpad[la][64:128, 1, 128:640], ps[64:128, bkb, :], [("ps", bkb)], [("kp", la, 1)])
            s3, n3 = load_piece((l, 3))
            w3 = wview(s3, n3, KC)
            for blk in range(NBLK):
                bank = nb()
                for k in range(KC):
                    mm(ps[:, bank, 0:128], xn[:, k, blk * 128:(blk + 1) * 128], w3[:, k, 0:128], k == 0, k == KC - 1,
                       [("w", s3), ("xn", k)], [("ps", bank)])
                src = ps[:, bank, 0:128].rearrange("p (k d) -> p k d", k=2)
                dst = vpad[la][:, 1 + blk, :, :].rearrange("p (k o) d -> p k o d", o=2)
                cp("act", dst[:, :, 0, 0:64], src, [("ps", bank)], [("vp", la, 1 + blk)])
                cp("dve", dst[:, :, 1, 64:128], src, [("ps", bank)], [("vp", la, 1 + blk)])
            groups = []
            for blk in range(NBLK):
                first = (t == 0 and blk == 0)
                for gi in range(3):
                    hs = []
                    for s in range(4):
                        h = 4 * gi + s
                        var = (h // 6) * 2 + (h % 2)
                        hs.append(dict(q=qT[:, h // 2, blk * 128:(blk + 1) * 128],
                                       k2=kpad[la][:, 2 * (h // 6):2 * (h // 6) + 2, blk * 128:blk * 128 + 256],
                                       mask=maskb[:, 1 if first else 0, :],
                                       v0=vpad[la][:, blk, var, :], v1=vpad[la][:, blk + 1, var, :],
                                       keys_qk=[("q", h // 2), ("kp", la, 2 * (h // 6)), ("kp", la, 2 * (h // 6) + 1)],
                                       keys_v=[("vp", la, blk), ("vp", la, blk + 1)]))
                    groups.append(make_group(hs, la * 16 + gi * 4, xn[:, 2 * gi:2 * gi + 2, blk * 128:(blk + 1) * 128],
                                             [("xn", 2 * gi), ("xn", 2 * gi + 1)], same_kv=(gi != 1)))
                groups.append(make_group(mem_heads(l, blk), 12, xn[:, 6:8, blk * 128:(blk + 1) * 128],
                                         [("xn", 6), ("xn", 7)]))
            run_deferred()
            run_groups(groups)
            cp("pool", kpad[la][:, :, 0:128], kpad[la][:, :, 512:640], [("kp", la, v) for v in range(4)],
               [("kp", la, v) for v in range(4)])
            cp("pool", vpad[la][:, 0, :, :], vpad[la][:, 4, :, :], [("vp", la, 4)], [("vp", la, 0)])
            out_proj_and_ffn(l, hb)

        def layer_B(l, hb, t):
            lb = l // 2
            rmsnorm(hb, 8 * (1 + l), T, lambda kc: xn[:, kc, :], lambda kc: [("xn", kc)])
            rhs = lambda k: xn[:, k, :]
            s2, n2 = load_piece((l, 2))
            s3, n3 = load_piece((l, 3))
            s0, n0 = load_piece((l, 0))
            s1, n1 = load_piece((l, 1))
            w3a = wview(s2, n2, KC)
            w3b = wview(s3, n3, KC)
            w30 = wview(s0, n0, KC)
            w31 = wview(s1, n1, KC)
            chunks = [(w30, s0, c * 128, "u", c) for c in range(4)] + \
                     [(w31, s1, c * 128, "u", 4 + c) for c in range(2)] + \
                     [(w31, s1, 256 + c * 128, "qm", c) for c in range(2)]
            def ln_chain(blk, vi):
                for g in range(6):
                    S.add("dve", lambda e, o=bnst[:, vi, g, :], a=vgb[:, vi, g * 128:(g + 1) * 128]: e.bn_stats(out=o, in_=a),
                          reads=[("vg", vi, 0), ("vg", vi, 1)], writes=[("bn", vi, g)])
                    S.add("dve", lambda e, o=mvb[:, vi, g, :], a=bnst[:, vi, g, :]: e.bn_aggr(out=o, in_=a),
                          reads=[("bn", vi, g)], writes=[("mv", vi)])
                act(lnr[:, vi, 0, :], mvb[:, vi, :, 1], AF.Ln, [("mv", vi), ("epsb",)], [("lnr", vi, 0)],
                    bias=epsb[:, 0:1])
                act(lnr[:, vi, 1, :], lnr[:, vi, 0, :], AF.Exp, [("lnr", vi, 0)], [("lnr", vi, 1)], scale=-0.5)
                stt(lnr[:, vi, 0, :], mvb[:, vi, :, 0], -1.0, lnr[:, vi, 1, :], ALU.mult, ALU.mult,
                    [("mv", vi), ("lnr", vi, 1)], [("lnr", vi, 0)])
                for g in range(6):
                    ts("pool", nbuf[:, blk, g * 128:(g + 1) * 128], vgb[:, vi, g * 128:(g + 1) * 128],
                       lnr[:, vi, 1, g:g + 1], lnr[:, vi, 0, g:g + 1], ALU.mult, ALU.add,
                       [("vg", vi, 0), ("vg", vi, 1), ("lnr", vi, 0), ("lnr", vi, 1)], [("n", blk)])

            for blk in range(NBLK):
                ba = nb()
                bb = nb()
                if blk == 0:
                    ba1 = nb()
                    bb1 = nb()
                    for k in range(KC):
                        for (bq, wq, nn, bl, sk) in ((ba, w3a, 512, 0, s2), (bb, w3b, 256, 0, s3), (ba1, w3a, 512, 1, s2), (bb1, w3b, 256, 1, s3)):
                            mm(ps[:, bq, 0:nn], xn[:, k, bl * 128:(bl + 1) * 128], wq[:, k, :], k == 0, k == KC - 1,
                               [("w", sk), ("xn", k)], [("ps", bq)])
                elif blk == 1:
                    ba, bb = ba1, bb1
                else:
                    for k in range(KC):
                        mm(ps[:, ba, :], xn[:, k, blk * 128:(blk + 1) * 128], w3a[:, k, :], k == 0, k == KC - 1,
                           [("w", s2), ("xn", k)], [("ps", ba)])
                    for k in range(KC):
                        mm(ps[:, bb, 0:256], xn[:, k, blk * 128:(blk + 1) * 128], w3b[:, k, :], k == 0, k == KC - 1,
                           [("w", s3), ("xn", k)], [("ps", bb)])
                vi = blk % 3
                if blk == 3:
                    ln_chain(0, 0)
                act(vgb[:, vi, 0:512], ps[:, ba, :], AF.Gelu_apprx_tanh, [("ps", ba)], [("vg", vi, 0)])
                act(vgb[:, vi, 512:768], ps[:, bb, 0:256], AF.Gelu_apprx_tanh, [("ps", bb)], [("vg", vi, 1)])
                for (w3c, sc, col0, kind, ci) in chunks[2 * blk:2 * blk + 2]:
                    bank = proj_chunk(w3c, col0, KC, rhs, xnk, ("w", sc))
                    if kind == "u":
                        act(uT[:, ci, :], ps[:, bank, :], AF.Gelu_apprx_tanh, [("ps", bank)], [("q", ci)])
                    else:
                        evac_scaled(qmT[:, ci, :], bank, 0.125, [("qm", ci)])
            ln_chains = [lambda: ln_chain(1, 1), lambda: ln_chain(2, 2), lambda: ln_chain(3, 0)]
            groups = []
            for blk in range(NBLK):
                groups.append(make_group(mem_heads(l, blk), 12, xn[:, 6:8, blk * 128:(blk + 1) * 128],
                                         [("xn", 6), ("xn", 7)]))
            run_deferred()
            run_groups(groups, extra=ln_chains)
            for g in range(6):
                bank = nb()
                idx = lb * 6 + g
                for blk in range(NBLK):
                    mm(ps[:, bank, blk * 128:(blk + 1) * 128], nbuf[:, blk, g * 128:(g + 1) * 128], WsTb[:, idx, :], True, True,
                       [("n", blk), ("WsTb", idx)], [("ps", bank)])
                ti = g % 2
                stt(tmpf[:, ti, :].rearrange("p (b t) -> p b t", b=NBLK),
                    ps[:, bank, :].rearrange("p (b t) -> p b t", b=NBLK), lg_ap(idx),
                    BT[:, idx, :].unsqueeze(1).broadcast_to([128, NBLK, 128]), ALU.mult, ALU.add,
                    [("ps", bank), ("prm",), ("BT", idx)], [("tmp", ti)])
                tt("pool", xn[:, g, :], tmpf[:, ti, :], uT[:, g, :], ALU.mult, [("tmp", ti), ("q", g)], [("xn", g)])
            out_proj_and_ffn(l, hb)

        xTv = xT.rearrange("(k p) t -> p k t", p=128)
        dma("sp", hbuf[0][:, :, :], xTv[:, :, 0:T], ("x", 0), reads=[], writes=[("h", 0, k) for k in range(KC)])
        if NT > 1:
            dma("sp", hbuf[1][:, :, :], xTv[:, :, T:2 * T], ("x", 1), reads=[],
                writes=[("h", 1, k) for k in range(KC)])
        ob_ctr = [0]

        def final_norm(t):
            hb = t % 2
            bank = nb()
            for kc in range(KC):
                i = sq_ctr[0] % 2
                sq_ctr[0] += 1
                act(sqb[:, i, :], hbuf[hb][:, kc, :], AF.Square, [("h", hb, kc)], [("sq", i)])
                mm(ps[:, bank, :], onesb[:, :], sqb[:, i, :], kc == 0, kc == KC - 1, [("sq", i), ("onesb",)], [("ps", bank)])
            act(rs_s[:, :], ps[:, bank, :], AF.Ln, [("ps", bank), ("epsb",)], [("rs",)], scale=1.0 / D, bias=epsb[:, 0:1])
            act(rstd[:, :], rs_s[:, :], AF.Exp, [("rs",)], [("rstd",)], scale=-0.5)
            for kc in range(KC):
                oi = ob_ctr[0] % 4
                ob_ctr[0] += 1
                stt(ob_ap[oi], hbuf[hb][:, kc, :], Gc(72 + kc), rstd[:, :], ALU.mult, ALU.mult,
                    [("h", hb, kc), ("prm",), ("rstd",)], [ob_key[oi]])
                dma("sp", y[kc * 128:(kc + 1) * 128, t * T:(t + 1) * T], ob_ap[oi], ("store", oi),
                    reads=[ob_key[oi]], writes=[])
            if t + 2 < NT:
                dma("sp", hbuf[hb][:, :, :], xTv[:, :, (t + 2) * T:(t + 3) * T], ("x", hb), reads=[],
                    writes=[("h", hb, k) for k in range(KC)])

        for t in range(NT):
            hb = t % 2
            for l in layers:
                if not mixers:
                    ffn(l, hb)
                elif l % 2 == 0:
                    layer_A(l, hb, t)
                else:
                    layer_B(l, hb, t)
            if t + 1 < NT and mixers:
                deferred.append(lambda t=t: final_norm(t))
            else:
                final_norm(t)

        S.emit(nc, st)
    return nc


_CACHE = {}


def kernel(**inputs):
    inp = {k: np.asarray(v) for k, v in inputs.items()}
    x = inp["x"]
    mem = inp["mem"]
    B = x.shape[0]
    wfull, _, _, _ = _pack_weights(inp)
    prm = _pack_params(inp)
    bt = _pack_bt(inp)
    cst = _consts()
    if "nc" not in _CACHE:
        _CACHE["nc"] = build_program()
    nc = _CACHE["nc"]
    in_maps = []
    for b in range(B):
        in_maps.append({
            "xT": np.ascontiguousarray(x[b].T),
            "memT": np.ascontiguousarray(mem[b].T),
            "wf": wfull, "prm": prm, "cst": cst, "btin": bt,
        })
    res = run_bass_kernel_spmd(nc, in_maps, core_ids=list(range(B)))
    out = np.stack([np.ascontiguousarray(res.results[b]["y"].T) for b in range(B)], axis=0)
    return out.astype(np.float32)
```

```python
import numpy as np
from contextlib import ExitStack
import concourse.bass as bass
import concourse.mybir as mybir
from concourse.bass_utils import run_bass_kernel_spmd

F32 = mybir.dt.float32
BF16 = mybir.dt.bfloat16
AF = mybir.ActivationFunctionType
ALU = mybir.AluOpType
AX = mybir.AxisListType

D = 1024
KC = 8
T = 512
NBLK = 4
SEQ = 8192
NMEM = 256
DFF = 2816
NJ = 22
EPS = 1e-6
NEG = -30000.0
RS = 5
NPAR = 3
SLOT = 4096


def _piece(w_cols):
    K, Fc = w_cols.shape
    kc = K // 128
    return np.ascontiguousarray(w_cols.reshape(kc, 128, Fc).transpose(1, 0, 2)).reshape(128, kc * Fc)


def _layer_pieces(l, inp):
    j = l // 2
    out = []
    if l % 2 == 0:
        w = inp["a_w_in"][j]
        q = w[:, 0:768]
        k = w[:, 768:896]
        v = w[:, 896:1024]
        qm = w[:, 1024:1280]
        kpad = np.concatenate([k, k[:, 64:128], k[:, 0:64]], axis=1)
        out.append(("in", _piece(q[:, 0:512])))
        out.append(("in", _piece(np.concatenate([q[:, 512:768], qm], axis=1))))
        out.append(("in", _piece(kpad)))
        out.append(("in", _piece(v)))
        wo = inp["a_w_out"][j]
    else:
        w = inp["b_w_in"][j]
        u = w[:, 0:768]
        v = w[:, 768:1536]
        qm = w[:, 1536:1792]
        out.append(("in", _piece(u[:, 0:512])))
        out.append(("in", _piece(np.concatenate([u[:, 512:768], qm], axis=1))))
        out.append(("in", _piece(v[:, 0:512])))
        out.append(("in", _piece(v[:, 512:768])))
        wo = inp["b_w_out"][j]
    out.append(("out", _piece(wo[:, 0:512])))
    out.append(("out", _piece(wo[:, 512:1024])))
    wgu = inp["w_gate_up"][l]
    for p in range(NJ // 2):
        j0, j1 = 2 * p, 2 * p + 1
        cols = np.concatenate([wgu[:, j0 * 128:(j0 + 1) * 128], wgu[:, DFF + j0 * 128:DFF + (j0 + 1) * 128],
                               wgu[:, j1 * 128:(j1 + 1) * 128], wgu[:, DFF + j1 * 128:DFF + (j1 + 1) * 128]], axis=1)
        out.append(("gu", _piece(cols)))
    wd = inp["w_down"][l]
    for m in range(8):
        out.append(("dn", _piece(wd[:, m * 128:(m + 1) * 128])))
    return out


def _pack_weights(inp):
    arrs = []
    table = {}
    groups = {}
    off = 0

    def put(key, grp, a):
        nonlocal off
        table[key] = (off, a.shape[1], grp)
        groups[grp] = groups.get(grp, 0) + 1
        arrs.append(a)
        off += a.shape[1]

    for l in range(4):
        put(("mem", l), ("mem",), _piece(inp["w_mem_kv"][l]))
    for l in range(4):
        for i, (fam, a) in enumerate(_layer_pieces(l, inp)):
            put((l, i), (l, fam), a)
    return np.concatenate(arrs, axis=1), table, groups, off


def _weight_table():
    table = {}
    groups = {}
    off = 0

    def put(key, grp, n):
        nonlocal off
        table[key] = (off, n, grp)
        groups[grp] = groups.get(grp, 0) + 1
        off += n

    for l in range(4):
        put(("mem", l), ("mem",), 4096)
    for l in range(4):
        sizes = [("in", 4096), ("in", 4096), ("in", 2048 if l % 2 == 0 else 4096), ("in", 1024 if l % 2 == 0 else 2048),
                 ("out", 4096), ("out", 4096)] + [("gu", 4096)] * 11 + [("dn", 2816)] * 8
        for i, (fam, n) in enumerate(sizes):
            put((l, i), (l, fam), n)
    return table, groups, off


def _pack_params(inp):
    gs = [inp["mem_norm_g"]] + [inp["mix_norm_g"][i] for i in range(4)] + \
         [inp["ffn_norm_g"][i] for i in range(4)] + [inp["final_norm_g"]]
    G = np.stack([g.reshape(KC, 128).T for g in gs], axis=1).reshape(128, 80)
    sk = np.full((2, 16), NEG, np.float32)
    sk[:, 0:12] = inp["a_sinks"]
    sk = np.broadcast_to(sk.reshape(1, 32), (128, 32))
    lg = inp["b_ln_g"].reshape(12, 128).T
    return np.ascontiguousarray(np.concatenate([G, sk, lg], axis=1).astype(np.float32))


def _pack_bt(inp):
    out = np.zeros((12, 128, 384), np.float32)
    for l in range(2):
        for g in range(6):
            i = l * 6 + g
            out[i, :, 0:128] = inp["b_w_s"][l, g].T
            out[i, :, 128:256] = np.broadcast_to(inp["b_ln_b"][l, g][None, :], (128, 128))
            out[i, :, 256:384] = np.broadcast_to(inp["b_bias_s"][l, g][None, :], (128, 128))
    return out


def _consts():
    c = np.zeros((128, 896), np.float32)
    c[:, 0:128] = np.eye(128, dtype=np.float32)
    qi = np.arange(128)[:, None]
    kj = np.arange(128)[None, :]
    prev = np.where(kj > qi, 0.0, NEG)
    cur = np.where(kj <= qi, 0.0, NEG)
    c[:, 128:256] = prev
    c[:, 256:384] = cur
    c[:, 384:512] = NEG
    c[:, 512:640] = cur
    c[:, 640:768] = (qi <= kj).astype(np.float32)
    c[:, 768:896] = 1.0
    return c


class _Op:
    __slots__ = ("eng", "fn", "deps", "signal", "done_sem", "done_val", "dma_key", "idx")


class Sched:
    ENGS = ("pe", "act", "dve", "pool", "sp")

    def __init__(self):
        self.ops = {e: [] for e in self.ENGS}
        self.n = 0
        self.lastw = {}
        self.rd = {}
        self.dma_cnt = {}
        self.dma_keys = []

    def add(self, eng, fn, reads=(), writes=(), dma_key=None, done_val=None):
        op = _Op()
        op.eng = eng
        op.fn = fn
        op.signal = False
        op.dma_key = dma_key
        op.idx = self.n
        op.done_sem = None
        op.done_val = None
        self.n += 1
        cand = []
        for k in reads:
            w = self.lastw.get(k)
            if w is not None:
                cand.append(w)
        for k in writes:
            w = self.lastw.get(k)
            if w is not None:
                cand.append(w)
            r = self.rd.get(k)
            if r:
                for v in r.values():
                    if isinstance(v, list):
                        cand.extend(v)
                    else:
                        cand.append(v)
        best = {}
        dmas = {}
        for d in cand:
            if d is op:
                continue
            if d.dma_key is not None:
                o = dmas.get(d.dma_key)
                if o is None or d.done_val > o.done_val:
                    dmas[d.dma_key] = d
            else:
                if d.eng == "pe" and eng == "pe" and dma_key is None:
                    continue
                o = best.get(d.eng)
                if o is None or d.idx > o.idx:
                    best[d.eng] = d
        op.deps = list(best.values()) + list(dmas.values())
        for d in best.values():
            d.signal = True
        for k in reads:
            r = self.rd.setdefault(k, {})
            if dma_key is not None:
                r.setdefault("dma", []).append(op)
            else:
                r[eng] = op
        for k in writes:
            self.lastw[k] = op
            self.rd[k] = {}
        if dma_key is not None:
            if dma_key not in self.dma_cnt:
                self.dma_cnt[dma_key] = 0
                self.dma_keys.append(dma_key)
            self.dma_cnt[dma_key] += 16
            op.done_sem = ("dma", dma_key)
            op.done_val = done_val if done_val is not None else self.dma_cnt[dma_key]
        self.ops[eng].append(op)
        return op

    def emit(self, nc, stack):
        engs = {"pe": nc.tensor, "act": nc.scalar, "dve": nc.vector, "pool": nc.gpsimd, "sp": nc.sync}
        sems = {}
        for e in self.ENGS:
            sems[e] = stack.enter_context(nc.semaphore("prog_" + e))
        for i, k in enumerate(self.dma_keys):
            sems[("dma", k)] = stack.enter_context(nc.semaphore("dma_%d" % i))
        for e in self.ENGS:
            c = 0
            for op in self.ops[e]:
                if op.dma_key is None and op.signal:
                    c += 1
                    op.done_sem = e
                    op.done_val = c
        with nc.Block() as blk:
            @blk.sync
            def _(sync):
                for s in sems.values():
                    sync.sem_clear(s)
        final = {("dma", k): v for k, v in self.dma_cnt.items() if k[0] == "store"}

        def body(ename):
            def run(e):
                waited = {}
                for op in self.ops[ename]:
                    need = {}
                    for d in op.deps:
                        v = need.get(d.done_sem, 0)
                        if d.done_val > v:
                            need[d.done_sem] = d.done_val
                    for s, v in need.items():
                        if waited.get(s, 0) < v:
                            e.wait_ge(sems[s], v)
                            waited[s] = v
                    ins = op.fn(e)
                    if op.dma_key is not None:
                        ins.then_inc(sems[op.done_sem], 16)
                    elif op.signal:
                        ins.then_inc(sems[ename], 1)
                if ename == "act":
                    for s, v in final.items():
                        e.wait_ge(sems[s], v)
            return run

        with nc.Block() as blk:
            blk.sync(body("sp"))
            blk.scalar(body("act"))
            blk.vector(body("dve"))
            blk.gpsimd(body("pool"))
            blk.tensor(body("pe"))


def build_program(NT=16, layers=(0, 1, 2, 3), mixers=True):
    nc = bass.Bass("TRN2", target_bir_lowering=False)
    wtab, wgroups, WTOT = _weight_table()
    ntok = NT * T
    xT = nc.dram_tensor("xT", [D, ntok], F32, kind="ExternalInput").ap()
    memT = nc.dram_tensor("memT", [D, NMEM], F32, kind="ExternalInput").ap()
    wf = nc.dram_tensor("wf", [128, WTOT], F32, kind="ExternalInput").ap()
    prm = nc.dram_tensor("prm", [128, 124], F32, kind="ExternalInput").ap()
    cst = nc.dram_tensor("cst", [128, 896], F32, kind="ExternalInput").ap()
    btin = nc.dram_tensor("btin", [12, 128, 384], F32, kind="ExternalInput").ap()
    y = nc.dram_tensor("y", [D, ntok], F32, kind="ExternalOutput").ap()
    wb = nc.dram_tensor("wb", [128, WTOT], BF16, kind="Internal").ap()

    S = Sched()
    st = ExitStack()
    with st:
        def sb(name, shape, dt):
            return st.enter_context(nc.sbuf_tensor(name, shape, dt))

        hbuf = [sb("h0", [128, KC, T], F32), sb("h1", [128, KC, T], F32)]
        sqb = sb("sqb", [128, 2, T], BF16)
        rs_s = sb("rs_s", [128, T], F32)
        rstd = sb("rstd", [128, T], F32)
        xn = sb("xn", [128, KC, T], BF16)
        qT = sb("qT", [128, 6, T], BF16)
        uT = qT
        qmT = sb("qmT", [128, 2, T], BF16)
        kpad = [sb("kpad%d" % i, [128, 4, 640], BF16) for i in range(2)]
        vpad = [sb("vpad%d" % i, [128, 5, 4, 128], BF16) for i in range(2)]
        kmp = sb("kmp", [128, 4, 4, 256], BF16)
        vmp = sb("vmp", [128, 4, 2, 4, 128], BF16)
        Pb = [sb("Pb%d" % i, [128, 4, 256], BF16) for i in range(NPAR)]
        PTs = [sb("PTs%d" % i, [128, 4, 2, 128], BF16) for i in range(NPAR)]
        dgb = [sb("dg%d" % i, [128, 4, 128], BF16) for i in range(NPAR)]
        stt_ = [sb("stt%d" % i, [128, 8, 4], F32) for i in range(NPAR)]
        actb = sb("actb", [128, NJ, T], BF16)
        sgb = sb("sgb", [128, 2, T], F32)
        vgb = sb("vgb", [128, 3, 768], F32)
        nbuf = sb("nbuf", [128, NBLK, 768], BF16)
        bnst = sb("bnst", [128, 3, 6, 6], F32)
        mvb = sb("mvb", [128, 3, 6, 2], F32)
        lnr = sb("lnr", [128, 3, 2, 6], F32)
        tmpf = sb("tmpf", [128, 2, T], F32)
        ob_ap = [sgb[:, 0, :], sgb[:, 1, :], tmpf[:, 0, :], tmpf[:, 1, :]]
        ob_key = [("sg", 0), ("sg", 1), ("tmp", 0), ("tmp", 1)]
        wring = sb("wring", [128, RS, SLOT], BF16)
        cstf = sb("cstf", [128, 896], F32)
        identb = sb("identb", [128, 128], BF16)
        onesb = sb("onesb", [128, 128], BF16)
        maskb = sb("maskb", [128, 2, 256], BF16)
        WsTb = sb("WsTb", [128, 12, 128], BF16)
        BT = sb("BT", [128, 12, 128], F32)
        prmb = sb("prmb", [128, 124], F32)
        nsink = sb("nsink", [128, 32], F32)
        ps = st.enter_context(nc.psum_tensor("ps", [128, 8, 512], F32))

        Gc = lambda col: prmb[:, col:col + 1]
        sink_ap = lambda c0: prmb[:, 80 + c0:80 + c0 + 4]
        nsink_ap = lambda c0: nsink[:, c0:c0 + 4]
        lg_ap = lambda i: prmb[:, 112 + i:112 + i + 1]

        bank_ptr = [0]

        def nb():
            b = bank_ptr[0]
            bank_ptr[0] = (b + 1) % 8
            return b

        def nb2():
            if bank_ptr[0] % 2:
                bank_ptr[0] = (bank_ptr[0] + 1) % 8
            b = bank_ptr[0]
            bank_ptr[0] = (b + 2) % 8
            return b

        def mm(out, lhsT, rhs, start, stop, reads, writes):
            S.add("pe", lambda e, o=out, l=lhsT, r=rhs, a=start, b=stop: e.matmul(o, l, r, start=a, stop=b),
                  reads=reads, writes=writes)

        def act(out, in_, func, reads, writes, scale=1.0, bias=None, accum_out=None):
            def f(e, o=out, i=in_, fn=func, sc=scale, bi=bias, ac=accum_out):
                kw = {}
                if bi is not None:
                    kw["bias"] = bi
                if ac is not None:
                    kw["accum_out"] = ac
                return e.activation(out=o, in_=i, func=fn, scale=sc, **kw)
            S.add("act", f, reads=reads, writes=writes)

        def tt(eng, out, in0, in1, op, reads, writes):
            S.add(eng, lambda e, o=out, a=in0, b=in1, p=op: e.tensor_tensor(out=o, in0=a, in1=b, op=p),
                  reads=reads, writes=writes)

        def ts(eng, out, in0, s1, s2, op0, op1, reads, writes):
            def f(e, o=out, a=in0, x=s1, y_=s2, p0=op0, p1=op1):
                if p1 is None:
                    return e.tensor_scalar(out=o, in0=a, scalar1=x, scalar2=None, op0=p0)
                return e.tensor_scalar(out=o, in0=a, scalar1=x, scalar2=y_, op0=p0, op1=p1)
            S.add(eng, f, reads=reads, writes=writes)

        def stt(out, in0, scalar, in1, op0, op1, reads, writes):
            S.add("dve", lambda e, o=out, a=in0, s=scalar, b=in1, p0=op0, p1=op1:
                  e.scalar_tensor_tensor(out=o, in0=a, scalar=s, in1=b, op0=p0, op1=p1),
                  reads=reads, writes=writes)

        def cp(eng, out, in_, reads, writes):
            if eng == "act":
                S.add("act", lambda e, o=out, i=in_: e.copy(out=o, in_=i), reads=reads, writes=writes)
            else:
                S.add(eng, lambda e, o=out, i=in_: e.tensor_copy(out=o, in_=i), reads=reads, writes=writes)

        def recip(out, in_, reads, writes):
            S.add("dve", lambda e, o=out, i=in_: e.reciprocal(out=o, in_=i), reads=reads, writes=writes)

        def recip_fast(out, in_, reads, writes, Tn=T):
            S.add("dve", lambda e, o=out, i=in_, sc=tmpf[:, 0, 0:Tn]: e.reciprocal_approx_accurate(out=o, in_=i, scratch=sc),
                  reads=reads + [("tmp", 0)], writes=writes + [("tmp", 0)])

        def memset(eng, ap, val, writes):
            S.add(eng, lambda e, a=ap, v=val: e.memset(a, v), writes=writes)

        def dma(eng, out, in_, key, reads, writes, done_val=None, **kw):
            S.add(eng, lambda e, o=out, i=in_, k=kw: e.dma_start(out=o, in_=i, **k),
                  reads=reads, writes=writes, dma_key=key, done_val=done_val)

        piece_ctr = [0]

        def load_piece(key):
            off, n, grp = wtab[key]
            slot = piece_ctr[0] % RS
            piece_ctr[0] += 1
            dma("sp", wring[:, slot, 0:n], wb[:, off:off + n], ("w", slot),
                reads=[("wb", key)], writes=[("w", slot)])
            return slot, n

        def wview(slot, n, kc):
            return wring[:, slot, 0:n].rearrange("p (k f) -> p k f", k=kc)

        dma("sp", cstf[:, :], cst[:, :], ("c", 0), reads=[], writes=[("cstf",)])
        dma("sp", prmb[:, :], prm[:, :], ("c", 1), reads=[], writes=[("prm",)])
        cp("dve", identb[:, :], cstf[:, 0:128], [("cstf",)], [("identb",)])
        cp("dve", maskb[:, :, :], cstf[:, 128:640].rearrange("p (a b) -> p a b", a=2), [("cstf",)], [("maskb",)])
        cp("dve", onesb[:, :], cstf[:, 768:896], [("cstf",)], [("onesb",)])
        ts("dve", nsink[:, :], prmb[:, 80:112], -1.0, None, ALU.mult, None, [("prm",)], [("nsink",)])
        for i in range(2):
            memset("pool", kpad[i][:, :, :], 0.0, [("kp", i, v) for v in range(4)])
            memset("pool", vpad[i][:, :, :, :], 0.0, [("vp", i, b) for b in range(5)])
        memset("pool", kmp[:, :, :, :], 0.0, [("kmp", l) for l in range(4)])
        memset("pool", vmp[:, :, :, :, :], 0.0, [("vmp", l) for l in range(4)])

        order = [("mem", l) for l in range(4)]
        for l in range(4):
            order += [(l, i) for i in range(25)]
        for key in order:
            off, n, grp = wtab[key]
            dma("pool", wb[:, off:off + n], wf[:, off:off + n], ("cast",) + grp, reads=[],
                writes=[("wb", key)], done_val=16 * wgroups[grp], max_dma_last_dim=8192)

        sq_ctr = [0]

        def rmsnorm(hb, gcol0, Tn, dst_fn, dst_keys_fn, final=False):
            bank = nb()
            for kc in range(KC):
                i = sq_ctr[0] % 2
                sq_ctr[0] += 1
                act(sqb[:, i, 0:Tn], hbuf[hb][:, kc, 0:Tn], AF.Square, [("h", hb, kc)], [("sq", i)])
                mm(ps[:, bank, 0:Tn], onesb[:, :], sqb[:, i, 0:Tn], kc == 0, kc == KC - 1,
                   [("sq", i), ("onesb",)], [("ps", bank)])
            act(rs_s[:, 0:Tn], ps[:, bank, 0:Tn], AF.Ln, [("ps", bank), ("epsb",)], [("rs",)],
                scale=1.0 / D, bias=epsb[:, 0:1])
            act(rstd[:, 0:Tn], rs_s[:, 0:Tn], AF.Exp, [("rs",)], [("rstd",)], scale=-0.5)
            for kc in range(KC):
                stt(dst_fn(kc), hbuf[hb][:, kc, 0:Tn], Gc(gcol0 + kc), rstd[:, 0:Tn], ALU.mult, ALU.mult,
                    [("h", hb, kc), ("prm",), ("rstd",)], dst_keys_fn(kc))

        epsb = sb("epsb", [128, 1], F32)
        dmy = sb("dmy", [128, 1], F32)
        memset("dve", epsb[:, :], EPS, [("epsb",)])

        def proj_chunk(w3, col0, nk, rhs_fn, rhs_keys_fn, wkey, Tn=T):
            bank = nb()
            for k in range(nk):
                mm(ps[:, bank, 0:Tn], w3[:, k, col0:col0 + 128], rhs_fn(k), k == 0, k == nk - 1,
                   [wkey] + rhs_keys_fn(k), [("ps", bank)])
            return bank

        def proj_chunks_kouter(w3, col0s, nk, rhs_fn, rhs_keys_fn, wkey, Tn=T):
            banks = [nb() for _ in col0s]
            for k in range(nk):
                for b, c0 in zip(banks, col0s):
                    mm(ps[:, b, 0:Tn], w3[:, k, c0:c0 + 128], rhs_fn(k), k == 0, k == nk - 1,
                       [wkey] + rhs_keys_fn(k), [("ps", b)])
            return banks

        ev_ctr = [0]

        def evac_scaled(out, bank, scale, writes, Tn=T):
            ev_ctr[0] += 1
            if ev_ctr[0] % 2:
                act(out, ps[:, bank, 0:Tn], AF.Copy, [("ps", bank)], writes, scale=scale)
            else:
                ts("dve", out, ps[:, bank, 0:Tn], scale, None, ALU.mult, None, [("ps", bank)], writes)

        xnk = lambda k: [("xn", k)]

        dma("sp", hbuf[1][:, :, 0:NMEM], memT.rearrange("(k p) t -> p k t", p=128), ("x", 1),
            reads=[], writes=[("h", 1, k) for k in range(KC)])
        rmsnorm(1, 0, NMEM, lambda kc: xn[:, kc, 0:NMEM], lambda kc: [("xn", kc)])
        for l in range(4):
            slot, n = load_piece(("mem", l))
            w3 = wview(slot, n, KC)
            for cm in range(2):
                bank = proj_chunk(w3, cm * 128, KC, lambda k: xn[:, k, 0:NMEM], xnk, ("w", slot), Tn=NMEM)
                cp("act", kmp[0:64, l, 2 * cm, :], ps[0:64, bank, 0:NMEM], [("ps", bank)], [("kmp", l)])
                cp("dve", kmp[64:128, l, 2 * cm + 1, :], ps[64:128, bank, 0:NMEM], [("ps", bank)], [("kmp", l)])
            for blk in range(2):
                bank = nb()
                for k in range(KC):
                    mm(ps[:, bank, 0:256], xn[:, k, blk * 128:(blk + 1) * 128], w3[:, k, 256:512], k == 0, k == KC - 1,
                       [("w", slot), ("xn", k)], [("ps", bank)])
                src = ps[:, bank, 0:256].rearrange("p (k o d) -> p k o d", k=2, o=2)
                dst = vmp[:, l, blk, :, :].rearrange("p (k o) d -> p k o d", o=2)
                cp("act", dst[:, :, 0, 0:64], src[:, :, 0, :], [("ps", bank)], [("vmp", l)])
                cp("dve", dst[:, :, 1, 64:128], src[:, :, 1, :], [("ps", bank)], [("vmp", l)])

        if mixers:
            for i in range(12):
                sgi = i % 2
                stg = sgb[:, sgi, 0:384]
                wm = tmpf[:, sgi, 0:128]
                dma("sp", stg, btin[i, :, :], ("bt", sgi), reads=[], writes=[("sg", sgi)])
                tt("dve", wm, stg[:, 0:128], cstf[:, 640:768], ALU.mult,
                   [("sg", sgi), ("cstf",)], [("tmp", sgi)])
                cp("dve", WsTb[:, i, :], wm, [("tmp", sgi)], [("WsTb", i)])
                bank = nb()
                mm(ps[:, bank, 0:128], stg[:, 128:256], wm, True, False,
                   [("sg", sgi), ("tmp", sgi)], [("ps", bank)])
                mm(ps[:, bank, 0:128], cstf[0:1, 768:896], stg[0:1, 256:384], False, True,
                   [("sg", sgi), ("cstf",)], [("ps", bank)])
                cp("act", BT[:, i, :], ps[:, bank, 0:128], [("ps", bank)], [("BT", i)])

        par = [0]

        def make_group(heads, sink_c0, out_ap, out_keys, same_kv=False):
            gidx = par[0]
            i = par[0] % NPAR
            par[0] += 1
            stv = stt_[i]
            nm, negm, rsum, stx, den, rr = (stv[:, j, :] for j in range(6))
            state = {}

            def stA():
                b0 = 2 * (gidx % 2)
                state["sc"] = b0
                for pr in range(2):
                    hd = heads[2 * pr]
                    bank = b0 + pr
                    o2 = ps[:, bank, :].rearrange("p (a k) -> p a k", a=2)
                    mm(o2, hd["q"], hd["k2"], True, hd["mask"] is None, hd["keys_qk"], [("ps", bank)])
                    if hd["mask"] is not None:
                        mm(o2, identb[:, :], hd["mask"].unsqueeze(1).broadcast_to([128, 2, 256]), False, True,
                           [("identb",), ("maskb",)], [("ps", bank)])

            def stB1():
                b0 = state["sc"]
                scv = ps[:, b0:b0 + 2, :].rearrange("p b (s k) -> p (b s) k", s=2)
                S.add("dve", lambda e: e.tensor_reduce(out=nm, in_=scv, axis=AX.X, op=ALU.max, negate=True),
                      reads=[("ps", b0), ("ps", b0 + 1)], writes=[("st", i, "nm")])
                tt("dve", negm, nm, nsink_ap(sink_c0), ALU.min, [("st", i, "nm"), ("nsink",)], [("st", i, "negm")])
                tt("dve", stx, negm, sink_ap(sink_c0), ALU.add, [("st", i, "negm"), ("prm",)], [("st", i, "stx")])

            def stB2():
                b0 = state["sc"]
                for s in range(4):
                    bank = b0 + s // 2
                    col = (s % 2) * 256
                    act(Pb[i][:, s, :], ps[:, bank, col:col + 256], AF.Exp, [("ps", bank), ("st", i, "negm")],
                        [("P", i), ("st", i, "rsum")], bias=negm[:, s:s + 1], accum_out=rsum[:, s:s + 1])
                act(stx, stx, AF.Exp, [("st", i, "stx")], [("st", i, "stx")])

            def stB3():
                tt("dve", den, rsum, stx, ALU.add, [("st", i, "rsum"), ("st", i, "stx")], [("st", i, "den")])
                recip(rr, den, [("st", i, "den")], [("st", i, "rr")])
                for s in range(4):
                    ts("pool", dgb[i][:, s, :], identb[:, :], rr[:, s:s + 1], 1.0, ALU.mult, ALU.mult,
                       [("identb",), ("st", i, "rr")], [("dg", i)])

            def stC():
                p0 = 4
                for s in range(4):
                    bank = p0 + s // 2
                    for kb in range(2):
                        col = (s % 2) * 256 + kb * 128
                        mm(ps[:, bank, col:col + 128], Pb[i][:, s, kb * 128:(kb + 1) * 128], dgb[i][:, s, :], True, True,
                           [("P", i), ("dg", i)], [("ps", bank)])
                cp("act",
                   PTs[i][:, :, :, :].rearrange("p (b s) k q -> p b (s k q)", b=2), ps[:, p0:p0 + 2, :],
                   [("ps", p0), ("ps", p0 + 1)], [("PT", i)])

            def stD():
                ob = 6 + gidx % 2
                if same_kv:
                    o2 = ps[:, ob, 0:256].rearrange("p (a q) -> p a q", a=2)
                    ptv = PTs[i][:, :, :, :].rearrange("p (a o) k q -> p o k a q", o=2)
                    cnt = 0
                    for o_ in range(2):
                        hd = heads[o_]
                        for kb in range(2):
                            mm(o2, hd["v%d" % kb], ptv[:, o_, kb, :, :], cnt == 0, cnt == 3,
                               [("PT", i)] + hd["keys_v"], [("ps", ob)])
                            cnt += 1
                    cp("dve", out_ap, o2, [("ps", ob)], out_keys)
                    return
                for pr in range(2):
                    cnt = 0
                    for s in (2 * pr, 2 * pr + 1):
                        hd = heads[s]
                        for kb in range(2):
                            mm(ps[:, ob, pr * 128:(pr + 1) * 128], hd["v%d" % kb], PTs[i][:, s, kb, :], cnt == 0, cnt == 3,
                               [("PT", i)] + hd["keys_v"], [("ps", ob)])
                            cnt += 1
                cp("dve", out_ap, ps[:, ob, 0:256].rearrange("p (a q) -> p a q", a=2), [("ps", ob)], out_keys)

            return [stA, stB1, stB2, stB3, stC, stD]

        SKEW = (0, 1, 1, 2, 3, 4)

        deferred = []

        def run_deferred():
            while deferred:
                deferred.pop(0)()

        def run_groups(groups, extra=()):
            n = len(groups)
            for step in range(max(n + SKEW[-1], len(extra))):
                for sidx in range(6):
                    g = step - SKEW[sidx]
                    if 0 <= g < n:
                        groups[g][sidx]()
                if step < len(extra):
                    extra[step]()

        def mem_heads(l, blk):
            hs = []
            for hm in range(4):
                hs.append(dict(q=qmT[:, hm // 2, blk * 128:(blk + 1) * 128], k2=kmp[:, l, 2 * (hm // 2):2 * (hm // 2) + 2, :], mask=None,
                               v0=vmp[:, l, 0, hm, :], v1=vmp[:, l, 1, hm, :],
                               keys_qk=[("qm", hm // 2), ("kmp", l)], keys_v=[("vmp", l)]))
            return hs

        def out_proj_and_ffn(l, hb):
            for pi in range(2):
                slot, n = load_piece((l, 4 + pi))
                w3 = wview(slot, n, KC)
                for m_ in range(4):
                    m = pi * 4 + m_
                    bank = proj_chunk(w3, m_ * 128, KC, lambda k: xn[:, k, :], xnk, ("w", slot))
                    tt("dve", hbuf[hb][:, m, :], ps[:, bank, :], hbuf[hb][:, m, :], ALU.add,
                       [("ps", bank), ("h", hb, m)], [("h", hb, m)])
            ffn(l, hb)

        def ffn(l, hb):
            rmsnorm(hb, 8 * (5 + l), T, lambda kc: xn[:, kc, :], lambda kc: [("xn", kc)])
            for pi in range(NJ // 2):
                slot, n = load_piece((l, 6 + pi))
                w3 = wview(slot, n, KC)
                if pi == 0:
                    b4 = proj_chunks_kouter(w3, [0, 128, 256, 384], KC, lambda k: xn[:, k, :], xnk, ("w", slot))
                for jj in range(2):
                    j = 2 * pi + jj
                    if pi == 0:
                        bg, bu = b4[2 * jj], b4[2 * jj + 1]
                    else:
                        bg = proj_chunk(w3, jj * 256, KC, lambda k: xn[:, k, :], xnk, ("w", slot))
                        bu = proj_chunk(w3, jj * 256 + 128, KC, lambda k: xn[:, k, :], xnk, ("w", slot))
                    si = j % 2
                    act(sgb[:, si, :], ps[:, bg, :], AF.Silu, [("ps", bg)], [("sg", si)])
                    tt("dve", actb[:, j, :], ps[:, bu, :], sgb[:, si, :], ALU.mult,
                       [("ps", bu), ("sg", si)], [("act", j)])
            act(dmy[:, 0:1], epsb[:, 0:1], AF.Exp, [("epsb",)], [("dmy",)])
            for m in range(8):
                slot, n = load_piece((l, 17 + m))
                w3 = wview(slot, n, NJ)
                bank = proj_chunk(w3, 0, NJ, lambda k: actb[:, k, :], lambda k: [("act", k)], ("w", slot))
                tt("dve", hbuf[hb][:, m, :], ps[:, bank, :], hbuf[hb][:, m, :], ALU.add,
                   [("ps", bank), ("h", hb, m)], [("h", hb, m)])

        def layer_A(l, hb, t):
            la = l // 2
            rmsnorm(hb, 8 * (1 + l), T, lambda kc: xn[:, kc, :], lambda kc: [("xn", kc)])
            rhs = lambda k: xn[:, k, :]
            s0, n0 = load_piece((l, 0))
            w3 = wview(s0, n0, KC)
            banks = proj_chunks_kouter(w3, [0, 128, 256, 384], KC, rhs, xnk, ("w", s0))
            for c in range(4):
                evac_scaled(qT[:, c, :], banks[c], 0.125, [("q", c)])
            s2, n2 = load_piece((l, 2))
            w3 = wview(s2, n2, KC)
            bka = proj_chunk(w3, 0, KC, rhs, xnk, ("w", s2))
            bkb = proj_chunk(w3, 128, KC, rhs, xnk, ("w", s2))
            cp("act", kpad[la][0:64, 0, 128:640], ps[0:64, bka, :], [("ps", bka)], [("kp", la, 0)])
            cp("dve", kpad[la][64:128, 3, 128:640], ps[64:128, bka, :], [("ps", bka)], [("kp", la, 3)])
            cp("act", kpad[la][0:64, 2, 128:640], ps[0:64, bkb, :], [("ps", bkb)], [("kp", la, 2)])
            cp("dve", kpad[la][64:128, 1, 128:640], ps[64:128, bkb, :], [("ps", bkb)], [("kp", la, 1)])
            s1, n1 = load_piece((l, 1))
            w3 = wview(s1, n1, KC)
            for c in range(2):
                bank = proj_chunk(w3, c * 128, KC, rhs, xnk, ("w", s1))
                evac_scaled(qT[:, 4 + c, :], bank, 0.125, [("q", 4 + c)])
            for c in range(2):
                bank = proj_chunk(w3, 256 + c * 128, KC, rhs, xnk, ("w", s1))
                evac_scaled(qmT[:, c, :], bank, 0.125, [("qm", c)])
            s3, n3 = load_piece((l, 3))
            w3 = wview(s3, n3, KC)
            for blk in range(NBLK):
                bank = nb()
                for k in range(KC):
                    mm(ps[:, bank, 0:128], xn[:, k, blk * 128:(blk + 1) * 128], w3[:, k, 0:128], k == 0, k == KC - 1,
                       [("w", s3), ("xn", k)], [("ps", bank)])
                src = ps[:, bank, 0:128].rearrange("p (k d) -> p k d", k=2)
                dst = vpad[la][:, 1 + blk, :, :].rearrange("p (k o) d -> p k o d", o=2)
                cp("act", dst[:, :, 0, 0:64], src, [("ps", bank)], [("vp", la, 1 + blk)])
                cp("dve", dst[:, :, 1, 64:128], src, [("ps", bank)], [("vp", la, 1 + blk)])
            groups = []
            for blk in range(NBLK):
                first = (t == 0 and blk == 0)
                for gi in range(3):
                    hs = []
                    for s in range(4):
                        h = 4 * gi + s
                        var = (h // 6) * 2 + (h % 2)
                        hs.append(dict(q=qT[:, h // 2, blk * 128:(blk + 1) * 128],
                                       k2=kpad[la][:, 2 * (h // 6):2 * (h // 6) + 2, blk * 128:blk * 128 + 256],
                                       mask=maskb[:, 1 if first else 0, :],
                                       v0=vpad[la][:, blk, var, :], v1=vpad[la][:, blk + 1, var, :],
                                       keys_qk=[("q", h // 2), ("kp", la, 2 * (h // 6)), ("kp", la, 2 * (h // 6) + 1)],
                                       keys_v=[("vp", la, blk), ("vp", la, blk + 1)]))
                    groups.append(make_group(hs, la * 16 + gi * 4, xn[:, 2 * gi:2 * gi + 2, blk * 128:(blk + 1) * 128],
                                             [("xn", 2 * gi), ("xn", 2 * gi + 1)], same_kv=(gi != 1)))
                groups.append(make_group(mem_heads(l, blk), 12, xn[:, 6:8, blk * 128:(blk + 1) * 128],
                                         [("xn", 6), ("xn", 7)]))
            run_deferred()
            run_groups(groups)
            cp("pool", kpad[la][:, :, 0:128], kpad[la][:, :, 512:640], [("kp", la, v) for v in range(4)],
               [("kp", la, v) for v in range(4)])
            cp("pool", vpad[la][:, 0, :, :], vpad[la][:, 4, :, :], [("vp", la, 4)], [("vp", la, 0)])
            out_proj_and_ffn(l, hb)

        def layer_B(l, hb, t):
            lb = l // 2
            rmsnorm(hb, 8 * (1 + l), T, lambda kc: xn[:, kc, :], lambda kc: [("xn", kc)])
            rhs = lambda k: xn[:, k, :]
            s2, n2 = load_piece((l, 2))
            s3, n3 = load_piece((l, 3))
            s0, n0 = load_piece((l, 0))
            s1, n1 = load_piece((l, 1))
            w3a = wview(s2, n2, KC)
            w3b = wview(s3, n3, KC)
            w30 = wview(s0, n0, KC)
            w31 = wview(s1, n1, KC)
            chunks = [(w30, s0, c * 128, "u", c) for c in range(4)] + \
                     [(w31, s1, c * 128, "u", 4 + c) for c in range(2)] + \
                     [(w31, s1, 256 + c * 128, "qm", c) for c in range(2)]
            def ln_chain(blk, vi):
                for g in range(6):
                    S.add("dve", lambda e, o=bnst[:, vi, g, :], a=vgb[:, vi, g * 128:(g + 1) * 128]: e.bn_stats(out=o, in_=a),
                          reads=[("vg", vi, 0), ("vg", vi, 1)], writes=[("bn", vi, g)])
                    S.add("dve", lambda e, o=mvb[:, vi, g, :], a=bnst[:, vi, g, :]: e.bn_aggr(out=o, in_=a),
                          reads=[("bn", vi, g)], writes=[("mv", vi)])
                act(lnr[:, vi, 0, :], mvb[:, vi, :, 1], AF.Ln, [("mv", vi), ("epsb",)], [("lnr", vi, 0)],
                    bias=epsb[:, 0:1])
                act(lnr[:, vi, 1, :], lnr[:, vi, 0, :], AF.Exp, [("lnr", vi, 0)], [("lnr", vi, 1)], scale=-0.5)
                stt(lnr[:, vi, 0, :], mvb[:, vi, :, 0], -1.0, lnr[:, vi, 1, :], ALU.mult, ALU.mult,
                    [("mv", vi), ("lnr", vi, 1)], [("lnr", vi, 0)])
                for g in range(6):
                    ts("pool", nbuf[:, blk, g * 128:(g + 1) * 128], vgb[:, vi, g * 128:(g + 1) * 128],
                       lnr[:, vi, 1, g:g + 1], lnr[:, vi, 0, g:g + 1], ALU.mult, ALU.add,
                       [("vg", vi, 0), ("vg", vi, 1), ("lnr", vi, 0), ("lnr", vi, 1)], [("n", blk)])

            for blk in range(NBLK):
                ba = nb()
                bb = nb()
                if blk == 0:
                    ba1 = nb()
                    bb1 = nb()
                    for k in range(KC):
                        for (bq, wq, nn, bl, sk) in ((ba, w3a, 512, 0, s2), (bb, w3b, 256, 0, s3), (ba1, w3a, 512, 1, s2), (bb1, w3b, 256, 1, s3)):
                            mm(ps[:, bq, 0:nn], xn[:, k, bl * 128:(bl + 1) * 128], wq[:, k, :], k == 0, k == KC - 1,
                               [("w", sk), ("xn", k)], [("ps", bq)])
                elif blk == 1:
                    ba, bb = ba1, bb1
                else:
                    for k in range(KC):
                        mm(ps[:, ba, :], xn[:, k, blk * 128:(blk + 1) * 128], w3a[:, k, :], k == 0, k == KC - 1,
                           [("w", s2), ("xn", k)], [("ps", ba)])
                    for k in range(KC):
                        mm(ps[:, bb, 0:256], xn[:, k, blk * 128:(blk + 1) * 128], w3b[:, k, :], k == 0, k == KC - 1,
                           [("w", s3), ("xn", k)], [("ps", bb)])
                vi = blk % 3
                if blk == 3:
                    ln_chain(0, 0)
                act(vgb[:, vi, 0:512], ps[:, ba, :], AF.Gelu_apprx_tanh, [("ps", ba)], [("vg", vi, 0)])
                act(vgb[:, vi, 512:768], ps[:, bb, 0:256], AF.Gelu_apprx_tanh, [("ps", bb)], [("vg", vi, 1)])
                for (w3c, sc, col0, kind, ci) in chunks[2 * blk:2 * blk + 2]:
                    bank = proj_chunk(w3c, col0, KC, rhs, xnk, ("w", sc))
                    if kind == "u":
                        act(uT[:, ci, :], ps[:, bank, :], AF.Gelu_apprx_tanh, [("ps", bank)], [("q", ci)])
                    else:
                        evac_scaled(qmT[:, ci, :], bank, 0.125, [("qm", ci)])
            ln_chains = [lambda: ln_chain(1, 1), lambda: ln_chain(2, 2), lambda: ln_chain(3, 0)]
            groups = []
            for blk in range(NBLK):
                groups.append(make_group(mem_heads(l, blk), 12, xn[:, 6:8, blk * 128:(blk + 1) * 128],
                                         [("xn", 6), ("xn", 7)]))
            run_deferred()
            run_groups(groups, extra=ln_chains)
            for g in range(6):
                bank = nb()
                idx = lb * 6 + g
                for blk in range(NBLK):
                    mm(ps[:, bank, blk * 128:(blk + 1) * 128], nbuf[:, blk, g * 128:(g + 1) * 128], WsTb[:, idx, :], True, True,
                       [("n", blk), ("WsTb", idx)], [("ps", bank)])
                ti = g % 2
                stt(tmpf[:, ti, :].rearrange("p (b t) -> p b t", b=NBLK),
                    ps[:, bank, :].rearrange("p (b t) -> p b t", b=NBLK), lg_ap(idx),
                    BT[:, idx, :].unsqueeze(1).broadcast_to([128, NBLK, 128]), ALU.mult, ALU.add,
                    [("ps", bank), ("prm",), ("BT", idx)], [("tmp", ti)])
                tt("pool", xn[:, g, :], tmpf[:, ti, :], uT[:, g, :], ALU.mult, [("tmp", ti), ("q", g)], [("xn", g)])
            out_proj_and_ffn(l, hb)

        xTv = xT.rearrange("(k p) t -> p k t", p=128)
        dma("sp", hbuf[0][:, :, :], xTv[:, :, 0:T], ("x", 0), reads=[], writes=[("h", 0, k) for k in range(KC)])
        if NT > 1:
            dma("sp", hbuf[1][:, :, :], xTv[:, :, T:2 * T], ("x", 1), reads=[],
                writes=[("h", 1, k) for k in range(KC)])
        ob_ctr = [0]

        def final_norm(t):
            hb = t % 2
            bank = nb()
            for kc in range(KC):
                i = sq_ctr[0] % 2
                sq_ctr[0] += 1
                act(sqb[:, i, :], hbuf[hb][:, kc, :], AF.Square, [("h", hb, kc)], [("sq", i)])
                mm(ps[:, bank, :], onesb[:, :], sqb[:, i, :], kc == 0, kc == KC - 1, [("sq", i), ("onesb",)], [("ps", bank)])
            act(rs_s[:, :], ps[:, bank, :], AF.Ln, [("ps", bank), ("epsb",)], [("rs",)], scale=1.0 / D, bias=epsb[:, 0:1])
            act(rstd[:, :], rs_s[:, :], AF.Exp, [("rs",)], [("rstd",)], scale=-0.5)
            for kc in range(KC):
                oi = ob_ctr[0] % 4
                ob_ctr[0] += 1
                stt(ob_ap[oi], hbuf[hb][:, kc, :], Gc(72 + kc), rstd[:, :], ALU.mult, ALU.mult,
                    [("h", hb, kc), ("prm",), ("rstd",)], [ob_key[oi]])
                dma("sp", y[kc * 128:(kc + 1) * 128, t * T:(t + 1) * T], ob_ap[oi], ("store", oi),
                    reads=[ob_key[oi]], writes=[])
            if t + 2 < NT:
                dma("sp", hbuf[hb][:, :, :], xTv[:, :, (t + 2) * T:(t + 3) * T], ("x", hb), reads=[],
                    writes=[("h", hb, k) for k in range(KC)])

        for t in range(NT):
            hb = t % 2
            for l in layers:
                if not mixers:
                    ffn(l, hb)
                elif l % 2 == 0:
                    layer_A(l, hb, t)
                else:
                    layer_B(l, hb, t)
            if t + 1 < NT and mixers:
                deferred.append(lambda t=t: final_norm(t))
            else:
                final_norm(t)

        S.emit(nc, st)
    return nc


_CACHE = {}


def kernel(**inputs):
    inp = {k: np.asarray(v) for k, v in inputs.items()}
    x = inp["x"]
    mem = inp["mem"]
    B = x.shape[0]
    wfull, _, _, _ = _pack_weights(inp)
    prm = _pack_params(inp)
    bt = _pack_bt(inp)
    cst = _consts()
    if "nc" not in _CACHE:
        _CACHE["nc"] = build_program()
    nc = _CACHE["nc"]
    in_maps = []
    for b in range(B):
        in_maps.append({
            "xT": np.ascontiguousarray(x[b].T),
            "memT": np.ascontiguousarray(mem[b].T),
            "wf": wfull, "prm": prm, "cst": cst, "btin": bt,
        })
    res = run_bass_kernel_spmd(nc, in_maps, core_ids=list(range(B)))
    out = np.stack([np.ascontiguousarray(res.results[b]["y"].T) for b in range(B)], axis=0)
    return out.astype(np.float32)
```
